# Optimizing a Trainium2 kernel written in Bass

```python
import math
import jax, jax.numpy as jnp
from jax import lax
import numpy as np


D_MODEL = 1024
BATCH = 8
SEQ = 2048
DEPTH = 4
DEC_BATCH = 128
DEC_SEQ = 8
PAST_LEN = 16384
PAGE_SIZE = 128

N_MIXERS = 2
N_A = (DEPTH + 1) // 2
N_B = DEPTH // 2
D_RNN = D_MODEL
RG_HEADS = 8
RG_BLK = D_RNN // RG_HEADS
RG_CONV = 4
RG_C = 8.0
HG_DK = 128
HG_HEADS = D_MODEL // HG_DK
HG_DV = D_MODEL // HG_HEADS
HK = HG_HEADS * HG_DK
HV = HG_HEADS * HG_DV
HG_CHUNK = 64
D_FF = 2816
FFN_CONV = 3
ALPHA = (2.0 * DEPTH) ** 0.25
BETA = (8.0 * DEPTH) ** -0.25
LN_EPS = 1e-5
RMS_EPS = 1e-6
F_FLOOR = 1e-30

kernel_name = 'hawk_hgrn2_convffn_deepnorm_step'


def layer_norm(x, g, b):
    xf = x.astype(jnp.float32)
    mu = jnp.mean(xf, axis=-1, keepdims=True)
    var = jnp.mean(jnp.square(xf - mu), axis=-1, keepdims=True)
    y = (xf - mu) * lax.rsqrt(var + LN_EPS) * g.astype(jnp.float32) + b.astype(jnp.float32)
    return y.astype(x.dtype)


def causal_dwconv(x, buf, w, b):
    width = w.shape[0]
    L = x.shape[1]
    xp = jnp.concatenate([buf.astype(x.dtype), x], axis=1)
    y = sum(xp[:, j:j + L] * w[j] for j in range(width)) + b
    return y, xp[:, xp.shape[1] - (width - 1):]


def rglru_mixer(x, h0, conv_buf, w_in, w_conv, b_conv, w_gate, b_gate, lam, w_out):
    B, L, _ = x.shape
    gate, xb = jnp.split(x @ w_in, 2, axis=-1)
    xc, new_buf = causal_dwconv(xb, conv_buf, w_conv, b_conv)
    xh = xc.reshape(B, L, RG_HEADS, RG_BLK)
    gates = jnp.einsum('blhi,ghij->gblhj', xh, w_gate).reshape(2, B, L, D_RNN)
    gates = gates.astype(jnp.float32) + b_gate.astype(jnp.float32)[:, None, None, :]
    r = jax.nn.sigmoid(gates[0])
    ig = jax.nn.sigmoid(gates[1])
    log_a = RG_C * r * jax.nn.log_sigmoid(lam.astype(jnp.float32))
    a = jnp.exp(log_a)
    bvals = jnp.sqrt(jnp.maximum(-jnp.expm1(2.0 * log_a), 0.0)) * ig * xc.astype(jnp.float32)
    bvals = bvals.at[:, 0].add(a[:, 0] * h0.astype(jnp.float32))

    def combine(left, right):
        a1, b1 = left
        a2, b2 = right
        return a1 * a2, a2 * b1 + b2

    _, h = lax.associative_scan(combine, (a, bvals), axis=1)
    y = (h.astype(x.dtype) * jax.nn.gelu(gate)) @ w_out
    return y, h[:, -1], new_buf


def hgrn2_mixer(x, S0, lb, w_in, norm_g, w_out):
    B, L, _ = x.shape
    q, f, iv, g = jnp.split(x @ w_in, [HK, 2 * HK, 2 * HK + HV], axis=-1)
    lbf = lb.astype(jnp.float32)
    fg = lbf + (1.0 - lbf) * jax.nn.sigmoid(f.astype(jnp.float32))
    log_f = jnp.log(jnp.maximum(fg, F_FLOOR))
    k = 1.0 - fg
    q = jax.nn.silu(q.astype(jnp.float32)) * (HG_DK ** -0.5)
    v = iv.astype(jnp.float32)
    c = min(HG_CHUNK, L)
    n = -(-L // c)
    pad = n * c - L

    def to_chunks(t, d):
        t = jnp.pad(t, ((0, 0), (0, pad), (0, 0)))
        return t.reshape(B, n, c, HG_HEADS, d).transpose(1, 0, 3, 2, 4)

    mask = jnp.tril(jnp.ones((c, c), dtype=bool))[None, None, :, :, None]

    def step(S, inp):
        qc, kc, vc, lc = inp
        cum = jnp.cumsum(lc, axis=2)
        diff = cum[:, :, :, None, :] - cum[:, :, None, :, :]
        decay = jnp.where(mask, jnp.exp(jnp.minimum(diff, 0.0)), 0.0)
        A = jnp.einsum('bhtk,bhsk,bhtsk->bhts', qc, kc, decay)
        o = jnp.einsum('bhts,bhsv->bhtv', A, vc) + jnp.einsum('bhtk,bhkv->bhtv', qc * jnp.exp(cum), S)
        last = cum[:, :, -1:, :]
        S_new = jnp.exp(last[:, :, 0, :])[..., None] * S + jnp.einsum('bhsk,bhsv->bhkv', kc * jnp.exp(last - cum), vc)
        return S_new, o

    S, o = lax.scan(step, S0.astype(jnp.float32),
                    (to_chunks(q, HG_DK), to_chunks(k, HG_DK), to_chunks(v, HG_DV), to_chunks(log_f, HG_DK)))
    o = o.transpose(1, 0, 3, 2, 4).reshape(B, n * c, HG_HEADS, HG_DV)[:, :L]
    o = o * lax.rsqrt(jnp.mean(jnp.square(o), axis=-1, keepdims=True) + RMS_EPS) * norm_g.astype(jnp.float32)
    o = o * jax.nn.silu(g.astype(jnp.float32).reshape(B, L, HG_HEADS, HG_DV))
    y = o.reshape(B, L, HV).astype(x.dtype) @ w_out
    return y, S


def conv_ffn(x, buf, w_in, cw, cb, w_out):
    gate, up = jnp.split(x @ w_in, 2, axis=-1)
    gc, new_buf = causal_dwconv(gate, buf, cw, cb)
    return (jax.nn.gelu(gc) * up) @ w_out, new_buf


def trunk(x, rg_h, rg_conv, hg_s, ffn_conv, ln_g, ln_b, rg_w_in, rg_conv_w, rg_conv_b,
          rg_gate_w, rg_gate_b, rg_lambda, rg_w_out, lbs, hg_w_in, hg_norm_g, hg_w_out,
          ffn_w_in, ffn_conv_w, ffn_conv_b, ffn_w_out):
    new_h, new_rc, new_s, new_fc = [], [], [], []
    for i in range(DEPTH):
        j = i // N_MIXERS
        if i % N_MIXERS == 0:
            mix, h, cbuf = rglru_mixer(x, rg_h[j], rg_conv[j], rg_w_in[j], rg_conv_w[j], rg_conv_b[j],
                                       rg_gate_w[j], rg_gate_b[j], rg_lambda[j], rg_w_out[j])
            new_h.append(h.astype(rg_h.dtype))
            new_rc.append(cbuf.astype(rg_conv.dtype))
        else:
            mix, S = hgrn2_mixer(x, hg_s[j], lbs[j], hg_w_in[j], hg_norm_g[j], hg_w_out[j])
            new_s.append(S.astype(hg_s.dtype))
        x = layer_norm(ALPHA * x + mix, ln_g[i, 0], ln_b[i, 0])
        f, fbuf = conv_ffn(x, ffn_conv[i], ffn_w_in[i], ffn_conv_w[i], ffn_conv_b[i], ffn_w_out[i])
        new_fc.append(fbuf.astype(ffn_conv.dtype))
        x = layer_norm(ALPHA * x + f, ln_g[i, 1], ln_b[i, 1])
    return x, jnp.stack(new_h), jnp.stack(new_rc), jnp.stack(new_s), jnp.stack(new_fc)


def setup_inputs(seed: int = 0) -> dict:
    key = jax.random.key(seed)
    ks = jax.random.split(key, 24)
    f32 = jnp.float32

    def nrm(k, shape, s):
        return jax.random.normal(k, shape, f32) * s

    u = jax.random.uniform(ks[13], (N_A, D_RNN), f32, 0.9, 0.999)
    p = u ** (1.0 / RG_C)
    return {
        'x_prompt': nrm(ks[0], (BATCH, SEQ, D_MODEL), 1.0),
        'x_sample': nrm(ks[1], (DEC_BATCH, DEC_SEQ, D_MODEL), 1.0),
        'state_rglru_h': nrm(ks[2], (N_A, DEC_BATCH, D_RNN), 0.5),
        'state_rglru_conv': nrm(ks[3], (N_A, DEC_BATCH, RG_CONV - 1, D_RNN), 1.0),
        'state_hgrn_s': nrm(ks[4], (N_B, DEC_BATCH, HG_HEADS, HG_DK, HG_DV), 0.5),
        'state_ffn_conv': nrm(ks[5], (DEPTH, DEC_BATCH, FFN_CONV - 1, D_FF), 1.0),
        'ln_g': 1.0 + nrm(ks[6], (DEPTH, 2, D_MODEL), 0.02),
        'ln_b': nrm(ks[7], (DEPTH, 2, D_MODEL), 0.02),
        'rg_w_in': nrm(ks[8], (N_A, D_MODEL, 2 * D_RNN), D_MODEL ** -0.5),
        'rg_conv_w': nrm(ks[9], (N_A, RG_CONV, D_RNN), RG_CONV ** -0.5),
        'rg_conv_b': nrm(ks[10], (N_A, D_RNN), 0.02),
        'rg_gate_w': nrm(ks[11], (N_A, 2, RG_HEADS, RG_BLK, RG_BLK), RG_BLK ** -0.5),
        'rg_gate_b': nrm(ks[12], (N_A, 2, D_RNN), 0.02),
        'rg_lambda': jnp.log(p) - jnp.log1p(-p),
        'rg_w_out': nrm(ks[14], (N_A, D_RNN, D_MODEL), D_RNN ** -0.5 * BETA),
        'hg_lower': nrm(ks[15], (N_B, HK), 0.1),
        'hg_w_in': nrm(ks[16], (N_B, D_MODEL, 2 * HK + 2 * HV), D_MODEL ** -0.5),
        'hg_norm_g': 1.0 + nrm(ks[17], (N_B, HG_DV), 0.02),
        'hg_w_out': nrm(ks[18], (N_B, HV, D_MODEL), HV ** -0.5 * BETA),
        'ffn_w_in': nrm(ks[19], (DEPTH, D_MODEL, 2 * D_FF), D_MODEL ** -0.5),
        'ffn_conv_w': nrm(ks[20], (DEPTH, FFN_CONV, D_FF), FFN_CONV ** -0.5),
        'ffn_conv_b': nrm(ks[21], (DEPTH, D_FF), 0.02),
        'ffn_w_out': nrm(ks[22], (DEPTH, D_FF, D_MODEL), D_FF ** -0.5 * BETA),
    }


def reference(x_prompt, x_sample, state_rglru_h, state_rglru_conv, state_hgrn_s, state_ffn_conv,
              ln_g, ln_b, rg_w_in, rg_conv_w, rg_conv_b, rg_gate_w, rg_gate_b, rg_lambda, rg_w_out,
              hg_lower, hg_w_in, hg_norm_g, hg_w_out, ffn_w_in, ffn_conv_w, ffn_conv_b, ffn_w_out):
    sm = jax.nn.softmax(hg_lower.astype(jnp.float32), axis=0)
    lbs = jnp.maximum(jnp.cumsum(sm, axis=0) - sm[0], 0.0)
    weights = (ln_g, ln_b, rg_w_in, rg_conv_w, rg_conv_b, rg_gate_w, rg_gate_b, rg_lambda, rg_w_out,
               lbs, hg_w_in, hg_norm_g, hg_w_out, ffn_w_in, ffn_conv_w, ffn_conv_b, ffn_w_out)
    pb = x_prompt.shape[0]
    dt = x_prompt.dtype
    z_h = jnp.zeros((N_A, pb, D_RNN), dt)
    z_rc = jnp.zeros((N_A, pb, RG_CONV - 1, D_RNN), dt)
    z_s = jnp.zeros((N_B, pb, HG_HEADS, HG_DK, HG_DV), dt)
    z_fc = jnp.zeros((DEPTH, pb, FFN_CONV - 1, D_FF), dt)
    y_prompt, p_h, p_rc, p_s, p_fc = trunk(x_prompt, z_h, z_rc, z_s, z_fc, *weights)
    y_sample, s_h, s_rc, s_s, s_fc = trunk(x_sample, state_rglru_h, state_rglru_conv, state_hgrn_s,
                                           state_ffn_conv, *weights)
    return (y_prompt, y_sample, p_h, p_rc, p_s, p_fc, s_h, s_rc, s_s, s_fc)
```

```python
import contextlib
import numpy as np
import concourse.bass as bass
import concourse.mybir as mybir
from concourse.bass_utils import run_bass_kernel_spmd

F32 = mybir.dt.float32
BF16 = mybir.dt.bfloat16
AF = mybir.ActivationFunctionType
ALU = mybir.AluOpType

NCORE = 8
D = 1024
SEQ = 2048
DEPTH = 4
DFF = 2816
NJ = 22
NP_ = 512
SQP = 4
NS = SQP * 8
NCOL = NP_ + NS
NPASS = 4
ALPHA = (2.0 * DEPTH) ** 0.25
LN_EPS = 1e-5
RMS_EPS = 1e-6
NSLOT = 4
TW = 560
SEGW = 1408

O_LNG = 0
O_LNB = 64
O_RCW = 128
O_RCB = 192
O_RGB = 208
O_LAM = 240
O_HLO = 256
O_HNG = 272
O_FCW = 274
O_FCB = 538
NROWS = 640
O_C1 = 640
O_C2 = 656
O_LB = 672
O_OML = 688
O_FA = 704
O_FB = 720
O_HNH = 736
O_HC1 = 738
O_HGB = 754
CVW = 786

M_ID = 0
M_A2 = 128
M_AS = 256
M_RST = 288
M_SM = 288 + NCOL
M_ONE = M_SM + 4
M_ONE2 = M_ONE + 128
CMW = M_ONE2 + 128


class Tok:
    def __init__(self, sem):
        self.sem = sem
        self.cnt = 0


class Buf:
    __slots__ = ("w", "rs", "name")

    def __init__(self, name=""):
        self.w = None
        self.rs = []
        self.name = name


class Eng:
    def __init__(self, name, tok, is_pe=False):
        self.name = name
        self.prog = []
        self.tok = tok
        self.seen = {}
        self.is_pe = is_pe

    def wait(self, tok, cnt):
        if cnt > self.seen.get(tok, 0):
            sem = tok.sem
            self.prog.append(lambda e: e.wait_ge(sem, cnt))
            self.seen[tok] = cnt


def _deps(eng, reads, writes, skip_tok=None, skip_readers=True):
    need = {}

    def add(st, skippable=True):
        if st is None:
            return
        tok, c = st
        if skippable and tok is skip_tok:
            return
        if need.get(tok, 0) < c:
            need[tok] = c

    for b in reads:
        add(b.w)
    for b in writes:
        add(b.w)
        for r in b.rs:
            add(r, skip_readers)
    for tok, c in need.items():
        eng.wait(tok, c)


def _stamp(reads, writes, st):
    for b in reads:
        b.rs = [r for r in b.rs if r[0] is not st[0]]
        b.rs.append(st)
    for b in writes:
        b.w = st
        b.rs = []


def op(eng, fn, reads=(), writes=(), inc=True):
    _deps(eng, reads, writes, skip_tok=eng.tok if eng.is_pe else None)
    if inc:
        eng.tok.cnt += 1
        sem = eng.tok.sem
        eng.prog.append(lambda e: fn(e).then_inc(sem, 1))
        st = (eng.tok, eng.tok.cnt)
    else:
        eng.prog.append(fn)
        st = (eng.tok, eng.tok.cnt + 1)
    _stamp(reads, writes, st)


def dma(eng, tok, out, in_, reads=(), writes=()):
    _deps(eng, reads, writes, skip_tok=tok, skip_readers=False)
    tok.cnt += 16
    sem = tok.sem
    eng.prog.append(lambda e: e.dma_start(out=out, in_=in_).then_inc(sem, 16))
    _stamp(reads, writes, (tok, tok.cnt))


def run_interleaved(gens):
    gens = [g for g in gens if g is not None]
    while gens:
        for g in list(gens):
            try:
                next(g)
            except StopIteration:
                gens.remove(g)


class Pool_:
    def __init__(self, items):
        self.free_ = list(items)

    def alloc(self):
        assert self.free_, "pool exhausted"
        return self.free_.pop(0)

    def free(self, it):
        self.free_.append(it)


def build_program():
    nc = bass.Bass("TRN2", target_bir_lowering=False)

    def din(name, shape):
        return nc.dram_tensor(name, list(shape), F32, kind="ExternalInput").ap()

    def dout(name, shape):
        return nc.dram_tensor(name, list(shape), F32, kind="ExternalOutput").ap()

    xp_d = din("xp", [SEQ, D])
    xs_d = din("xs", [16 * 8, D])
    srh_d = din("srh", [2, 16, D])
    src_d = din("src", [2, 16 * 3, D])
    shs_d = din("shs", [2, 16, 8, 128, 128])
    sfc_d = din("sfc", [4, 16 * 2, DFF])
    cvec_d = din("cvec", [NROWS, 128])
    cmask_d = din("cmask", [128, CMW])
    rg_w_in = din("rg_w_in", [2, D, 2 * D])
    rg_gate_w = din("rg_gate_w", [2, 2, 8, 128, 128])
    rg_w_out = din("rg_w_out", [2, D, D])
    hg_w_in = din("hg_w_in", [2, D, 4 * D])
    hg_w_out = din("hg_w_out", [2, D, D])
    ffn_w_in = din("ffn_w_in", [4, D, 2 * DFF])
    ffn_w_out = din("ffn_w_out", [4, DFF, D])

    yp_d = dout("yp", [SEQ, D])
    ys_d = dout("ys", [128, D])
    prh_d = dout("prh", [2, D])
    prc_d = dout("prc", [2, 3, D])
    phs_d = dout("phs", [2, 8, 128, 128])
    pfc_d = dout("pfc", [4, 2, DFF])
    orh_d = dout("orh", [2, 16, D])
    orc_d = dout("orc", [2, 16 * 3, D])
    ohs_d = dout("ohs", [2, 16, 8, 128, 128])
    ofc_d = dout("ofc", [4, 16 * 2, DFF])

    es = contextlib.ExitStack()
    with es:
        def sb(name, shape, dt=F32):
            return es.enter_context(nc.sbuf_tensor(name, list(shape), dt))

        def newtok(name):
            return Tok(es.enter_context(nc.semaphore(name)))

        xf = sb("xf", [128, 8, NCOL])
        xb = sb("xb", [128, 8, NCOL], BF16)
        mixo = sb("mixo", [128, 8, NCOL], BF16)
        hff = sb("hff", [128, NJ, NCOL], BF16)
        NT = 14
        NTB = 12
        tf = [sb(f"tf{i}", [128, TW]) for i in range(NT)]
        tb = [sb(f"tb{i}", [128, 640], BF16) for i in range(NTB)]
        wring = sb("wring", [128, NSLOT, 4096], BF16)
        stA = sb("stA", [128, SEGW])
        stB = sb("stB", [128, D])
        CV = sb("CV", [128, CVW])
        CM = sb("CM", [128, CMW])
        ONB = sb("ONB", [128, 128], BF16)
        ONB2 = sb("ONB2", [128, 128], BF16)
        Sst = [sb(f"Sst{j}", [128, 8, 128]) for j in range(2)]
        Sbf = [sb(f"Sbf{j}", [128, 8, 128], BF16) for j in range(2)]
        S0f = sb("S0f", [128, SQP, 8, 128])
        S0b = sb("S0b", [128, SQP, 8, 128], BF16)
        rg_tail = [sb(f"rgtail{j}", [128, 8, 3]) for j in range(2)]
        rg_hst = [sb(f"rghst{j}", [128, 8]) for j in range(2)]
        ffn_tail = [sb(f"fftail{i}", [128, NJ, 2]) for i in range(4)]
        h0s = sb("h0s", [128, 2, 8, SQP])
        cv0s = sb("cv0s", [128, 2, 8, SQP * 3])
        fc0s = sb("fc0s", [128, 4, NJ, SQP * 2])
        hs_o = sb("hs_o", [128, 2, 8, SQP])
        cv_o = sb("cv_o", [128, 2, 8, SQP * 3])
        fc_o = sb("fc_o", [128, 4, NJ, SQP * 2])
        ps_all = es.enter_context(nc.psum_tensor("ps_all", [128, 8, 512], F32))

        pe_t, act_t, dve_t, pool_t = newtok("pe"), newtok("act"), newtok("dve"), newtok("pool")

        xf_b = [Buf(f"xf{c}") for c in range(8)]
        xb_b = [Buf(f"xb{c}") for c in range(8)]
        mixo_b = [Buf() for _ in range(8)]
        hff_b = [Buf() for _ in range(NJ)]
        CV_b, CM_b = Buf("CV"), Buf("CM")
        Sst_b = [[Buf() for _ in range(8)] for _ in range(2)]
        Sbf_b = [[Buf() for _ in range(8)] for _ in range(2)]
        S0f_b, S0b_b = Buf("S0f"), Buf("S0b")
        rg_tail_b = [Buf() for _ in range(2)]
        rg_hst_b = [Buf() for _ in range(2)]
        ffn_tail_b = [Buf() for _ in range(4)]
        h0s_b, cv0s_b, fc0s_b = Buf(), Buf(), Buf()
        hs_o_b, cv_o_b, fc_o_b = Buf(), Buf(), Buf()
        stA_b, stB_b = Buf("stA"), Buf("stB")
        stA_t, stB_t = newtok("stA"), newtok("stB")
        S0f_t, S0b_t, S0o_t, Spo_t, cst_t = newtok("S0f"), newtok("S0b"), newtok("S0o"), newtok("Spo"), newtok("cst")
        wslot_b = [Buf(f"w{s}") for s in range(NSLOT)]
        wslot_t = [newtok(f"w{s}") for s in range(NSLOT)]

        tfp = Pool_([(tf[i], Buf(f"tf{i}"), Buf(f"tfx{i}")) for i in range(NT)])
        tbp = Pool_([(tb[i], Buf(f"tb{i}")) for i in range(NTB)])
        ps2p = Pool_([(ps_all[:, 2 * k:2 * k + 2, :], (Buf(f"ps{2 * k}"), Buf(f"ps{2 * k + 1}"))) for k in range(4)])

        class _Ps1:
            def alloc(self):
                t = ps2p.alloc()
                return (t[0][:, 0, :], t[1][0], t)

            def free(self, x):
                ps2p.free(x[2])
        ps1p = _Ps1()

        PE = Eng("pe", pe_t, is_pe=True)
        ACT = Eng("act", act_t)
        DVE = Eng("dve", dve_t)
        POOL = Eng("pool", pool_t)
        SP = Eng("sp", Tok(None))

        def ps2(t):
            return t[0].rearrange("p a b -> p (a b)")

        def mm(out, lhsT, rhs, start, stop, reads, writes, inc=False):
            op(PE, lambda e: e.matmul(out, lhsT, rhs, start=start, stop=stop), reads, writes, inc=inc)

        def tr(out, in_, ident, reads, writes, inc=False):
            op(PE, lambda e: e.matmul(out, in_, ident, start=True, stop=True), reads, writes, inc=inc)

        def act(out, in_, func, reads, writes, bias=None, scale=None):
            kw = {}
            if bias is not None:
                kw["bias"] = bias
            if scale is not None:
                kw["scale"] = scale
            op(ACT, lambda e: e.activation(out=out, in_=in_, func=func, **kw), reads, writes)

        def ts(eng, out, in0, s1, s2, op0, op1, reads, writes):
            if op1 is None:
                op(eng, lambda e: e.tensor_scalar(out=out, in0=in0, scalar1=s1, scalar2=None, op0=op0), reads, writes)
            else:
                op(eng, lambda e: e.tensor_scalar(out=out, in0=in0, scalar1=s1, scalar2=s2, op0=op0, op1=op1), reads, writes)

        def tt(eng, out, in0, in1, o, reads, writes):
            op(eng, lambda e: e.tensor_tensor(out=out, in0=in0, in1=in1, op=o), reads, writes)

        def stt(eng, out, in0, s, in1, op0, op1, reads, writes):
            op(eng, lambda e: e.scalar_tensor_tensor(out=out, in0=in0, scalar=s, in1=in1, op0=op0, op1=op1), reads, writes)

        def cp(eng, out, in_, reads, writes):
            if eng is ACT:
                op(eng, lambda e: e.copy(out=out, in_=in_), reads, writes)
            else:
                op(eng, lambda e: e.tensor_copy(out=out, in_=in_), reads, writes)

        def memset(eng, ap, val, writes):
            op(eng, lambda e: e.memset(ap, val), (), writes)

        def scan(out, d0, d1, init, reads, writes):
            op(DVE, lambda e: e.tensor_tensor_scan(out=out, data0=d0, data1=d1, initial=init, op0=ALU.mult, op1=ALU.add),
               reads, writes)

        ident = CM[:, M_ID:M_ID + 128]

        witems = []
        wstate = {"issued": 0, "next": 0, "released": set()}

        def w_prefetch():
            while wstate["issued"] < len(witems):
                k = wstate["issued"]
                if k >= NSLOT and (k - NSLOT) not in wstate["released"]:
                    break
                s = k % NSLOT
                for (dst_fn, src) in witems[k]:
                    dma(POOL, wslot_t[s], dst_fn(wring[:, s, :]), src, reads=(), writes=(wslot_b[s],))
                wstate["issued"] += 1

        def w_next():
            k = wstate["next"]
            wstate["next"] += 1
            w_prefetch()
            assert wstate["issued"] > k, "weight ring deadlock"
            s = k % NSLOT
            return wring[:, s, :], wslot_b[s], k

        def w_release(w):
            wstate["released"].add(w[2])
            w_prefetch()

        def it_cols(W, c0, w):
            return [(lambda sl: sl[:, 0:8 * w].rearrange("p (k n) -> p k n", n=w),
                     W[:, c0:c0 + w].rearrange("(k p) n -> p k n", p=128))]

        def build_witems():
            for p in range(DBG["npass"]):
                for i in range(DBG["depth"]):
                    j = i // 2
                    if not DBG["mixer"]:
                        pass
                    elif i % 2 == 0:
                        W = rg_w_in[j]
                        witems.append(it_cols(W, 0, 512))
                        witems.append(it_cols(W, 512, 512))
                        witems.append(it_cols(W, 1024, 512))
                        witems.append([(lambda sl: sl[:, 0:2048].rearrange("p (a n) -> p a n", n=128),
                                        rg_gate_w[j].rearrange("g h i n -> i (g h) n"))])
                        witems.append(it_cols(W, 1536, 512))
                        witems.append(it_cols(rg_w_out[j], 0, 512))
                        witems.append(it_cols(rg_w_out[j], 512, 512))
                    else:
                        W = hg_w_in[j]
                        for h in range(8):
                            witems.append([(lambda sl, t=t: sl[:, 0:4096].rearrange("p (k t n) -> p k t n", t=4, n=128)[:, :, t, :],
                                            W[:, t * 1024 + h * 128:t * 1024 + (h + 1) * 128].rearrange("(k p) n -> p k n", p=128))
                                           for t in range(4)])
                        witems.append(it_cols(hg_w_out[j], 0, 512))
                        witems.append(it_cols(hg_w_out[j], 512, 512))
                    if not DBG["ffn"]:
                        continue
                    W = ffn_w_in[i]
                    for q in range(6):
                        w = 512 if q < 5 else 256
                        witems.append(it_cols(W, q * 512, w))
                        witems.append(it_cols(W, DFF + q * 512, w))
                    for o in range(8):
                        witems.append([(lambda sl: sl[:, 0:NJ * 128].rearrange("p (k n) -> p k n", n=128),
                                        ffn_w_out[i][:, o * 128:(o + 1) * 128].rearrange("(k p) n -> p k n", p=128))])

        build_witems()

        dma(SP, cst_t, CM[:, :], cmask_d[:, :], writes=(CM_b,))
        for blk in range(5):
            dma(SP, stB_t, stB[:, 0:128], cvec_d[blk * 128:(blk + 1) * 128, :], writes=(stB_b,))
            pt = ps1p.alloc()
            tr(pt[0][:, 0:128], stB[:, 0:128], ident, (stB_b, CM_b), (pt[1],), inc=True)
            cp(DVE, CV[:, blk * 128:(blk + 1) * 128], pt[0][:, 0:128], (pt[1],), (CV_b,))
            ps1p.free(pt)
        act(CV[:, O_C1:O_C1 + 16], CV[:, O_LAM:O_LAM + 16], AF.Sigmoid, (CV_b,), (CV_b,))
        act(CV[:, O_C1:O_C1 + 16], CV[:, O_C1:O_C1 + 16], AF.Ln, (CV_b,), (CV_b,))
        ts(DVE, CV[:, O_C2:O_C2 + 16], CV[:, O_C1:O_C1 + 16], 16.0, None, ALU.mult, None, (CV_b,), (CV_b,))
        ts(DVE, CV[:, O_C1:O_C1 + 16], CV[:, O_C1:O_C1 + 16], 8.0, None, ALU.mult, None, (CV_b,), (CV_b,))
        memset(DVE, CV[:, O_LB:O_LB + 8], 0.0, (CV_b,))
        tt(DVE, CV[:, O_LB + 8:O_LB + 16], CV[:, O_HLO + 8:O_HLO + 16], CV[:, O_HLO:O_HLO + 8], ALU.subtract, (CV_b,), (CV_b,))
        act(CV[:, O_LB + 8:O_LB + 16], CV[:, O_LB + 8:O_LB + 16], AF.Sigmoid, (CV_b,), (CV_b,))
        ts(DVE, CV[:, O_OML:O_OML + 16], CV[:, O_LB:O_LB + 16], -1.0, 1.0, ALU.mult, ALU.add, (CV_b,), (CV_b,))
        ts(DVE, CV[:, O_FA:O_FA + 16], CV[:, O_OML:O_OML + 16], 0.5, None, ALU.mult, None, (CV_b,), (CV_b,))
        tt(DVE, CV[:, O_FB:O_FB + 16], CV[:, O_LB:O_LB + 16], CV[:, O_FA:O_FA + 16], ALU.add, (CV_b,), (CV_b,))
        ts(DVE, CV[:, O_HNH:O_HNH + 2], CV[:, O_HNG:O_HNG + 2], 0.5, None, ALU.mult, None, (CV_b,), (CV_b,))
        ts(DVE, CV[:, O_HC1:O_HC1 + 16], CV[:, O_C1:O_C1 + 16], 0.5, None, ALU.mult, None, (CV_b,), (CV_b,))
        ts(DVE, CV[:, O_HGB:O_HGB + 32], CV[:, O_RGB:O_RGB + 32], 0.5, None, ALU.mult, None, (CV_b,), (CV_b,))
        for j in range(2):
            memset(DVE, Sst[j][:, :, :], 0.0, tuple(Sst_b[j]))
            memset(DVE, Sbf[j][:, :, :], 0.0, tuple(Sbf_b[j]))
            memset(DVE, rg_tail[j][:, :, :], 0.0, (rg_tail_b[j],))
            memset(DVE, rg_hst[j][:, :], 0.0, (rg_hst_b[j],))
        for i in range(4):
            memset(DVE, ffn_tail[i][:, :, :], 0.0, (ffn_tail_b[i],))

        ones_ln = CM[:, M_ONE:M_ONE + 128]
        cp(DVE, ONB[:, :], ones_ln, (CM_b,), (CM_b,))
        cp(DVE, ONB2[:, :], CM[:, M_ONE2:M_ONE2 + 128], (CM_b,), (CM_b,))
        ones_rms = CM[:, M_ONE2:M_ONE2 + 128]

        def cvc(off):
            return CV[:, off:off + 1]

        def rows_in(dram, R, C, dst_fn, dst_bufs):
            for s0 in range(0, C, SEGW):
                sw = min(SEGW, C - s0)
                dma(SP, stA_t, stA[0:R, 0:sw], dram[:, s0:s0 + sw], writes=(stA_b,))
                nch = sw // 128
                g = max(1, min(nch, 512 // R))
                c0 = 0
                while c0 < nch:
                    n = min(g, nch - c0)
                    pt = ps1p.alloc()
                    for k in range(n):
                        tr(pt[0][:, k * R:(k + 1) * R], stA[0:R, (c0 + k) * 128:(c0 + k + 1) * 128], ident[0:R, 0:R],
                           (stA_b, CM_b), (pt[1],), inc=(k == n - 1))
                    cp(DVE, dst_fn(s0 // 128 + c0, n), pt[0][:, 0:n * R].rearrange("p (n r) -> p n r", r=R), (pt[1],), dst_bufs)
                    ps1p.free(pt)
                    c0 += n

        def rows_out(src_fn, src_bufs, R, C, dram):
            for s0 in range(0, C, SEGW):
                sw = min(SEGW, C - s0)
                nch = sw // 128
                c0 = 0
                while c0 < nch:
                    n = min(4, nch - c0)
                    pt = ps1p.alloc()
                    for k in range(n):
                        tr(pt[0][0:R, k * 128:(k + 1) * 128], src_fn(s0 // 128 + c0 + k), ident, tuple(src_bufs) + (CM_b,), (pt[1],),
                           inc=(k == n - 1))
                    cp(ACT, stA[0:R, c0 * 128:(c0 + n) * 128], pt[0][0:R, 0:n * 128], (pt[1],), (stA_b,))
                    ps1p.free(pt)
                    c0 += n
                dma(SP, stA_t, dram[:, s0:s0 + sw], stA[0:R, 0:sw], reads=(stA_b,))

        def proj(ps_tile, lhs_fn, rhs_t, rhs_bufs, nk, wbuf, last_inc=True):
            pv = ps2(ps_tile)
            for k in range(nk):
                mm(pv[:, 0:NP_], lhs_fn(k), rhs_t[:, k, 0:NP_], k == 0, k == nk - 1,
                   (wbuf, rhs_bufs[k]), (ps_tile[1][0],))
            for k in range(nk):
                mm(pv[:, 512:512 + NS], lhs_fn(k), rhs_t[:, k, NP_:NCOL], k == 0, k == nk - 1,
                   (wbuf, rhs_bufs[k]), (ps_tile[1][1],), inc=(last_inc and k == nk - 1))

        def pcols(ps_tile):
            pv = ps2(ps_tile)
            return pv[:, 0:NP_], pv[:, 512:512 + NS]

        ln_state = {}

        def ln_stats_begin():
            pm = ps2p.alloc()
            pq = ps2p.alloc()
            ln_state.update(pm=pm, pq=pq, pend=None)

        def ln_stats_prep(c):
            zb = tbp.alloc()
            zq = tbp.alloc()
            cp(DVE, zb[0][:, 0:NCOL], xf[:, c, :], (xf_b[c],), (zb[1],))
            act(zq[0][:, 0:NCOL], xf[:, c, :], AF.Square, (xf_b[c],), (zq[1],))
            return (c, zb, zq)

        def ln_stats_mm(prep):
            c, zb, zq = prep
            pm, pq = ln_state["pm"], ln_state["pq"]
            for (pt, src) in ((pm, zb), (pq, zq)):
                ptv = ps2(pt)
                mm(ptv[:, 0:NP_], ONB[:, :], src[0][:, 0:NP_], c == 0, c == 7, (CM_b, src[1]), (pt[1][0],))
                mm(ptv[:, 512:512 + NS], ONB[:, :], src[0][:, NP_:NCOL], c == 0, c == 7, (CM_b, src[1]), (pt[1][1],),
                   inc=True)
            tbp.free(zb)
            tbp.free(zq)

        def layer_norm(i, s):
            if "pm" not in ln_state:
                ln_stats_begin()
                for c in range(8):
                    ln_stats_mm(ln_stats_prep(c))
            pm, pq = ln_state.pop("pm"), ln_state.pop("pq")
            ln_state.clear()
            pmv, pqv = ps2(pm), ps2(pq)
            m2, rstd, nmr = tfp.alloc(), tfp.alloc(), tfp.alloc()
            for (lo, hi, plo, hb) in ((0, NP_, 0, 0), (NP_, NCOL, 512, 1)):
                n = hi - lo
                act(m2[0][:, lo:hi], pmv[:, plo:plo + n], AF.Square, (pm[1][hb],), (m2[1],))
                tt(DVE, m2[0][:, lo:hi], pqv[:, plo:plo + n], m2[0][:, lo:hi], ALU.subtract, (pq[1][hb], m2[1]), (m2[1],))
            act(rstd[0][:, 0:NCOL], m2[0][:, 0:NCOL], AF.Ln, (m2[1],), (rstd[1],), bias=LN_EPS)
            act(rstd[0][:, 0:NCOL], rstd[0][:, 0:NCOL], AF.Exp, (rstd[1],), (rstd[1],), scale=-0.5)
            for (lo, hi, plo, hb) in ((0, NP_, 0, 0), (NP_, NCOL, 512, 1)):
                n = hi - lo
                stt(DVE, nmr[0][:, lo:hi], pmv[:, plo:plo + n], -1.0, rstd[0][:, lo:hi], ALU.mult, ALU.mult,
                    (pm[1][hb], rstd[1]), (nmr[1],))
            ps2p.free(pm)
            ps2p.free(pq)
            tfp.free(m2)
            for c in range(8):
                t = tfp.alloc()
                tt(DVE, t[0][:, 0:NCOL], xf[:, c, :], rstd[0][:, 0:NCOL], ALU.mult, (xf_b[c], rstd[1]), (t[1],))
                tt(POOL, t[0][:, 0:NCOL], t[0][:, 0:NCOL], nmr[0][:, 0:NCOL], ALU.add, (t[1], nmr[1]), (t[1],))
                g_ap = cvc(O_LNG + (i * 2 + s) * 8 + c)
                b_ap = cvc(O_LNB + (i * 2 + s) * 8 + c)
                act(xf[:, c, :], t[0][:, 0:NCOL], AF.Identity, (t[1], CV_b), (xf_b[c],), bias=b_ap, scale=g_ap)
                act(xb[:, c, :], t[0][:, 0:NCOL], AF.Identity, (t[1], CV_b), (xb_b[c],), bias=b_ap, scale=g_ap)
                tfp.free(t)
            tfp.free(rstd)
            tfp.free(nmr)

        def out_proj_and_z(src_t, src_bufs, nk, items_fn):
            fuse_ln = DBG["ln"]
            if fuse_ln:
                ln_stats_begin()
            prep_prev = None
            for o in range(8):
                lhs_fn, wbuf, rel = items_fn(o)
                pt = ps2p.alloc()
                proj(pt, lhs_fn, src_t, src_bufs, nk, wbuf)
                if rel is not None:
                    rel()
                if prep_prev is not None:
                    ln_stats_mm(prep_prev)
                    prep_prev = None
                a, b = pcols(pt)
                stt(DVE, xf[:, o, 0:NP_], xf[:, o, 0:NP_], ALPHA, a, ALU.mult, ALU.add, (xf_b[o], pt[1][0]), (xf_b[o],))
                stt(DVE, xf[:, o, NP_:NCOL], xf[:, o, NP_:NCOL], ALPHA, b, ALU.mult, ALU.add, (xf_b[o], pt[1][1]), (xf_b[o],))
                ps2p.free(pt)
                if fuse_ln:
                    prep_prev = ln_stats_prep(o)
            if prep_prev is not None:
                ln_stats_mm(prep_prev)

        def std_out_items(nhalf_getter):
            cache = {}

            def f(o):
                hh = o // 4
                if hh not in cache:
                    cache[hh] = w_next()
                ws, wb_, _k = cache[hh]
                ol = o % 4
                rel = (lambda: w_release(cache[hh])) if ol == 3 else None
                return (lambda k: ws[:, k * 512 + ol * 128:k * 512 + (ol + 1) * 128]), wb_, rel
            return f

        def rglru(i, p):
            j = i // 2
            wg = None
            for c in range(8):
                cl = c % 4
                if cl == 0:
                    wg = w_next()
                pg = ps2p.alloc()
                proj(pg, lambda k: wg[0][:, k * 512 + cl * 128:k * 512 + (cl + 1) * 128], xb, xb_b, 8, wg[1])
                act(mixo[:, c, :], ps2(pg)[:, 0:NCOL], AF.Gelu_apprx_tanh, pg[1], (mixo_b[c],))
                ps2p.free(pg)
                if cl == 3:
                    w_release(wg)
            wst = {}

            def s1(c, cx):
                cl = c % 4
                if cl == 0:
                    wst["wx"] = w_next()
                    if c == 0:
                        wst["gw"] = w_next()
                wx = wst["wx"]
                px = ps2p.alloc()
                proj(px, lambda k: wx[0][:, k * 512 + cl * 128:k * 512 + (cl + 1) * 128], xb, xb_b, 8, wx[1])
                if cl == 3:
                    w_release(wx)
                XB = tfp.alloc()
                xbs = XB[0][:, 515:515 + SQP * 11].rearrange("p (s k) -> p s k", k=11)
                pxa, pxb = pcols(px)
                cp(DVE, XB[0][:, 0:3], rg_tail[j][:, c, :], (rg_tail_b[j],), (XB[2],))
                cp(DVE, xbs[:, :, 0:3], cv0s[:, j, c, :].rearrange("p (s k) -> p s k", k=3), (cv0s_b,), (XB[2],))
                cp(ACT, XB[0][:, 3:515], pxa, (px[1][0],), (XB[1],))
                cp(ACT, xbs[:, :, 3:11], pxb.rearrange("p (s k) -> p s k", k=8), (px[1][1],), (XB[1],))
                ps2p.free(px)
                yield
                cp(POOL, rg_tail[j][:, c, :], XB[0][:, 512:515], (XB[1],), (rg_tail_b[j],))
                cp(POOL, cv_o[:, j, c, :].rearrange("p (s k) -> p s k", k=3), xbs[:, :, 8:11], (XB[1],), (cv_o_b,))
                XC = tfp.alloc()
                xcs = XC[0][:, NP_:NCOL].rearrange("p (s k) -> p s k", k=8)
                wcol = lambda tap: cvc(O_RCW + (j * 4 + tap) * 8 + c)
                bcol = cvc(O_RCB + j * 8 + c)
                act(XC[0][:, 0:NP_], XB[0][:, 0:512], AF.Identity, (XB[1], XB[2], CV_b), (XC[1],), bias=bcol, scale=wcol(0))
                ts(POOL, xcs, xbs[:, :, 0:8], wcol(0), bcol, ALU.mult, ALU.add, (XB[1], XB[2], CV_b), (XC[2],))
                yield
                for tap in (1, 2, 3):
                    stt(DVE, XC[0][:, 0:NP_], XB[0][:, tap:tap + 512], wcol(tap), XC[0][:, 0:NP_], ALU.mult, ALU.add,
                        (XB[1], XB[2], CV_b, XC[1]), (XC[1],))
                    stt(DVE, xcs, xbs[:, :, tap:tap + 8], wcol(tap), xcs, ALU.mult, ALU.add, (XB[1], XB[2], CV_b, XC[2]), (XC[2],))
                tfp.free(XB)
                yield
                XCB = tbp.alloc()
                cp(ACT, XCB[0][:, 0:NCOL], XC[0][:, 0:NCOL], (XC[1], XC[2]), (XCB[1],))
                cx.update(XC=XC, XCB=XCB)
                yield

            def s2(cx):
                c, XCB = cx["c"], cx["XCB"]
                gw = wst["gw"]
                gwv = gw[0][:, 0:2048].rearrange("p (a n) -> p a n", n=128)
                pr = ps2p.alloc()
                pi = ps2p.alloc()
                for (pt, gi) in ((pr, 0), (pi, 1)):
                    ptv = ps2(pt)
                    mm(ptv[:, 0:NP_], gwv[:, gi * 8 + c, :], XCB[0][:, 0:NP_], True, True, (gw[1], XCB[1]), (pt[1][0],))
                    mm(ptv[:, 512:512 + NS], gwv[:, gi * 8 + c, :], XCB[0][:, NP_:NCOL], True, True, (gw[1], XCB[1]), (pt[1][1],), inc=True)
                tbp.free(XCB)
                yield
                if c == 7:
                    w_release(gw)
                R, IG, A = tfp.alloc(), tfp.alloc(), tfp.alloc()
                act(R[0][:, 0:NCOL], ps2(pr)[:, 0:NCOL], AF.Tanh, tuple(pr[1]) + (CV_b,), (R[1],),
                    bias=cvc(O_HGB + (j * 2 + 0) * 8 + c), scale=0.5)
                act(IG[0][:, 0:NCOL], ps2(pi)[:, 0:NCOL], AF.Tanh, tuple(pi[1]) + (CV_b,), (IG[1],),
                    bias=cvc(O_HGB + (j * 2 + 1) * 8 + c), scale=0.5)
                ps2p.free(pr)
                ps2p.free(pi)
                yield
                act(A[0][:, 0:NCOL], R[0][:, 0:NCOL], AF.Exp, (R[1], CV_b), (A[1],),
                    bias=cvc(O_HC1 + j * 8 + c), scale=cvc(O_HC1 + j * 8 + c))
                act(R[0][:, 0:NCOL], R[0][:, 0:NCOL], AF.Exp, (R[1], CV_b), (R[1],),
                    bias=cvc(O_C1 + j * 8 + c), scale=cvc(O_C1 + j * 8 + c))
                yield
                act(R[0][:, 0:NCOL], R[0][:, 0:NCOL], AF.Ln, (R[1],), (R[1],), bias=1.0, scale=-1.0)
                act(R[0][:, 0:NCOL], R[0][:, 0:NCOL], AF.Exp, (R[1],), (R[1],), scale=0.5)
                cx.update(R=R, IG=IG, A=A)
                yield

            def s3(cx):
                c, XC, R, IG, A = cx["c"], cx["XC"], cx["R"], cx["IG"], cx["A"]
                stt(DVE, IG[0][:, 0:NCOL], IG[0][:, 0:NCOL], 1.0, R[0][:, 0:NCOL], ALU.add, ALU.mult, (IG[1], R[1]), (IG[1],))
                stt(DVE, IG[0][:, 0:NCOL], IG[0][:, 0:NCOL], 0.5, XC[0][:, 0:NCOL], ALU.mult, ALU.mult,
                    (IG[1], XC[1], XC[2]), (IG[1],))
                tfp.free(XC)
                yield
                As = A[0][:, NP_:NCOL].rearrange("p (s k) -> p s k", k=8)
                Bs = IG[0][:, NP_:NCOL].rearrange("p (s k) -> p s k", k=8)
                tmp = R[0][:, 0:SQP].rearrange("p (s k) -> p s k", k=1)
                tt(DVE, tmp, As[:, :, 0:1], h0s[:, j, c, :].rearrange("p (s k) -> p s k", k=1), ALU.mult, (A[1], h0s_b, R[1]), (R[1],))
                tt(DVE, Bs[:, :, 0:1], Bs[:, :, 0:1], tmp, ALU.add, (IG[1], R[1]), (IG[1],))
                memset(DVE, As[:, :, 0:1], 0.0, (A[1],))
                yield
                H = R
                scan(H[0][:, 0:NP_], A[0][:, 0:NP_], IG[0][:, 0:NP_], rg_hst[j][:, c:c + 1], (A[1], IG[1], rg_hst_b[j]), (H[1],))
                scan(H[0][:, NP_:NCOL], A[0][:, NP_:NCOL], IG[0][:, NP_:NCOL], 0.0, (A[1], IG[1]), (H[1],))
                yield
                cp(POOL, rg_hst[j][:, c:c + 1], H[0][:, NP_ - 1:NP_], (H[1],), (rg_hst_b[j],))
                Hs = H[0][:, NP_:NCOL].rearrange("p (s k) -> p s k", k=8)
                cp(POOL, hs_o[:, j, c, :].rearrange("p (s k) -> p s k", k=1), Hs[:, :, 7:8], (H[1],), (hs_o_b,))
                tt(DVE, mixo[:, c, :], H[0][:, 0:NCOL], mixo[:, c, :], ALU.mult, (H[1], mixo_b[c]), (mixo_b[c],))
                tfp.free(R)
                tfp.free(IG)
                tfp.free(A)

            ctxs = [dict(c=c) for c in range(8)]
            for t in range(8 + 2):
                gens = []
                if 0 <= t - 2 < 8:
                    gens.append(s3(ctxs[t - 2]))
                if 0 <= t - 1 < 8:
                    gens.append(s2(ctxs[t - 1]))
                if t < 8:
                    gens.append(s1(t, ctxs[t]))
                run_interleaved(gens)
            out_proj_and_z(mixo, mixo_b, 8, std_out_items(None))
            rows_out(lambda c: hs_o[:, j, c, :], (hs_o_b,), SQP, D, orh_d[j, p * SQP:(p + 1) * SQP, :])
            rows_out(lambda c: cv_o[:, j, c, :], (cv_o_b,), SQP * 3, D, orc_d[j, p * SQP * 3:(p + 1) * SQP * 3, :])
            if p == NPASS - 1:
                rows_out(lambda c: rg_hst[j][:, c:c + 1], (rg_hst_b[j],), 1, D, prh_d[j:j + 1, :])
                rows_out(lambda c: rg_tail[j][:, c, :], (rg_tail_b[j],), 3, D, prc_d[j])

        def ffn(i, p):
            def stage_a(jc, cl, w, wg, wu):
                pg = ps2p.alloc()
                pu = ps2p.alloc()
                proj(pg, lambda k: wg[0][:, k * w + cl * 128:k * w + (cl + 1) * 128], xb, xb_b, 8, wg[1])
                proj(pu, lambda k: wu[0][:, k * w + cl * 128:k * w + (cl + 1) * 128], xb, xb_b, 8, wu[1])
                GB = tfp.alloc()
                gbs = GB[0][:, 514:514 + SQP * 10].rearrange("p (s k) -> p s k", k=10)
                pga, pgb = pcols(pg)
                cp(DVE, GB[0][:, 0:2], ffn_tail[i][:, jc, :], (ffn_tail_b[i],), (GB[2],))
                cp(DVE, gbs[:, :, 0:2], fc0s[:, i, jc, :].rearrange("p (s k) -> p s k", k=2), (fc0s_b,), (GB[2],))
                cp(ACT, GB[0][:, 2:514], pga, (pg[1][0],), (GB[1],))
                cp(ACT, gbs[:, :, 2:10], pgb.rearrange("p (s k) -> p s k", k=8), (pg[1][1],), (GB[1],))
                ps2p.free(pg)
                UP = tfp.alloc()
                cp(DVE, UP[0][:, 0:NCOL], ps2(pu)[:, 0:NCOL], pu[1], (UP[1],))
                ps2p.free(pu)
                cp(POOL, ffn_tail[i][:, jc, :], GB[0][:, 512:514], (GB[1],), (ffn_tail_b[i],))
                cp(POOL, fc_o[:, i, jc, :].rearrange("p (s k) -> p s k", k=2), gbs[:, :, 8:10], (GB[1],), (fc_o_b,))
                AC = tfp.alloc()
                acs = AC[0][:, NP_:NCOL].rearrange("p (s k) -> p s k", k=8)
                bcol = cvc(O_FCB + i * NJ + jc)
                w0 = cvc(O_FCW + (i * 3 + 0) * NJ + jc)
                act(AC[0][:, 0:NP_], GB[0][:, 0:512], AF.Identity, (GB[1], GB[2], CV_b), (AC[1],), bias=bcol, scale=w0)
                ts(POOL, acs, gbs[:, :, 0:8], w0, bcol, ALU.mult, ALU.add, (GB[1], GB[2], CV_b), (AC[2],))
                return dict(jc=jc, GB=GB, gbs=gbs, AC=AC, acs=acs, UP=UP)

            def stage_b(cx):
                jc, GB, gbs, AC, acs, UP = cx["jc"], cx["GB"], cx["gbs"], cx["AC"], cx["acs"], cx["UP"]
                for tap in (1, 2):
                    wt = cvc(O_FCW + (i * 3 + tap) * NJ + jc)
                    stt(DVE, AC[0][:, 0:NP_], GB[0][:, tap:tap + 512], wt, AC[0][:, 0:NP_], ALU.mult, ALU.add,
                        (GB[1], GB[2], CV_b, AC[1]), (AC[1],))
                    stt(DVE, acs, gbs[:, :, tap:tap + 8], wt, acs, ALU.mult, ALU.add, (GB[1], GB[2], CV_b, AC[2]), (AC[2],))
                tfp.free(GB)
                act(AC[0][:, 0:NCOL], AC[0][:, 0:NCOL], AF.Gelu_apprx_tanh, (AC[1], AC[2]), (AC[1], AC[2]))
                tt(POOL, hff[:, jc, :], UP[0][:, 0:NCOL], AC[0][:, 0:NCOL], ALU.mult, (UP[1], AC[1], AC[2]), (hff_b[jc],))
                tfp.free(UP)
                tfp.free(AC)

            pending = None
            for q in range(6):
                w = 512 if q < 5 else 256
                wg = w_next()
                wu = w_next()
                ncl = w // 128
                for cl in range(ncl):
                    cx = stage_a(q * 4 + cl, cl, w, wg, wu)
                    if cl == ncl - 1:
                        w_release(wg)
                        w_release(wu)
                    if pending is not None:
                        stage_b(pending)
                    pending = cx
            stage_b(pending)

            def items(o):
                w_ = w_next()
                return (lambda k: w_[0][:, k * 128:(k + 1) * 128]), w_[1], (lambda: w_release(w_))
            out_proj_and_z(hff, hff_b, NJ, items)
            rows_out(lambda c: fc_o[:, i, c, :], (fc_o_b,), SQP * 2, DFF, ofc_d[i, p * SQP * 2:(p + 1) * SQP * 2, :])
            if p == NPASS - 1:
                rows_out(lambda c: ffn_tail[i][:, c, :], (ffn_tail_b[i],), 2, DFF, pfc_d[i])

        def hgrn(i, p):
            j = i // 2
            for s_ in range(SQP):
                dma(SP, S0f_t, S0f[:, s_, :, :], shs_d[j, p * SQP + s_].rearrange("h k v -> k h v"), writes=(S0f_b,))
            for s_ in range(SQP):
                dma(POOL, S0b_t, S0b[:, s_, :, :], shs_d[j, p * SQP + s_].rearrange("h k v -> k h v"), writes=(S0b_b,))
            rst = CM[:, M_RST:M_RST + NCOL]

            def proj_gen(h, ctx):
                wit = w_next()
                ws, wb_ = wit[0], wit[1]
                wv = ws[:, 0:4096].rearrange("p (k t n) -> p k t n", t=4, n=128)
                ctx["wv"], ctx["wb"], ctx["wit"] = wv, wb_, wit
                pq = ps2p.alloc()
                proj(pq, lambda k: wv[:, k, 0, :], xb, xb_b, 8, wb_)
                TQ = tfp.alloc()
                act(TQ[0][:, 0:NCOL], ps2(pq)[:, 0:NCOL], AF.Tanh, pq[1], (TQ[1],), scale=0.5)
                stt(DVE, TQ[0][:, 0:NCOL], TQ[0][:, 0:NCOL], 1.0, ps2(pq)[:, 0:NCOL], ALU.add, ALU.mult,
                    (TQ[1],) + tuple(pq[1]), (TQ[1],))
                ps2p.free(pq)
                yield
                pf = ps2p.alloc()
                proj(pf, lambda k: wv[:, k, 1, :], xb, xb_b, 8, wb_)
                T1 = tfp.alloc()
                act(T1[0][:, 0:NCOL], ps2(pf)[:, 0:NCOL], AF.Tanh, pf[1], (T1[1],), scale=0.5)
                ps2p.free(pf)
                yield
                pg = ps2p.alloc()
                proj(pg, lambda k: wv[:, k, 3, :], xb, xb_b, 8, wb_)
                SG = tfp.alloc()
                act(SG[0][:, 0:NCOL], ps2(pg)[:, 0:NCOL], AF.Tanh, pg[1], (SG[1],), scale=0.5)
                stt(DVE, SG[0][:, 0:NCOL], SG[0][:, 0:NCOL], 1.0, ps2(pg)[:, 0:NCOL], ALU.add, ALU.mult,
                    (SG[1],) + tuple(pg[1]), (SG[1],))
                ps2p.free(pg)
                ctx["SG"] = SG
                yield
                T2, T3 = tfp.alloc(), tfp.alloc()
                ts(DVE, T1[0][:, 0:NCOL], T1[0][:, 0:NCOL], cvc(O_FA + j * 8 + h), cvc(O_FB + j * 8 + h), ALU.mult, ALU.add,
                   (T1[1], CV_b), (T1[1],))
                yield
                yield
                act(T2[0][:, 0:NCOL], T1[0][:, 0:NCOL], AF.Ln, (T1[1],), (T2[1],), bias=1e-30)
                yield
                ts(DVE, T1[0][:, 0:NCOL], T1[0][:, 0:NCOL], -1.0, 1.0, ALU.mult, ALU.add, (T1[1],), (T1[1],))
                yield
                scan(T3[0][:, 0:NCOL], rst, T2[0][:, 0:NCOL], 0.0, (CM_b, T2[1]), (T3[1],))
                yield
                EC = tfp.alloc()
                act(EC[0][:, 0:NCOL], T3[0][:, 0:NCOL], AF.Exp, (T3[1],), (EC[1],))
                act(T2[0][:, 0:NCOL], T3[0][:, 0:NCOL], AF.Exp, (T3[1],), (T2[1],), scale=-1.0)
                yield
                ctx["EC"] = EC
                tfp.free(T3)
                QT = tbp.alloc()
                stt(DVE, QT[0][:, 0:NCOL], TQ[0][:, 0:NCOL], 0.5 * 128.0 ** -0.5, EC[0][:, 0:NCOL], ALU.mult, ALU.mult,
                    (TQ[1], EC[1]), (QT[1],))
                yield
                ctx["QT"] = QT
                tfp.free(TQ)
                tt(DVE, T2[0][:, 0:NCOL], T1[0][:, 0:NCOL], T2[0][:, 0:NCOL], ALU.mult, (T1[1], T2[1]), (T2[1],))
                yield
                tfp.free(T1)
                KT = tbp.alloc()
                cp(ACT, KT[0][:, 0:NCOL], T2[0][:, 0:NCOL], (T2[1],), (KT[1],))
                yield
                KH = tfp.alloc()
                tt(DVE, KH[0][:, 0:NP_].rearrange("p (c k) -> p c k", k=64), T2[0][:, 0:NP_].rearrange("p (c k) -> p c k", k=64),
                   EC[0][:, 63:NP_:64].unsqueeze(2).broadcast_to([128, 8, 64]), ALU.mult, (T2[1], EC[1]), (KH[1],))
                tt(DVE, KH[0][:, NP_:NCOL].rearrange("p (c k) -> p c k", k=8), T2[0][:, NP_:NCOL].rearrange("p (c k) -> p c k", k=8),
                   EC[0][:, NP_ + 7:NCOL:8].unsqueeze(2).broadcast_to([128, SQP, 8]), ALU.mult, (T2[1], EC[1]), (KH[1],))
                tfp.free(T2)
                yield
                pvt = ps2p.alloc()
                pv = (pvt[0][:, 0, :], pvt[1][0])
                pvs = (pvt[0][:, 1, :], pvt[1][1])
                for tbk in range(4):
                    for k in range(8):
                        mm(pv[0][:, tbk * 128:(tbk + 1) * 128], xb[:, k, tbk * 128:(tbk + 1) * 128], wv[:, k, 2, :], k == 0, k == 7,
                           (xb_b[k], wb_), (pv[1],), inc=(tbk == 3 and k == 7))
                    if tbk % 2 == 1:
                        yield
                for k in range(8):
                    mm(pvs[0][0:NS, 0:128], xb[:, k, NP_:NCOL], wv[:, k, 2, :], k == 0, k == 7, (xb_b[k], wb_), (pvs[1],), inc=(k == 7))
                w_release(ctx["wit"])
                VT = tbp.alloc()
                vtv = VT[0][:, 0:640].rearrange("p (b n) -> p b n", n=128)
                cp(ACT, vtv[:, 0:4, :], pv[0][:, 0:512].rearrange("p (b n) -> p b n", n=128), (pv[1],), (VT[1],))
                cp(ACT, vtv[0:NS, 4, :], pvs[0][0:NS, 0:128], (pvs[1],), (VT[1],))
                yield
                ctx["VT"] = VT
                for sc in range(4):
                    mm(pv[0][:, sc * 128:(sc + 1) * 128], KT[0][:, sc * 128:(sc + 1) * 128], QT[0][:, sc * 128:(sc + 1) * 128], True, True,
                       (KT[1], QT[1]), (pv[1],), inc=(sc == 3))
                mm(pvs[0][0:NS, 0:NS], KT[0][:, NP_:NCOL], QT[0][:, NP_:NCOL], True, True, (KT[1], QT[1]), (pvs[1],), inc=True)
                AB = tbp.alloc()
                tt(DVE, AB[0][:, 0:512].rearrange("p (b n) -> p b n", n=128), pv[0][:, 0:512].rearrange("p (b n) -> p b n", n=128),
                   CM[:, M_A2:M_A2 + 128].unsqueeze(1).broadcast_to([128, 4, 128]), ALU.mult, (pv[1], CM_b), (AB[1],))
                tt(DVE, AB[0][0:NS, 512:512 + NS], pvs[0][0:NS, 0:NS], CM[0:NS, M_AS:M_AS + NS], ALU.mult, (pvs[1], CM_b), (AB[1],))
                yield
                ctx["AB"] = AB
                tbp.free(KT)
                yield
                for sc in range(4):
                    tr(pv[0][:, sc * 128:(sc + 1) * 128], KH[0][:, sc * 128:(sc + 1) * 128], ident, (KH[1], CM_b), (pv[1],), inc=(sc == 3))
                tr(pvs[0][0:NS, 0:128], KH[0][:, NP_:NCOL], ident, (KH[1], CM_b), (pvs[1],), inc=True)
                KK = tbp.alloc()
                cp(ACT, KK[0][:, 0:512], pv[0][:, 0:512], (pv[1],), (KK[1],))
                cp(ACT, KK[0][0:NS, 512:640], pvs[0][0:NS, 0:128], (pvs[1],), (KK[1],))
                yield
                ctx["KK"] = KK
                tfp.free(KH)
                ps2p.free(pvt)
                VB = tbp.alloc()
                tt(DVE, VB[0][0:NS, 0:512].rearrange("p (s n) -> p s n", n=128),
                   vtv[0:NS, 4, :].unsqueeze(1).broadcast_to([NS, SQP, 128]),
                   CM[0:NS, M_SM:M_SM + SQP].unsqueeze(2).broadcast_to([NS, SQP, 128]), ALU.mult, (VT[1], CM_b), (VB[1],))
                ctx["VB"] = VB
                yield

            def chain_gen(h, ctx):
                wv, wb_ = ctx["wv"], ctx["wb"]
                EC, QT, VT, AB, KK, VB = ctx["EC"], ctx["QT"], ctx["VT"], ctx["AB"], ctx["KK"], ctx["VB"]
                vtv = VT[0][:, 0:640].rearrange("p (b n) -> p b n", n=128)
                kkv = KK[0][:, 0:640].rearrange("p (b n) -> p b n", n=128)
                po = ps2p.alloc()
                pov = ps2(po)
                for sc in range(4):
                    mm(pov[:, sc * 128:(sc + 1) * 128], vtv[:, sc, :], AB[0][:, sc * 128:(sc + 1) * 128], sc == 0, False,
                       (VT[1], AB[1]), (po[1][0],))
                    for cc in range(2):
                        cch = sc * 2 + cc
                        lo = cch * 64
                        mm(pov[:, lo:lo + 64], Sbf[j][:, h, :], QT[0][:, lo:lo + 64], False, (cch == 7),
                           (Sbf_b[j][h], QT[1]), (po[1][0],))
                        pd = psp_alloc1()
                        mm(pd[0][:, 0:128], kkv[cc * 64:(cc + 1) * 64, sc, :], vtv[cc * 64:(cc + 1) * 64, sc, :], True, True,
                           (KK[1], VT[1]), (pd[1],), inc=True)
                        stt(DVE, Sst[j][:, h, :], Sst[j][:, h, :], EC[0][:, lo + 63:lo + 64], pd[0][:, 0:128], ALU.mult, ALU.add,
                            (Sst_b[j][h], EC[1], pd[1]), (Sst_b[j][h],))
                        psp_free1(pd)
                        cp(DVE, Sbf[j][:, h, :], Sst[j][:, h, :], (Sst_b[j][h],), (Sbf_b[j][h],))
                        yield
                mm(pov[:, 512:512 + NS], vtv[0:NS, 4, :], AB[0][0:NS, 512:512 + NS], True, False, (VT[1], AB[1]), (po[1][1],))
                for s_ in range(SQP):
                    lo = NP_ + s_ * 8
                    mm(pov[:, 512 + s_ * 8:512 + (s_ + 1) * 8], S0b[:, s_, h, :], QT[0][:, lo:lo + 8], False, s_ == SQP - 1,
                       (S0b_b, QT[1]), (po[1][1],), inc=(s_ == SQP - 1))
                pd = psp_alloc1()
                mm(pd[0][:, 0:512], kkv[0:NS, 4, :], VB[0][0:NS, 0:512], True, True, (KK[1], VB[1]), (pd[1],), inc=True)
                for s_ in range(SQP):
                    lo = NP_ + s_ * 8
                    stt(DVE, S0f[:, s_, h, :], S0f[:, s_, h, :], EC[0][:, lo + 7:lo + 8], pd[0][:, s_ * 128:(s_ + 1) * 128], ALU.mult, ALU.add,
                        (S0f_b, EC[1], pd[1]), (S0f_b,))
                psp_free1(pd)
                tbp.free(KK)
                tbp.free(VB)
                tbp.free(AB)
                tbp.free(VT)
                tbp.free(QT)
                tfp.free(EC)
                yield
                OS = tfp.alloc()
                OQ = tbp.alloc()
                act(OQ[0][:, 0:NCOL], ps2(po)[:, 0:NCOL], AF.Square, po[1], (OQ[1],))
                pm = ps2p.alloc()
                pmv = ps2(pm)
                mm(pmv[:, 0:NP_], ONB2[:, :], OQ[0][:, 0:NP_], True, True, (CM_b, OQ[1]), (pm[1][0],))
                mm(pmv[:, 512:512 + NS], ONB2[:, :], OQ[0][:, NP_:NCOL], True, True, (CM_b, OQ[1]), (pm[1][1],), inc=True)
                tbp.free(OQ)
                yield
                act(OS[0][:, 0:NCOL], pmv[:, 0:NCOL], AF.Ln, pm[1], (OS[1],), bias=RMS_EPS)
                ps2p.free(pm)
                act(OS[0][:, 0:NCOL], OS[0][:, 0:NCOL], AF.Exp, (OS[1],), (OS[1],), scale=-0.5)
                ON = tfp.alloc()
                stt(DVE, ON[0][:, 0:NCOL], ps2(po)[:, 0:NCOL], cvc(O_HNH + j), OS[0][:, 0:NCOL], ALU.mult, ALU.mult,
                    tuple(po[1]) + (CV_b, OS[1]), (ON[1],))
                ps2p.free(po)
                SG = ctx["SG"]
                tt(DVE, mixo[:, h, :], ON[0][:, 0:NCOL], SG[0][:, 0:NCOL], ALU.mult, (ON[1], SG[1]), (mixo_b[h],))
                tfp.free(OS)
                tfp.free(ON)
                tfp.free(SG)
                yield

            ctxs = [dict() for _ in range(8)]
            run_interleaved([proj_gen(0, ctxs[0])])
            for h in range(8):
                run_interleaved([chain_gen(h, ctxs[h]), proj_gen(h + 1, ctxs[h + 1]) if h + 1 < 8 else None])
            out_proj_and_z(mixo, mixo_b, 8, std_out_items(None))
            for s_ in range(SQP):
                dma(SP, S0o_t, ohs_d[j, p * SQP + s_].rearrange("h k v -> k h v"), S0f[:, s_, :, :], reads=(S0f_b,))
            if p == NPASS - 1:
                dma(SP, Spo_t, phs_d[j].rearrange("h k v -> k h v"), Sst[j][:, :, :], reads=tuple(Sst_b[j]))

        def psp_alloc1():
            return ps1p.alloc()

        def psp_free1(t):
            ps1p.free(t)

        def load_x(p):
            if DBG.get("p0", False):
                p = 0
            for tbk in range(DBG.get("ntbk", 4)):
                st, stb, stt_ = (stA, stA_b, stA_t) if tbk % 2 == 0 else (stB, stB_b, stB_t)
                dma(SP, stt_, st[:, 0:D], xp_d[p * NP_ + tbk * 128:p * NP_ + (tbk + 1) * 128, :], writes=(stb,))
                pt = ps2p.alloc()
                ptv = ps2(pt)
                for c in range(8):
                    tr(ptv[:, c * 128:(c + 1) * 128], st[:, c * 128:(c + 1) * 128], ident, (stb, CM_b), (pt[1][c // 4],), inc=(c % 4 == 3))
                cp(ACT, xf[:, :, tbk * 128:(tbk + 1) * 128], ptv.rearrange("p (c n) -> p c n", n=128), pt[1], tuple(xf_b))
                cp(DVE, xb[:, :, tbk * 128:(tbk + 1) * 128], xf[:, :, tbk * 128:(tbk + 1) * 128], tuple(xf_b), tuple(xb_b))
                ps2p.free(pt)
            if DBG.get("no_xs", False):
                return
            dma(SP, stA_t, stA[0:NS, 0:D], xs_d[p * NS:(p + 1) * NS, :], writes=(stA_b,))
            pt = ps1p.alloc()
            for c in range(8):
                tr(pt[0][:, c * NS:(c + 1) * NS], stA[0:NS, c * 128:(c + 1) * 128], ident[0:NS, 0:NS], (stA_b, CM_b), (pt[1],), inc=(c == 7))
            cp(ACT, xf[:, :, NP_:NCOL], pt[0][:, 0:8 * NS].rearrange("p (c n) -> p c n", n=NS), (pt[1],), tuple(xf_b))
            cp(DVE, xb[:, :, NP_:NCOL], xf[:, :, NP_:NCOL], tuple(xf_b), tuple(xb_b))
            ps1p.free(pt)
            if not DBG.get("io_rows", True):
                return
            for j in range(2):
                dma(SP, stB_t, stB[4 * j:4 * j + 4, 0:D], srh_d[j, p * SQP:(p + 1) * SQP, :], writes=(stB_b,))
            for j in range(2):
                dma(SP, stB_t, stB[32 + 12 * j:32 + 12 * j + 12, 0:D], src_d[j, p * SQP * 3:(p + 1) * SQP * 3, :], writes=(stB_b,))
            for g in range(2):
                for i in range(4):
                    dma(SP, stA_t, stA[32 * g + 8 * i:32 * g + 8 * i + 8, 0:SEGW],
                        sfc_d[i, p * SQP * 2:(p + 1) * SQP * 2, g * SEGW:(g + 1) * SEGW], writes=(stA_b,))
            pt = ps1p.alloc()
            for c in range(8):
                tr(pt[0][:, c * 8:(c + 1) * 8], stB[0:8, c * 128:(c + 1) * 128], ident[0:8, 0:8], (stB_b, CM_b), (pt[1],), inc=(c == 7))
            cp(DVE, h0s[:, :, :, :].rearrange("p j c s -> p c j s"),
               pt[0][:, 0:64].rearrange("p (c j s) -> p c j s", j=2, s=SQP), (pt[1],), (h0s_b,))
            ps1p.free(pt)
            pt = ps1p.alloc()
            for c in range(8):
                tr(pt[0][:, c * 24:(c + 1) * 24], stB[32:56, c * 128:(c + 1) * 128], ident[32:56, 32:56], (stB_b, CM_b), (pt[1],), inc=(c == 7))
            cp(DVE, cv0s[:, :, :, :].rearrange("p j c r -> p c j r"),
               pt[0][:, 0:192].rearrange("p (c j r) -> p c j r", j=2, r=SQP * 3), (pt[1],), (cv0s_b,))
            ps1p.free(pt)
            for g in range(2):
                pt = ps1p.alloc()
                for cc in range(11):
                    tr(pt[0][:, cc * 32:(cc + 1) * 32], stA[32 * g:32 * g + 32, cc * 128:(cc + 1) * 128],
                       ident[32 * g:32 * g + 32, 32 * g:32 * g + 32], (stA_b, CM_b), (pt[1],), inc=(cc == 10))
                cp(DVE, fc0s[:, :, 11 * g:11 * g + 11, :].rearrange("p i c r -> p c i r"),
                   pt[0][:, 0:352].rearrange("p (c i r) -> p c i r", i=4, r=SQP * 2), (pt[1],), (fc0s_b,))
                ps1p.free(pt)

        def store_y(p):
            for tbk in range(4):
                pt = ps2p.alloc()
                ptv = ps2(pt)
                for c in range(8):
                    tr(ptv[:, c * 128:(c + 1) * 128], xf[:, c, tbk * 128:(tbk + 1) * 128], ident, (xf_b[c], CM_b), (pt[1][c // 4],), inc=(c % 4 == 3))
                st, stb, stt_ = (stA, stA_b, stA_t) if tbk % 2 == 0 else (stB, stB_b, stB_t)
                cp(ACT, st[:, 0:D], ptv, pt[1], (stb,))
                ps2p.free(pt)
                dma(SP, stt_, yp_d[p * NP_ + tbk * 128:p * NP_ + (tbk + 1) * 128, :], st[:, 0:D], reads=(stb,))
            pt = ps2p.alloc()
            ptv = ps2(pt)
            for c in range(8):
                tr(ptv[0:NS, c * 128:(c + 1) * 128], xf[:, c, NP_:NCOL], ident, (xf_b[c], CM_b), (pt[1][c // 4],), inc=(c % 4 == 3))
            cp(ACT, stA[0:NS, 0:D], ptv[0:NS, :], pt[1], (stA_b,))
            ps2p.free(pt)
            dma(SP, stA_t, ys_d[p * NS:(p + 1) * NS, :], stA[0:NS, 0:D], reads=(stA_b,))

        marks = []

        def mark():
            marks.append(tuple(len(g.prog) for g in (PE, ACT, DVE, POOL, SP)))

        pass_idx = []
        for p in range(DBG["npass"]):
            pass_idx.append(len(marks))
            mark()
            load_x(p)
            for i in range(DBG["depth"]):
                mark()
                if DBG["mixer"]:
                    if i % 2 == 0:
                        rglru(i, p)
                    else:
                        hgrn(i, p)
                    if DBG["ln"]:
                        layer_norm(i, 0)
                if DBG["ffn"]:
                    ffn(i, p)
                    if DBG["ln"]:
                        layer_norm(i, 1)
            if DBG.get("io_store", True):
                store_y(p)

        for t in (stA_t, stB_t, S0o_t, Spo_t):
            if t.cnt:
                SP.wait(t, t.cnt)

        mark()
        engs = (PE, ACT, DVE, POOL, SP)
        bounds = [tuple(0 for _ in engs)] + marks
        nseg = len(bounds)
        starts = [0] + [pi + 1 for pi in pass_idx[1:]] + [nseg]
        if DBG.get("one_block", True):
            starts = [0, nseg]
        for bi in range(len(starts) - 1):
            with nc.Block() as block:
                regs = (block.tensor, block.scalar, block.vector, block.gpsimd, block.sync)
                for gi, (g, reg) in enumerate(zip(engs, regs)):
                    for si in range(starts[bi], starts[bi + 1]):
                        lo = bounds[si][gi]
                        hi = bounds[si + 1][gi] if si + 1 < nseg else len(g.prog)
                        if hi <= lo:
                            continue

                        def body(e, g=g, lo=lo, hi=hi):
                            for f_ in g.prog[lo:hi]:
                                f_(e)
                        reg(body)
    return nc


_CACHE = {}
DBG = {"npass": NPASS, "depth": DEPTH, "mixer": True, "ln": True, "ffn": True, "cores": NCORE}


def _consts():
    cm = np.zeros((128, CMW), np.float32)
    cm[:, M_ID:M_ID + 128] = np.eye(128, dtype=np.float32)
    s = np.arange(128)[:, None]
    t = np.arange(128)[None, :]
    cm[:, M_A2:M_A2 + 128] = ((s // 64 == t // 64) & (s <= t)).astype(np.float32)
    s = np.arange(32)[:, None]
    t = np.arange(32)[None, :]
    cm[0:32, M_AS:M_AS + 32] = ((s // 8 == t // 8) & (s <= t)).astype(np.float32)
    rst = np.ones(NCOL, np.float32)
    rst[0:NP_:64] = 0.0
    rst[NP_:NCOL:8] = 0.0
    cm[:, M_RST:M_RST + NCOL] = rst[None, :]
    for q in range(SQP):
        cm[q * 8:(q + 1) * 8, M_SM + q] = 1.0
    cm[:, M_ONE:M_ONE + 128] = 1.0 / 1024.0
    cm[:, M_ONE2:M_ONE2 + 128] = 1.0 / 128.0
    return cm


def kernel(x_prompt, x_sample, state_rglru_h, state_rglru_conv, state_hgrn_s, state_ffn_conv,
           ln_g, ln_b, rg_w_in, rg_conv_w, rg_conv_b, rg_gate_w, rg_gate_b, rg_lambda, rg_w_out,
           hg_lower, hg_w_in, hg_norm_g, hg_w_out, ffn_w_in, ffn_conv_w, ffn_conv_b, ffn_w_out):
    f = lambda a: np.ascontiguousarray(np.asarray(a, dtype=np.float32))
    x_prompt, x_sample = f(x_prompt), f(x_sample)
    state_rglru_h, state_rglru_conv = f(state_rglru_h), f(state_rglru_conv)
    state_hgrn_s, state_ffn_conv = f(state_hgrn_s), f(state_ffn_conv)
    cvec = np.zeros((NROWS, 128), np.float32)
    parts = [(O_LNG, ln_g), (O_LNB, ln_b), (O_RCW, rg_conv_w), (O_RCB, rg_conv_b), (O_RGB, rg_gate_b),
             (O_LAM, rg_lambda), (O_HLO, hg_lower), (O_HNG, hg_norm_g), (O_FCW, ffn_conv_w), (O_FCB, ffn_conv_b)]
    for off, a in parts:
        r = f(a).reshape(-1, 128)
        cvec[off:off + r.shape[0]] = r
    cmask = _consts()
    if "nc" not in _CACHE:
        _CACHE["nc"] = build_program()
    nc = _CACHE["nc"]
    shared = dict(cvec=cvec, cmask=cmask, rg_w_in=f(rg_w_in), rg_gate_w=f(rg_gate_w), rg_w_out=f(rg_w_out),
                  hg_w_in=f(hg_w_in), hg_w_out=f(hg_w_out), ffn_w_in=f(ffn_w_in), ffn_w_out=f(ffn_w_out))
    in_maps = []
    for c in range(NCORE):
        sl = slice(16 * c, 16 * c + 16)
        m = dict(shared)
        m["xp"] = x_prompt[c]
        m["xs"] = x_sample[sl].reshape(128, D)
        m["srh"] = np.ascontiguousarray(state_rglru_h[:, sl])
        m["src"] = np.ascontiguousarray(state_rglru_conv[:, sl]).reshape(2, 48, D)
        m["shs"] = np.ascontiguousarray(state_hgrn_s[:, sl])
        m["sfc"] = np.ascontiguousarray(state_ffn_conv[:, sl]).reshape(4, 32, DFF)
        in_maps.append(m)
    ncr = DBG["cores"]
    res = run_bass_kernel_spmd(nc, in_maps[:ncr], core_ids=list(range(ncr)))
    R = list(res.results) + [res.results[0]] * (NCORE - ncr)
    y_prompt = np.stack([R[c]["yp"] for c in range(NCORE)], 0)
    y_sample = np.concatenate([R[c]["ys"].reshape(16, 8, D) for c in range(NCORE)], 0)
    p_h = np.stack([R[c]["prh"] for c in range(NCORE)], 1)
    p_rc = np.stack([R[c]["prc"] for c in range(NCORE)], 1)
    p_s = np.stack([R[c]["phs"] for c in range(NCORE)], 1)
    p_fc = np.stack([R[c]["pfc"] for c in range(NCORE)], 1)
    s_h = np.concatenate([R[c]["orh"] for c in range(NCORE)], 1)
    s_rc = np.concatenate([R[c]["orc"].reshape(2, 16, 3, D) for c in range(NCORE)], 1)
    s_s = np.concatenate([R[c]["ohs"] for c in range(NCORE)], 1)
    s_fc = np.concatenate([R[c]["ofc"].reshape(4, 16, 2, DFF) for c in range(NCORE)], 1)
    return tuple(np.ascontiguousarray(a, dtype=np.float32) for a in
                 (y_prompt, y_sample, p_h, p_rc, p_s, p_fc, s_h, s_rc, s_s, s_fc))
```

```python
import contextlib
import numpy as np
import concourse.bass as bass
import concourse.mybir as mybir
from concourse.bass_utils import run_bass_kernel_spmd

F32 = mybir.dt.float32
BF16 = mybir.dt.bfloat16
AF = mybir.ActivationFunctionType
ALU = mybir.AluOpType

NCORE = 8
D = 1024
SEQ = 2048
DEPTH = 4
DFF = 2816
NJ = 22
NP_ = 512
SQP = 4
NS = SQP * 8
NCOL = NP_ + NS
NPASS = 4
ALPHA = (2.0 * DEPTH) ** 0.25
LN_EPS = 1e-5
RMS_EPS = 1e-6
NSLOT = 4
TW = 560
SEGW = 1408

O_LNG = 0
O_LNB = 64
O_RCW = 128
O_RCB = 192
O_RGB = 208
O_LAM = 240
O_HLO = 256
O_HNG = 272
O_FCW = 274
O_FCB = 538
NROWS = 640
O_C1 = 640
O_C2 = 656
O_LB = 672
O_OML = 688
O_FA = 704
O_FB = 720
O_HNH = 736
O_HC1 = 738
O_HGB = 754
CVW = 786

M_ID = 0
M_A2 = 128
M_AS = 256
M_RST = 288
M_SM = 288 + NCOL
M_ONE = M_SM + 4
M_ONE2 = M_ONE + 128
CMW = M_ONE2 + 128


class Tok:
    def __init__(self, sem):
        self.sem = sem
        self.cnt = 0


class Buf:
    __slots__ = ("w", "rs", "name")

    def __init__(self, name=""):
        self.w = None
        self.rs = []
        self.name = name


class Eng:
    def __init__(self, name, tok, is_pe=False):
        self.name = name
        self.prog = []
        self.tok = tok
        self.seen = {}
        self.is_pe = is_pe

    def wait(self, tok, cnt):
        if cnt > self.seen.get(tok, 0):
            sem = tok.sem
            self.prog.append(lambda e: e.wait_ge(sem, cnt))
            self.seen[tok] = cnt


def _deps(eng, reads, writes, skip_tok=None, skip_readers=True):
    need = {}

    def add(st, skippable=True):
        if st is None:
            return
        tok, c = st
        if skippable and tok is skip_tok:
            return
        if need.get(tok, 0) < c:
            need[tok] = c

    for b in reads:
        add(b.w)
    for b in writes:
        add(b.w)
        for r in b.rs:
            add(r, skip_readers)
    for tok, c in need.items():
        eng.wait(tok, c)


def _stamp(reads, writes, st):
    for b in reads:
        b.rs = [r for r in b.rs if r[0] is not st[0]]
        b.rs.append(st)
    for b in writes:
        b.w = st
        b.rs = []


def op(eng, fn, reads=(), writes=(), inc=True):
    _deps(eng, reads, writes, skip_tok=eng.tok if eng.is_pe else None)
    if inc:
        eng.tok.cnt += 1
        sem = eng.tok.sem
        eng.prog.append(lambda e: fn(e).then_inc(sem, 1))
        st = (eng.tok, eng.tok.cnt)
    else:
        eng.prog.append(fn)
        st = (eng.tok, eng.tok.cnt + 1)
    _stamp(reads, writes, st)


def dma(eng, tok, out, in_, reads=(), writes=()):
    _deps(eng, reads, writes, skip_tok=tok, skip_readers=False)
    tok.cnt += 16
    sem = tok.sem
    eng.prog.append(lambda e: e.dma_start(out=out, in_=in_).then_inc(sem, 16))
    _stamp(reads, writes, (tok, tok.cnt))


def run_interleaved(gens):
    gens = [g for g in gens if g is not None]
    while gens:
        for g in list(gens):
            try:
                next(g)
            except StopIteration:
                gens.remove(g)


class Pool_:
    def __init__(self, items):
        self.free_ = list(items)

    def alloc(self):
        assert self.free_, "pool exhausted"
        return self.free_.pop(0)

    def free(self, it):
        self.free_.append(it)


def build_program():
    nc = bass.Bass("TRN2", target_bir_lowering=False)

    def din(name, shape):
        return nc.dram_tensor(name, list(shape), F32, kind="ExternalInput").ap()

    def dout(name, shape):
        return nc.dram_tensor(name, list(shape), F32, kind="ExternalOutput").ap()

    xp_d = din("xp", [SEQ, D])
    xs_d = din("xs", [16 * 8, D])
    srh_d = din("srh", [2, 16, D])
    src_d = din("src", [2, 16 * 3, D])
    shs_d = din("shs", [2, 16, 8, 128, 128])
    sfc_d = din("sfc", [4, 16 * 2, DFF])
    cvec_d = din("cvec", [NROWS, 128])
    cmask_d = din("cmask", [128, CMW])
    rg_w_in = din("rg_w_in", [2, D, 2 * D])
    rg_gate_w = din("rg_gate_w", [2, 2, 8, 128, 128])
    rg_w_out = din("rg_w_out", [2, D, D])
    hg_w_in = din("hg_w_in", [2, D, 4 * D])
    hg_w_out = din("hg_w_out", [2, D, D])
    ffn_w_in = din("ffn_w_in", [4, D, 2 * DFF])
    ffn_w_out = din("ffn_w_out", [4, DFF, D])

    yp_d = dout("yp", [SEQ, D])
    ys_d = dout("ys", [128, D])
    prh_d = dout("prh", [2, D])
    prc_d = dout("prc", [2, 3, D])
    phs_d = dout("phs", [2, 8, 128, 128])
    pfc_d = dout("pfc", [4, 2, DFF])
    orh_d = dout("orh", [2, 16, D])
    orc_d = dout("orc", [2, 16 * 3, D])
    ohs_d = dout("ohs", [2, 16, 8, 128, 128])
    ofc_d = dout("ofc", [4, 16 * 2, DFF])

    es = contextlib.ExitStack()
    with es:
        def sb(name, shape, dt=F32):
            return es.enter_context(nc.sbuf_tensor(name, list(shape), dt))

        def newtok(name):
            return Tok(es.enter_context(nc.semaphore(name)))

        xf = sb("xf", [128, 8, NCOL])
        xb = sb("xb", [128, 8, NCOL], BF16)
        mixo = sb("mixo", [128, 8, NCOL], BF16)
        hff = sb("hff", [128, NJ, NCOL], BF16)
        NT = 14
        NTB = 12
        tf = [sb(f"tf{i}", [128, TW]) for i in range(NT)]
        tb = [sb(f"tb{i}", [128, 640], BF16) for i in range(NTB)]
        wring = sb("wring", [128, NSLOT, 4096], BF16)
        stA = sb("stA", [128, SEGW])
        stB = sb("stB", [128, D])
        CV = sb("CV", [128, CVW])
        CM = sb("CM", [128, CMW])
        ONB = sb("ONB", [128, 128], BF16)
        ONB2 = sb("ONB2", [128, 128], BF16)
        Sst = [sb(f"Sst{j}", [128, 8, 128]) for j in range(2)]
        Sbf = [sb(f"Sbf{j}", [128, 8, 128], BF16) for j in range(2)]
        S0f = sb("S0f", [128, SQP, 8, 128])
        S0b = sb("S0b", [128, SQP, 8, 128], BF16)
        rg_tail = [sb(f"rgtail{j}", [128, 8, 3]) for j in range(2)]
        rg_hst = [sb(f"rghst{j}", [128, 8]) for j in range(2)]
        ffn_tail = [sb(f"fftail{i}", [128, NJ, 2]) for i in range(4)]
        h0s = sb("h0s", [128, 2, 8, SQP])
        cv0s = sb("cv0s", [128, 2, 8, SQP * 3])
        fc0s = sb("fc0s", [128, 4, NJ, SQP * 2])
        hs_o = sb("hs_o", [128, 2, 8, SQP])
        cv_o = sb("cv_o", [128, 2, 8, SQP * 3])
        fc_o = sb("fc_o", [128, 4, NJ, SQP * 2])
        ps_all = es.enter_context(nc.psum_tensor("ps_all", [128, 8, 512], F32))

        pe_t, act_t, dve_t, pool_t = newtok("pe"), newtok("act"), newtok("dve"), newtok("pool")

        xf_b = [Buf(f"xf{c}") for c in range(8)]
        xb_b = [Buf(f"xb{c}") for c in range(8)]
        mixo_b = [Buf() for _ in range(8)]
        hff_b = [Buf() for _ in range(NJ)]
        CV_b, CM_b = Buf("CV"), Buf("CM")
        Sst_b = [[Buf() for _ in range(8)] for _ in range(2)]
        Sbf_b = [[Buf() for _ in range(8)] for _ in range(2)]
        S0f_b, S0b_b = Buf("S0f"), Buf("S0b")
        rg_tail_b = [Buf() for _ in range(2)]
        rg_hst_b = [Buf() for _ in range(2)]
        ffn_tail_b = [Buf() for _ in range(4)]
        h0s_b, cv0s_b, fc0s_b = Buf(), Buf(), Buf()
        hs_o_b, cv_o_b, fc_o_b = Buf(), Buf(), Buf()
        stA_b, stB_b = Buf("stA"), Buf("stB")
        stA_t, stB_t = newtok("stA"), newtok("stB")
        S0f_t, S0b_t, S0o_t, Spo_t, cst_t = newtok("S0f"), newtok("S0b"), newtok("S0o"), newtok("Spo"), newtok("cst")
        wslot_b = [Buf(f"w{s}") for s in range(NSLOT)]
        wslot_t = [newtok(f"w{s}") for s in range(NSLOT)]

        tfp = Pool_([(tf[i], Buf(f"tf{i}"), Buf(f"tfx{i}")) for i in range(NT)])
        tbp = Pool_([(tb[i], Buf(f"tb{i}")) for i in range(NTB)])
        ps2p = Pool_([(ps_all[:, 2 * k:2 * k + 2, :], (Buf(f"ps{2 * k}"), Buf(f"ps{2 * k + 1}"))) for k in range(4)])

        class _Ps1:
            def alloc(self):
                t = ps2p.alloc()
                return (t[0][:, 0, :], t[1][0], t)

            def free(self, x):
                ps2p.free(x[2])
        ps1p = _Ps1()

        PE = Eng("pe", pe_t, is_pe=True)
        ACT = Eng("act", act_t)
        DVE = Eng("dve", dve_t)
        POOL = Eng("pool", pool_t)
        SP = Eng("sp", Tok(None))

        def ps2(t):
            return t[0].rearrange("p a b -> p (a b)")

        def mm(out, lhsT, rhs, start, stop, reads, writes, inc=False):
            op(PE, lambda e: e.matmul(out, lhsT, rhs, start=start, stop=stop), reads, writes, inc=inc)

        def tr(out, in_, ident, reads, writes, inc=False):
            op(PE, lambda e: e.matmul(out, in_, ident, start=True, stop=True), reads, writes, inc=inc)

        def act(out, in_, func, reads, writes, bias=None, scale=None):
            kw = {}
            if bias is not None:
                kw["bias"] = bias
            if scale is not None:
                kw["scale"] = scale
            op(ACT, lambda e: e.activation(out=out, in_=in_, func=func, **kw), reads, writes)

        def ts(eng, out, in0, s1, s2, op0, op1, reads, writes):
            if op1 is None:
                op(eng, lambda e: e.tensor_scalar(out=out, in0=in0, scalar1=s1, scalar2=None, op0=op0), reads, writes)
            else:
                op(eng, lambda e: e.tensor_scalar(out=out, in0=in0, scalar1=s1, scalar2=s2, op0=op0, op1=op1), reads, writes)

        def tt(eng, out, in0, in1, o, reads, writes):
            op(eng, lambda e: e.tensor_tensor(out=out, in0=in0, in1=in1, op=o), reads, writes)

        def stt(eng, out, in0, s, in1, op0, op1, reads, writes):
            op(eng, lambda e: e.scalar_tensor_tensor(out=out, in0=in0, scalar=s, in1=in1, op0=op0, op1=op1), reads, writes)

        def cp(eng, out, in_, reads, writes):
            if eng is ACT:
                op(eng, lambda e: e.copy(out=out, in_=in_), reads, writes)
            else:
                op(eng, lambda e: e.tensor_copy(out=out, in_=in_), reads, writes)

        def memset(eng, ap, val, writes):
            op(eng, lambda e: e.memset(ap, val), (), writes)

        def scan(out, d0, d1, init, reads, writes):
            op(DVE, lambda e: e.tensor_tensor_scan(out=out, data0=d0, data1=d1, initial=init, op0=ALU.mult, op1=ALU.add),
               reads, writes)

        ident = CM[:, M_ID:M_ID + 128]

        witems = []
        wstate = {"issued": 0, "next": 0, "released": set()}

        def w_prefetch():
            while wstate["issued"] < len(witems):
                k = wstate["issued"]
                if k >= NSLOT and (k - NSLOT) not in wstate["released"]:
                    break
                s = k % NSLOT
                for (dst_fn, src) in witems[k]:
                    dma(POOL, wslot_t[s], dst_fn(wring[:, s, :]), src, reads=(), writes=(wslot_b[s],))
                wstate["issued"] += 1

        def w_next():
            k = wstate["next"]
            wstate["next"] += 1
            w_prefetch()
            assert wstate["issued"] > k, "weight ring deadlock"
            s = k % NSLOT
            return wring[:, s, :], wslot_b[s], k

        def w_release(w):
            wstate["released"].add(w[2])
            w_prefetch()

        def it_cols(W, c0, w):
            return [(lambda sl: sl[:, 0:8 * w].rearrange("p (k n) -> p k n", n=w),
                     W[:, c0:c0 + w].rearrange("(k p) n -> p k n", p=128))]

        def build_witems():
            for p in range(DBG["npass"]):
                for i in range(DBG["depth"]):
                    j = i // 2
                    if not DBG["mixer"]:
                        pass
                    elif i % 2 == 0:
                        W = rg_w_in[j]
                        witems.append(it_cols(W, 0, 512))
                        witems.append(it_cols(W, 512, 512))
                        witems.append(it_cols(W, 1024, 512))
                        witems.append([(lambda sl: sl[:, 0:2048].rearrange("p (a n) -> p a n", n=128),
                                        rg_gate_w[j].rearrange("g h i n -> i (g h) n"))])
                        witems.append(it_cols(W, 1536, 512))
                        witems.append(it_cols(rg_w_out[j], 0, 512))
                        witems.append(it_cols(rg_w_out[j], 512, 512))
                    else:
                        W = hg_w_in[j]
                        for h in range(8):
                            witems.append([(lambda sl, t=t: sl[:, 0:4096].rearrange("p (k t n) -> p k t n", t=4, n=128)[:, :, t, :],
                                            W[:, t * 1024 + h * 128:t * 1024 + (h + 1) * 128].rearrange("(k p) n -> p k n", p=128))
                                           for t in range(4)])
                        witems.append(it_cols(hg_w_out[j], 0, 512))
                        witems.append(it_cols(hg_w_out[j], 512, 512))
                    if not DBG["ffn"]:
                        continue
                    W = ffn_w_in[i]
                    for q in range(6):
                        w = 512 if q < 5 else 256
                        witems.append(it_cols(W, q * 512, w))
                        witems.append(it_cols(W, DFF + q * 512, w))
                    for o in range(8):
                        witems.append([(lambda sl: sl[:, 0:NJ * 128].rearrange("p (k n) -> p k n", n=128),
                                        ffn_w_out[i][:, o * 128:(o + 1) * 128].rearrange("(k p) n -> p k n", p=128))])

        build_witems()

        dma(SP, cst_t, CM[:, :], cmask_d[:, :], writes=(CM_b,))
        for blk in range(5):
            dma(SP, stB_t, stB[:, 0:128], cvec_d[blk * 128:(blk + 1) * 128, :], writes=(stB_b,))
            pt = ps1p.alloc()
            tr(pt[0][:, 0:128], stB[:, 0:128], ident, (stB_b, CM_b), (pt[1],), inc=True)
            cp(DVE, CV[:, blk * 128:(blk + 1) * 128], pt[0][:, 0:128], (pt[1],), (CV_b,))
            ps1p.free(pt)
        act(CV[:, O_C1:O_C1 + 16], CV[:, O_LAM:O_LAM + 16], AF.Sigmoid, (CV_b,), (CV_b,))
        act(CV[:, O_C1:O_C1 + 16], CV[:, O_C1:O_C1 + 16], AF.Ln, (CV_b,), (CV_b,))
        ts(DVE, CV[:, O_C2:O_C2 + 16], CV[:, O_C1:O_C1 + 16], 16.0, None, ALU.mult, None, (CV_b,), (CV_b,))
        ts(DVE, CV[:, O_C1:O_C1 + 16], CV[:, O_C1:O_C1 + 16], 8.0, None, ALU.mult, None, (CV_b,), (CV_b,))
        memset(DVE, CV[:, O_LB:O_LB + 8], 0.0, (CV_b,))
        tt(DVE, CV[:, O_LB + 8:O_LB + 16], CV[:, O_HLO + 8:O_HLO + 16], CV[:, O_HLO:O_HLO + 8], ALU.subtract, (CV_b,), (CV_b,))
        act(CV[:, O_LB + 8:O_LB + 16], CV[:, O_LB + 8:O_LB + 16], AF.Sigmoid, (CV_b,), (CV_b,))
        ts(DVE, CV[:, O_OML:O_OML + 16], CV[:, O_LB:O_LB + 16], -1.0, 1.0, ALU.mult, ALU.add, (CV_b,), (CV_b,))
        ts(DVE, CV[:, O_FA:O_FA + 16], CV[:, O_OML:O_OML + 16], 0.5, None, ALU.mult, None, (CV_b,), (CV_b,))
        tt(DVE, CV[:, O_FB:O_FB + 16], CV[:, O_LB:O_LB + 16], CV[:, O_FA:O_FA + 16], ALU.add, (CV_b,), (CV_b,))
        ts(DVE, CV[:, O_HNH:O_HNH + 2], CV[:, O_HNG:O_HNG + 2], 0.5, None, ALU.mult, None, (CV_b,), (CV_b,))
        ts(DVE, CV[:, O_HC1:O_HC1 + 16], CV[:, O_C1:O_C1 + 16], 0.5, None, ALU.mult, None, (CV_b,), (CV_b,))
        ts(DVE, CV[:, O_HGB:O_HGB + 32], CV[:, O_RGB:O_RGB + 32], 0.5, None, ALU.mult, None, (CV_b,), (CV_b,))
        for j in range(2):
            memset(DVE, Sst[j][:, :, :], 0.0, tuple(Sst_b[j]))
            memset(DVE, Sbf[j][:, :, :], 0.0, tuple(Sbf_b[j]))
            memset(DVE, rg_tail[j][:, :, :], 0.0, (rg_tail_b[j],))
            memset(DVE, rg_hst[j][:, :], 0.0, (rg_hst_b[j],))
        for i in range(4):
            memset(DVE, ffn_tail[i][:, :, :], 0.0, (ffn_tail_b[i],))

        ones_ln = CM[:, M_ONE:M_ONE + 128]
        cp(DVE, ONB[:, :], ones_ln, (CM_b,), (CM_b,))
        cp(DVE, ONB2[:, :], CM[:, M_ONE2:M_ONE2 + 128], (CM_b,), (CM_b,))
        ones_rms = CM[:, M_ONE2:M_ONE2 + 128]

        def cvc(off):
            return CV[:, off:off + 1]

        def rows_in(dram, R, C, dst_fn, dst_bufs):
            for s0 in range(0, C, SEGW):
                sw = min(SEGW, C - s0)
                dma(SP, stA_t, stA[0:R, 0:sw], dram[:, s0:s0 + sw], writes=(stA_b,))
                nch = sw // 128
                g = max(1, min(nch, 512 // R))
                c0 = 0
                while c0 < nch:
                    n = min(g, nch - c0)
                    pt = ps1p.alloc()
                    for k in range(n):
                        tr(pt[0][:, k * R:(k + 1) * R], stA[0:R, (c0 + k) * 128:(c0 + k + 1) * 128], ident[0:R, 0:R],
                           (stA_b, CM_b), (pt[1],), inc=(k == n - 1))
                    cp(DVE, dst_fn(s0 // 128 + c0, n), pt[0][:, 0:n * R].rearrange("p (n r) -> p n r", r=R), (pt[1],), dst_bufs)
                    ps1p.free(pt)
                    c0 += n

        def rows_out(src_fn, src_bufs, R, C, dram):
            for s0 in range(0, C, SEGW):
                sw = min(SEGW, C - s0)
                nch = sw // 128
                c0 = 0
                while c0 < nch:
                    n = min(4, nch - c0)
                    pt = ps1p.alloc()
                    for k in range(n):
                        tr(pt[0][0:R, k * 128:(k + 1) * 128], src_fn(s0 // 128 + c0 + k), ident, tuple(src_bufs) + (CM_b,), (pt[1],),
                           inc=(k == n - 1))
                    cp(ACT, stA[0:R, c0 * 128:(c0 + n) * 128], pt[0][0:R, 0:n * 128], (pt[1],), (stA_b,))
                    ps1p.free(pt)
                    c0 += n
                dma(SP, stA_t, dram[:, s0:s0 + sw], stA[0:R, 0:sw], reads=(stA_b,))

        def proj(ps_tile, lhs_fn, rhs_t, rhs_bufs, nk, wbuf, last_inc=True):
            pv = ps2(ps_tile)
            for k in range(nk):
                mm(pv[:, 0:NP_], lhs_fn(k), rhs_t[:, k, 0:NP_], k == 0, k == nk - 1,
                   (wbuf, rhs_bufs[k]), (ps_tile[1][0],))
            for k in range(nk):
                mm(pv[:, 512:512 + NS], lhs_fn(k), rhs_t[:, k, NP_:NCOL], k == 0, k == nk - 1,
                   (wbuf, rhs_bufs[k]), (ps_tile[1][1],), inc=(last_inc and k == nk - 1))

        def pcols(ps_tile):
            pv = ps2(ps_tile)
            return pv[:, 0:NP_], pv[:, 512:512 + NS]

        ln_state = {}

        def ln_stats_begin():
            pm = ps2p.alloc()
            pq = ps2p.alloc()
            ln_state.update(pm=pm, pq=pq, pend=None)

        def ln_stats_prep(c):
            zb = tbp.alloc()
            zq = tbp.alloc()
            cp(DVE, zb[0][:, 0:NCOL], xf[:, c, :], (xf_b[c],), (zb[1],))
            act(zq[0][:, 0:NCOL], xf[:, c, :], AF.Square, (xf_b[c],), (zq[1],))
            return (c, zb, zq)

        def ln_stats_mm(prep):
            c, zb, zq = prep
            pm, pq = ln_state["pm"], ln_state["pq"]
            for (pt, src) in ((pm, zb), (pq, zq)):
                ptv = ps2(pt)
                mm(ptv[:, 0:NP_], ONB[:, :], src[0][:, 0:NP_], c == 0, c == 7, (CM_b, src[1]), (pt[1][0],))
                mm(ptv[:, 512:512 + NS], ONB[:, :], src[0][:, NP_:NCOL], c == 0, c == 7, (CM_b, src[1]), (pt[1][1],),
                   inc=True)
            tbp.free(zb)
            tbp.free(zq)

        def layer_norm(i, s):
            if "pm" not in ln_state:
                ln_stats_begin()
                for c in range(8):
                    ln_stats_mm(ln_stats_prep(c))
            pm, pq = ln_state.pop("pm"), ln_state.pop("pq")
            ln_state.clear()
            pmv, pqv = ps2(pm), ps2(pq)
            m2, rstd, nmr = tfp.alloc(), tfp.alloc(), tfp.alloc()
            for (lo, hi, plo, hb) in ((0, NP_, 0, 0), (NP_, NCOL, 512, 1)):
                n = hi - lo
                act(m2[0][:, lo:hi], pmv[:, plo:plo + n], AF.Square, (pm[1][hb],), (m2[1],))
                tt(DVE, m2[0][:, lo:hi], pqv[:, plo:plo + n], m2[0][:, lo:hi], ALU.subtract, (pq[1][hb], m2[1]), (m2[1],))
            act(rstd[0][:, 0:NCOL], m2[0][:, 0:NCOL], AF.Ln, (m2[1],), (rstd[1],), bias=LN_EPS)
            act(rstd[0][:, 0:NCOL], rstd[0][:, 0:NCOL], AF.Exp, (rstd[1],), (rstd[1],), scale=-0.5)
            for (lo, hi, plo, hb) in ((0, NP_, 0, 0), (NP_, NCOL, 512, 1)):
                n = hi - lo
                stt(DVE, nmr[0][:, lo:hi], pmv[:, plo:plo + n], -1.0, rstd[0][:, lo:hi], ALU.mult, ALU.mult,
                    (pm[1][hb], rstd[1]), (nmr[1],))
            ps2p.free(pm)
            ps2p.free(pq)
            tfp.free(m2)
            for c in range(8):
                t = tfp.alloc()
                tt(DVE, t[0][:, 0:NCOL], xf[:, c, :], rstd[0][:, 0:NCOL], ALU.mult, (xf_b[c], rstd[1]), (t[1],))
                tt(POOL, t[0][:, 0:NCOL], t[0][:, 0:NCOL], nmr[0][:, 0:NCOL], ALU.add, (t[1], nmr[1]), (t[1],))
                g_ap = cvc(O_LNG + (i * 2 + s) * 8 + c)
                b_ap = cvc(O_LNB + (i * 2 + s) * 8 + c)
                act(xf[:, c, :], t[0][:, 0:NCOL], AF.Identity, (t[1], CV_b), (xf_b[c],), bias=b_ap, scale=g_ap)
                act(xb[:, c, :], t[0][:, 0:NCOL], AF.Identity, (t[1], CV_b), (xb_b[c],), bias=b_ap, scale=g_ap)
                tfp.free(t)
            tfp.free(rstd)
            tfp.free(nmr)

        def out_proj_and_z(src_t, src_bufs, nk, items_fn):
            fuse_ln = DBG["ln"]
            if fuse_ln:
                ln_stats_begin()
            prep_prev = None
            for o in range(8):
                lhs_fn, wbuf, rel = items_fn(o)
                pt = ps2p.alloc()
                proj(pt, lhs_fn, src_t, src_bufs, nk, wbuf)
                if rel is not None:
                    rel()
                if prep_prev is not None:
                    ln_stats_mm(prep_prev)
                    prep_prev = None
                a, b = pcols(pt)
                stt(DVE, xf[:, o, 0:NP_], xf[:, o, 0:NP_], ALPHA, a, ALU.mult, ALU.add, (xf_b[o], pt[1][0]), (xf_b[o],))
                stt(DVE, xf[:, o, NP_:NCOL], xf[:, o, NP_:NCOL], ALPHA, b, ALU.mult, ALU.add, (xf_b[o], pt[1][1]), (xf_b[o],))
                ps2p.free(pt)
                if fuse_ln:
                    prep_prev = ln_stats_prep(o)
            if prep_prev is not None:
                ln_stats_mm(prep_prev)

        def std_out_items(nhalf_getter):
            cache = {}

            def f(o):
                hh = o // 4
                if hh not in cache:
                    cache[hh] = w_next()
                ws, wb_, _k = cache[hh]
                ol = o % 4
                rel = (lambda: w_release(cache[hh])) if ol == 3 else None
                return (lambda k: ws[:, k * 512 + ol * 128:k * 512 + (ol + 1) * 128]), wb_, rel
            return f

        def rglru(i, p):
            j = i // 2
            wg = None
            for c in range(8):
                cl = c % 4
                if cl == 0:
                    wg = w_next()
                pg = ps2p.alloc()
                proj(pg, lambda k: wg[0][:, k * 512 + cl * 128:k * 512 + (cl + 1) * 128], xb, xb_b, 8, wg[1])
                act(mixo[:, c, :], ps2(pg)[:, 0:NCOL], AF.Gelu_apprx_tanh, pg[1], (mixo_b[c],))
                ps2p.free(pg)
                if cl == 3:
                    w_release(wg)
            wst = {}

            def s1(c):
                cl = c % 4
                if cl == 0:
                    wst["wx"] = w_next()
                    if c == 0:
                        wst["gw"] = w_next()
                wx = wst["wx"]
                px = ps2p.alloc()
                proj(px, lambda k: wx[0][:, k * 512 + cl * 128:k * 512 + (cl + 1) * 128], xb, xb_b, 8, wx[1])
                if cl == 3:
                    w_release(wx)
                XB = tfp.alloc()
                xbs = XB[0][:, 515:515 + SQP * 11].rearrange("p (s k) -> p s k", k=11)
                pxa, pxb = pcols(px)
                cp(DVE, XB[0][:, 0:3], rg_tail[j][:, c, :], (rg_tail_b[j],), (XB[2],))
                cp(DVE, xbs[:, :, 0:3], cv0s[:, j, c, :].rearrange("p (s k) -> p s k", k=3), (cv0s_b,), (XB[2],))
                cp(ACT, XB[0][:, 3:515], pxa, (px[1][0],), (XB[1],))
                cp(ACT, xbs[:, :, 3:11], pxb.rearrange("p (s k) -> p s k", k=8), (px[1][1],), (XB[1],))
                ps2p.free(px)
                cp(POOL, rg_tail[j][:, c, :], XB[0][:, 512:515], (XB[1],), (rg_tail_b[j],))
                cp(POOL, cv_o[:, j, c, :].rearrange("p (s k) -> p s k", k=3), xbs[:, :, 8:11], (XB[1],), (cv_o_b,))
                XC = tfp.alloc()
                xcs = XC[0][:, NP_:NCOL].rearrange("p (s k) -> p s k", k=8)
                wcol = lambda tap: cvc(O_RCW + (j * 4 + tap) * 8 + c)
                bcol = cvc(O_RCB + j * 8 + c)
                act(XC[0][:, 0:NP_], XB[0][:, 0:512], AF.Identity, (XB[1], XB[2], CV_b), (XC[1],), bias=bcol, scale=wcol(0))
                ts(POOL, xcs, xbs[:, :, 0:8], wcol(0), bcol, ALU.mult, ALU.add, (XB[1], XB[2], CV_b), (XC[2],))
                for tap in (1, 2, 3):
                    stt(DVE, XC[0][:, 0:NP_], XB[0][:, tap:tap + 512], wcol(tap), XC[0][:, 0:NP_], ALU.mult, ALU.add,
                        (XB[1], XB[2], CV_b, XC[1]), (XC[1],))
                    stt(DVE, xcs, xbs[:, :, tap:tap + 8], wcol(tap), xcs, ALU.mult, ALU.add, (XB[1], XB[2], CV_b, XC[2]), (XC[2],))
                tfp.free(XB)
                XCB = tbp.alloc()
                cp(ACT, XCB[0][:, 0:NCOL], XC[0][:, 0:NCOL], (XC[1], XC[2]), (XCB[1],))
                return dict(c=c, XC=XC, XCB=XCB)

            def s2(cx):
                c, XCB = cx["c"], cx["XCB"]
                gw = wst["gw"]
                gwv = gw[0][:, 0:2048].rearrange("p (a n) -> p a n", n=128)
                pr = ps2p.alloc()
                pi = ps2p.alloc()
                for (pt, gi) in ((pr, 0), (pi, 1)):
                    ptv = ps2(pt)
                    mm(ptv[:, 0:NP_], gwv[:, gi * 8 + c, :], XCB[0][:, 0:NP_], True, True, (gw[1], XCB[1]), (pt[1][0],))
                    mm(ptv[:, 512:512 + NS], gwv[:, gi * 8 + c, :], XCB[0][:, NP_:NCOL], True, True, (gw[1], XCB[1]), (pt[1][1],), inc=True)
                tbp.free(XCB)
                if c == 7:
                    w_release(gw)
                R, IG, A = tfp.alloc(), tfp.alloc(), tfp.alloc()
                act(R[0][:, 0:NCOL], ps2(pr)[:, 0:NCOL], AF.Tanh, tuple(pr[1]) + (CV_b,), (R[1],),
                    bias=cvc(O_HGB + (j * 2 + 0) * 8 + c), scale=0.5)
                act(IG[0][:, 0:NCOL], ps2(pi)[:, 0:NCOL], AF.Tanh, tuple(pi[1]) + (CV_b,), (IG[1],),
                    bias=cvc(O_HGB + (j * 2 + 1) * 8 + c), scale=0.5)
                ps2p.free(pr)
                ps2p.free(pi)
                act(A[0][:, 0:NCOL], R[0][:, 0:NCOL], AF.Exp, (R[1], CV_b), (A[1],),
                    bias=cvc(O_HC1 + j * 8 + c), scale=cvc(O_HC1 + j * 8 + c))
                act(R[0][:, 0:NCOL], R[0][:, 0:NCOL], AF.Exp, (R[1], CV_b), (R[1],),
                    bias=cvc(O_C1 + j * 8 + c), scale=cvc(O_C1 + j * 8 + c))
                act(R[0][:, 0:NCOL], R[0][:, 0:NCOL], AF.Ln, (R[1],), (R[1],), bias=1.0, scale=-1.0)
                act(R[0][:, 0:NCOL], R[0][:, 0:NCOL], AF.Exp, (R[1],), (R[1],), scale=0.5)
                cx.update(R=R, IG=IG, A=A)
                return cx

            def s3(cx):
                c, XC, R, IG, A = cx["c"], cx["XC"], cx["R"], cx["IG"], cx["A"]
                stt(DVE, IG[0][:, 0:NCOL], IG[0][:, 0:NCOL], 1.0, R[0][:, 0:NCOL], ALU.add, ALU.mult, (IG[1], R[1]), (IG[1],))
                stt(DVE, IG[0][:, 0:NCOL], IG[0][:, 0:NCOL], 0.5, XC[0][:, 0:NCOL], ALU.mult, ALU.mult,
                    (IG[1], XC[1], XC[2]), (IG[1],))
                tfp.free(XC)
                As = A[0][:, NP_:NCOL].rearrange("p (s k) -> p s k", k=8)
                Bs = IG[0][:, NP_:NCOL].rearrange("p (s k) -> p s k", k=8)
                tmp = R[0][:, 0:SQP].rearrange("p (s k) -> p s k", k=1)
                tt(DVE, tmp, As[:, :, 0:1], h0s[:, j, c, :].rearrange("p (s k) -> p s k", k=1), ALU.mult, (A[1], h0s_b, R[1]), (R[1],))
                tt(DVE, Bs[:, :, 0:1], Bs[:, :, 0:1], tmp, ALU.add, (IG[1], R[1]), (IG[1],))
                memset(DVE, As[:, :, 0:1], 0.0, (A[1],))
                H = R
                scan(H[0][:, 0:NP_], A[0][:, 0:NP_], IG[0][:, 0:NP_], rg_hst[j][:, c:c + 1], (A[1], IG[1], rg_hst_b[j]), (H[1],))
                scan(H[0][:, NP_:NCOL], A[0][:, NP_:NCOL], IG[0][:, NP_:NCOL], 0.0, (A[1], IG[1]), (H[1],))
                cp(POOL, rg_hst[j][:, c:c + 1], H[0][:, NP_ - 1:NP_], (H[1],), (rg_hst_b[j],))
                Hs = H[0][:, NP_:NCOL].rearrange("p (s k) -> p s k", k=8)
                cp(POOL, hs_o[:, j, c, :].rearrange("p (s k) -> p s k", k=1), Hs[:, :, 7:8], (H[1],), (hs_o_b,))
                tt(DVE, mixo[:, c, :], H[0][:, 0:NCOL], mixo[:, c, :], ALU.mult, (H[1], mixo_b[c]), (mixo_b[c],))
                tfp.free(R)
                tfp.free(IG)
                tfp.free(A)

            st1, st2 = {}, {}
            for t in range(8 + 2):
                if t < 8:
                    st1[t] = s1(t)
                if 0 <= t - 1 < 8:
                    st2[t - 1] = s2(st1.pop(t - 1))
                if 0 <= t - 2 < 8:
                    s3(st2.pop(t - 2))
            out_proj_and_z(mixo, mixo_b, 8, std_out_items(None))
            rows_out(lambda c: hs_o[:, j, c, :], (hs_o_b,), SQP, D, orh_d[j, p * SQP:(p + 1) * SQP, :])
            rows_out(lambda c: cv_o[:, j, c, :], (cv_o_b,), SQP * 3, D, orc_d[j, p * SQP * 3:(p + 1) * SQP * 3, :])
            if p == NPASS - 1:
                rows_out(lambda c: rg_hst[j][:, c:c + 1], (rg_hst_b[j],), 1, D, prh_d[j:j + 1, :])
                rows_out(lambda c: rg_tail[j][:, c, :], (rg_tail_b[j],), 3, D, prc_d[j])

        def ffn(i, p):
            def stage_a(jc, cl, w, wg, wu):
                pg = ps2p.alloc()
                pu = ps2p.alloc()
                proj(pg, lambda k: wg[0][:, k * w + cl * 128:k * w + (cl + 1) * 128], xb, xb_b, 8, wg[1])
                proj(pu, lambda k: wu[0][:, k * w + cl * 128:k * w + (cl + 1) * 128], xb, xb_b, 8, wu[1])
                GB = tfp.alloc()
                gbs = GB[0][:, 514:514 + SQP * 10].rearrange("p (s k) -> p s k", k=10)
                pga, pgb = pcols(pg)
                cp(DVE, GB[0][:, 0:2], ffn_tail[i][:, jc, :], (ffn_tail_b[i],), (GB[2],))
                cp(DVE, gbs[:, :, 0:2], fc0s[:, i, jc, :].rearrange("p (s k) -> p s k", k=2), (fc0s_b,), (GB[2],))
                cp(ACT, GB[0][:, 2:514], pga, (pg[1][0],), (GB[1],))
                cp(ACT, gbs[:, :, 2:10], pgb.rearrange("p (s k) -> p s k", k=8), (pg[1][1],), (GB[1],))
                ps2p.free(pg)
                UP = tfp.alloc()
                cp(DVE, UP[0][:, 0:NCOL], ps2(pu)[:, 0:NCOL], pu[1], (UP[1],))
                ps2p.free(pu)
                cp(POOL, ffn_tail[i][:, jc, :], GB[0][:, 512:514], (GB[1],), (ffn_tail_b[i],))
                cp(POOL, fc_o[:, i, jc, :].rearrange("p (s k) -> p s k", k=2), gbs[:, :, 8:10], (GB[1],), (fc_o_b,))
                AC = tfp.alloc()
                acs = AC[0][:, NP_:NCOL].rearrange("p (s k) -> p s k", k=8)
                bcol = cvc(O_FCB + i * NJ + jc)
                w0 = cvc(O_FCW + (i * 3 + 0) * NJ + jc)
                act(AC[0][:, 0:NP_], GB[0][:, 0:512], AF.Identity, (GB[1], GB[2], CV_b), (AC[1],), bias=bcol, scale=w0)
                ts(POOL, acs, gbs[:, :, 0:8], w0, bcol, ALU.mult, ALU.add, (GB[1], GB[2], CV_b), (AC[2],))
                return dict(jc=jc, GB=GB, gbs=gbs, AC=AC, acs=acs, UP=UP)

            def stage_b(cx):
                jc, GB, gbs, AC, acs, UP = cx["jc"], cx["GB"], cx["gbs"], cx["AC"], cx["acs"], cx["UP"]
                for tap in (1, 2):
                    wt = cvc(O_FCW + (i * 3 + tap) * NJ + jc)
                    stt(DVE, AC[0][:, 0:NP_], GB[0][:, tap:tap + 512], wt, AC[0][:, 0:NP_], ALU.mult, ALU.add,
                        (GB[1], GB[2], CV_b, AC[1]), (AC[1],))
                    stt(DVE, acs, gbs[:, :, tap:tap + 8], wt, acs, ALU.mult, ALU.add, (GB[1], GB[2], CV_b, AC[2]), (AC[2],))
                tfp.free(GB)
                act(AC[0][:, 0:NCOL], AC[0][:, 0:NCOL], AF.Gelu_apprx_tanh, (AC[1], AC[2]), (AC[1], AC[2]))
                tt(POOL, hff[:, jc, :], UP[0][:, 0:NCOL], AC[0][:, 0:NCOL], ALU.mult, (UP[1], AC[1], AC[2]), (hff_b[jc],))
                tfp.free(UP)
                tfp.free(AC)

            pending = None
            for q in range(6):
                w = 512 if q < 5 else 256
                wg = w_next()
                wu = w_next()
                ncl = w // 128
                for cl in range(ncl):
                    cx = stage_a(q * 4 + cl, cl, w, wg, wu)
                    if cl == ncl - 1:
                        w_release(wg)
                        w_release(wu)
                    if pending is not None:
                        stage_b(pending)
                    pending = cx
            stage_b(pending)

            def items(o):
                w_ = w_next()
                return (lambda k: w_[0][:, k * 128:(k + 1) * 128]), w_[1], (lambda: w_release(w_))
            out_proj_and_z(hff, hff_b, NJ, items)
            rows_out(lambda c: fc_o[:, i, c, :], (fc_o_b,), SQP * 2, DFF, ofc_d[i, p * SQP * 2:(p + 1) * SQP * 2, :])
            if p == NPASS - 1:
                rows_out(lambda c: ffn_tail[i][:, c, :], (ffn_tail_b[i],), 2, DFF, pfc_d[i])

        def hgrn(i, p):
            j = i // 2
            for s_ in range(SQP):
                dma(SP, S0f_t, S0f[:, s_, :, :], shs_d[j, p * SQP + s_].rearrange("h k v -> k h v"), writes=(S0f_b,))
            for s_ in range(SQP):
                dma(POOL, S0b_t, S0b[:, s_, :, :], shs_d[j, p * SQP + s_].rearrange("h k v -> k h v"), writes=(S0b_b,))
            rst = CM[:, M_RST:M_RST + NCOL]

            def proj_gen(h, ctx):
                wit = w_next()
                ws, wb_ = wit[0], wit[1]
                wv = ws[:, 0:4096].rearrange("p (k t n) -> p k t n", t=4, n=128)
                ctx["wv"], ctx["wb"], ctx["wit"] = wv, wb_, wit
                pq = ps2p.alloc()
                proj(pq, lambda k: wv[:, k, 0, :], xb, xb_b, 8, wb_)
                TQ = tfp.alloc()
                act(TQ[0][:, 0:NCOL], ps2(pq)[:, 0:NCOL], AF.Tanh, pq[1], (TQ[1],), scale=0.5)
                stt(DVE, TQ[0][:, 0:NCOL], TQ[0][:, 0:NCOL], 1.0, ps2(pq)[:, 0:NCOL], ALU.add, ALU.mult,
                    (TQ[1],) + tuple(pq[1]), (TQ[1],))
                ps2p.free(pq)
                yield
                pf = ps2p.alloc()
                proj(pf, lambda k: wv[:, k, 1, :], xb, xb_b, 8, wb_)
                T1 = tfp.alloc()
                act(T1[0][:, 0:NCOL], ps2(pf)[:, 0:NCOL], AF.Tanh, pf[1], (T1[1],), scale=0.5)
                ps2p.free(pf)
                yield
                pg = ps2p.alloc()
                proj(pg, lambda k: wv[:, k, 3, :], xb, xb_b, 8, wb_)
                SG = tfp.alloc()
                act(SG[0][:, 0:NCOL], ps2(pg)[:, 0:NCOL], AF.Tanh, pg[1], (SG[1],), scale=0.5)
                stt(DVE, SG[0][:, 0:NCOL], SG[0][:, 0:NCOL], 1.0, ps2(pg)[:, 0:NCOL], ALU.add, ALU.mult,
                    (SG[1],) + tuple(pg[1]), (SG[1],))
                ps2p.free(pg)
                ctx["SG"] = SG
                yield
                T2, T3 = tfp.alloc(), tfp.alloc()
                ts(DVE, T1[0][:, 0:NCOL], T1[0][:, 0:NCOL], cvc(O_FA + j * 8 + h), cvc(O_FB + j * 8 + h), ALU.mult, ALU.add,
                   (T1[1], CV_b), (T1[1],))
                yield
                yield
                act(T2[0][:, 0:NCOL], T1[0][:, 0:NCOL], AF.Ln, (T1[1],), (T2[1],), bias=1e-30)
                yield
                ts(DVE, T1[0][:, 0:NCOL], T1[0][:, 0:NCOL], -1.0, 1.0, ALU.mult, ALU.add, (T1[1],), (T1[1],))
                yield
                scan(T3[0][:, 0:NCOL], rst, T2[0][:, 0:NCOL], 0.0, (CM_b, T2[1]), (T3[1],))
                yield
                EC = tfp.alloc()
                act(EC[0][:, 0:NCOL], T3[0][:, 0:NCOL], AF.Exp, (T3[1],), (EC[1],))
                act(T2[0][:, 0:NCOL], T3[0][:, 0:NCOL], AF.Exp, (T3[1],), (T2[1],), scale=-1.0)
                yield
                ctx["EC"] = EC
                tfp.free(T3)
                QT = tbp.alloc()
                stt(DVE, QT[0][:, 0:NCOL], TQ[0][:, 0:NCOL], 0.5 * 128.0 ** -0.5, EC[0][:, 0:NCOL], ALU.mult, ALU.mult,
                    (TQ[1], EC[1]), (QT[1],))
                yield
                ctx["QT"] = QT
                tfp.free(TQ)
                tt(DVE, T2[0][:, 0:NCOL], T1[0][:, 0:NCOL], T2[0][:, 0:NCOL], ALU.mult, (T1[1], T2[1]), (T2[1],))
                yield
                tfp.free(T1)
                KT = tbp.alloc()
                cp(ACT, KT[0][:, 0:NCOL], T2[0][:, 0:NCOL], (T2[1],), (KT[1],))
                yield
                KH = tfp.alloc()
                tt(DVE, KH[0][:, 0:NP_].rearrange("p (c k) -> p c k", k=64), T2[0][:, 0:NP_].rearrange("p (c k) -> p c k", k=64),
                   EC[0][:, 63:NP_:64].unsqueeze(2).broadcast_to([128, 8, 64]), ALU.mult, (T2[1], EC[1]), (KH[1],))
                tt(DVE, KH[0][:, NP_:NCOL].rearrange("p (c k) -> p c k", k=8), T2[0][:, NP_:NCOL].rearrange("p (c k) -> p c k", k=8),
                   EC[0][:, NP_ + 7:NCOL:8].unsqueeze(2).broadcast_to([128, SQP, 8]), ALU.mult, (T2[1], EC[1]), (KH[1],))
                tfp.free(T2)
                yield
                pvt = ps2p.alloc()
                pv = (pvt[0][:, 0, :], pvt[1][0])
                pvs = (pvt[0][:, 1, :], pvt[1][1])
                for tbk in range(4):
                    for k in range(8):
                        mm(pv[0][:, tbk * 128:(tbk + 1) * 128], xb[:, k, tbk * 128:(tbk + 1) * 128], wv[:, k, 2, :], k == 0, k == 7,
                           (xb_b[k], wb_), (pv[1],), inc=(tbk == 3 and k == 7))
                    if tbk % 2 == 1:
                        yield
                for k in range(8):
                    mm(pvs[0][0:NS, 0:128], xb[:, k, NP_:NCOL], wv[:, k, 2, :], k == 0, k == 7, (xb_b[k], wb_), (pvs[1],), inc=(k == 7))
                w_release(ctx["wit"])
                VT = tbp.alloc()
                vtv = VT[0][:, 0:640].rearrange("p (b n) -> p b n", n=128)
                cp(ACT, vtv[:, 0:4, :], pv[0][:, 0:512].rearrange("p (b n) -> p b n", n=128), (pv[1],), (VT[1],))
                cp(ACT, vtv[0:NS, 4, :], pvs[0][0:NS, 0:128], (pvs[1],), (VT[1],))
                yield
                ctx["VT"] = VT
                for sc in range(4):
                    mm(pv[0][:, sc * 128:(sc + 1) * 128], KT[0][:, sc * 128:(sc + 1) * 128], QT[0][:, sc * 128:(sc + 1) * 128], True, True,
                       (KT[1], QT[1]), (pv[1],), inc=(sc == 3))
                mm(pvs[0][0:NS, 0:NS], KT[0][:, NP_:NCOL], QT[0][:, NP_:NCOL], True, True, (KT[1], QT[1]), (pvs[1],), inc=True)
                AB = tbp.alloc()
                tt(DVE, AB[0][:, 0:512].rearrange("p (b n) -> p b n", n=128), pv[0][:, 0:512].rearrange("p (b n) -> p b n", n=128),
                   CM[:, M_A2:M_A2 + 128].unsqueeze(1).broadcast_to([128, 4, 128]), ALU.mult, (pv[1], CM_b), (AB[1],))
                tt(DVE, AB[0][0:NS, 512:512 + NS], pvs[0][0:NS, 0:NS], CM[0:NS, M_AS:M_AS + NS], ALU.mult, (pvs[1], CM_b), (AB[1],))
                yield
                ctx["AB"] = AB
                tbp.free(KT)
                yield
                for sc in range(4):
                    tr(pv[0][:, sc * 128:(sc + 1) * 128], KH[0][:, sc * 128:(sc + 1) * 128], ident, (KH[1], CM_b), (pv[1],), inc=(sc == 3))
                tr(pvs[0][0:NS, 0:128], KH[0][:, NP_:NCOL], ident, (KH[1], CM_b), (pvs[1],), inc=True)
                KK = tbp.alloc()
                cp(ACT, KK[0][:, 0:512], pv[0][:, 0:512], (pv[1],), (KK[1],))
                cp(ACT, KK[0][0:NS, 512:640], pvs[0][0:NS, 0:128], (pvs[1],), (KK[1],))
                yield
                ctx["KK"] = KK
                tfp.free(KH)
                ps2p.free(pvt)
                VB = tbp.alloc()
                tt(DVE, VB[0][0:NS, 0:512].rearrange("p (s n) -> p s n", n=128),
                   vtv[0:NS, 4, :].unsqueeze(1).broadcast_to([NS, SQP, 128]),
                   CM[0:NS, M_SM:M_SM + SQP].unsqueeze(2).broadcast_to([NS, SQP, 128]), ALU.mult, (VT[1], CM_b), (VB[1],))
                ctx["VB"] = VB
                yield

            def chain_gen(h, ctx):
                wv, wb_ = ctx["wv"], ctx["wb"]
                EC, QT, VT, AB, KK, VB = ctx["EC"], ctx["QT"], ctx["VT"], ctx["AB"], ctx["KK"], ctx["VB"]
                vtv = VT[0][:, 0:640].rearrange("p (b n) -> p b n", n=128)
                kkv = KK[0][:, 0:640].rearrange("p (b n) -> p b n", n=128)
                po = ps2p.alloc()
                pov = ps2(po)
                for sc in range(4):
                    mm(pov[:, sc * 128:(sc + 1) * 128], vtv[:, sc, :], AB[0][:, sc * 128:(sc + 1) * 128], sc == 0, False,
                       (VT[1], AB[1]), (po[1][0],))
                    for cc in range(2):
                        cch = sc * 2 + cc
                        lo = cch * 64
                        mm(pov[:, lo:lo + 64], Sbf[j][:, h, :], QT[0][:, lo:lo + 64], False, (cch == 7),
                           (Sbf_b[j][h], QT[1]), (po[1][0],))
                        pd = psp_alloc1()
                        mm(pd[0][:, 0:128], kkv[cc * 64:(cc + 1) * 64, sc, :], vtv[cc * 64:(cc + 1) * 64, sc, :], True, True,
                           (KK[1], VT[1]), (pd[1],), inc=True)
                        stt(DVE, Sst[j][:, h, :], Sst[j][:, h, :], EC[0][:, lo + 63:lo + 64], pd[0][:, 0:128], ALU.mult, ALU.add,
                            (Sst_b[j][h], EC[1], pd[1]), (Sst_b[j][h],))
                        psp_free1(pd)
                        cp(DVE, Sbf[j][:, h, :], Sst[j][:, h, :], (Sst_b[j][h],), (Sbf_b[j][h],))
                        yield
                mm(pov[:, 512:512 + NS], vtv[0:NS, 4, :], AB[0][0:NS, 512:512 + NS], True, False, (VT[1], AB[1]), (po[1][1],))
                for s_ in range(SQP):
                    lo = NP_ + s_ * 8
                    mm(pov[:, 512 + s_ * 8:512 + (s_ + 1) * 8], S0b[:, s_, h, :], QT[0][:, lo:lo + 8], False, s_ == SQP - 1,
                       (S0b_b, QT[1]), (po[1][1],), inc=(s_ == SQP - 1))
                pd = psp_alloc1()
                mm(pd[0][:, 0:512], kkv[0:NS, 4, :], VB[0][0:NS, 0:512], True, True, (KK[1], VB[1]), (pd[1],), inc=True)
                for s_ in range(SQP):
                    lo = NP_ + s_ * 8
                    stt(DVE, S0f[:, s_, h, :], S0f[:, s_, h, :], EC[0][:, lo + 7:lo + 8], pd[0][:, s_ * 128:(s_ + 1) * 128], ALU.mult, ALU.add,
                        (S0f_b, EC[1], pd[1]), (S0f_b,))
                psp_free1(pd)
                tbp.free(KK)
                tbp.free(VB)
                tbp.free(AB)
                tbp.free(VT)
                tbp.free(QT)
                tfp.free(EC)
                yield
                OS = tfp.alloc()
                OQ = tbp.alloc()
                act(OQ[0][:, 0:NCOL], ps2(po)[:, 0:NCOL], AF.Square, po[1], (OQ[1],))
                pm = ps2p.alloc()
                pmv = ps2(pm)
                mm(pmv[:, 0:NP_], ONB2[:, :], OQ[0][:, 0:NP_], True, True, (CM_b, OQ[1]), (pm[1][0],))
                mm(pmv[:, 512:512 + NS], ONB2[:, :], OQ[0][:, NP_:NCOL], True, True, (CM_b, OQ[1]), (pm[1][1],), inc=True)
                tbp.free(OQ)
                yield
                act(OS[0][:, 0:NCOL], pmv[:, 0:NCOL], AF.Ln, pm[1], (OS[1],), bias=RMS_EPS)
                ps2p.free(pm)
                act(OS[0][:, 0:NCOL], OS[0][:, 0:NCOL], AF.Exp, (OS[1],), (OS[1],), scale=-0.5)
                ON = tfp.alloc()
                stt(DVE, ON[0][:, 0:NCOL], ps2(po)[:, 0:NCOL], cvc(O_HNH + j), OS[0][:, 0:NCOL], ALU.mult, ALU.mult,
                    tuple(po[1]) + (CV_b, OS[1]), (ON[1],))
                ps2p.free(po)
                SG = ctx["SG"]
                tt(DVE, mixo[:, h, :], ON[0][:, 0:NCOL], SG[0][:, 0:NCOL], ALU.mult, (ON[1], SG[1]), (mixo_b[h],))
                tfp.free(OS)
                tfp.free(ON)
                tfp.free(SG)
                yield

            ctxs = [dict() for _ in range(8)]
            run_interleaved([proj_gen(0, ctxs[0])])
            for h in range(8):
                run_interleaved([chain_gen(h, ctxs[h]), proj_gen(h + 1, ctxs[h + 1]) if h + 1 < 8 else None])
            out_proj_and_z(mixo, mixo_b, 8, std_out_items(None))
            for s_ in range(SQP):
                dma(SP, S0o_t, ohs_d[j, p * SQP + s_].rearrange("h k v -> k h v"), S0f[:, s_, :, :], reads=(S0f_b,))
            if p == NPASS - 1:
                dma(SP, Spo_t, phs_d[j].rearrange("h k v -> k h v"), Sst[j][:, :, :], reads=tuple(Sst_b[j]))

        def psp_alloc1():
            return ps1p.alloc()

        def psp_free1(t):
            ps1p.free(t)

        def load_x(p):
            if DBG.get("p0", False):
                p = 0
            for tbk in range(DBG.get("ntbk", 4)):
                st, stb, stt_ = (stA, stA_b, stA_t) if tbk % 2 == 0 else (stB, stB_b, stB_t)
                dma(SP, stt_, st[:, 0:D], xp_d[p * NP_ + tbk * 128:p * NP_ + (tbk + 1) * 128, :], writes=(stb,))
                pt = ps2p.alloc()
                ptv = ps2(pt)
                for c in range(8):
                    tr(ptv[:, c * 128:(c + 1) * 128], st[:, c * 128:(c + 1) * 128], ident, (stb, CM_b), (pt[1][c // 4],), inc=(c % 4 == 3))
                cp(ACT, xf[:, :, tbk * 128:(tbk + 1) * 128], ptv.rearrange("p (c n) -> p c n", n=128), pt[1], tuple(xf_b))
                cp(DVE, xb[:, :, tbk * 128:(tbk + 1) * 128], xf[:, :, tbk * 128:(tbk + 1) * 128], tuple(xf_b), tuple(xb_b))
                ps2p.free(pt)
            if DBG.get("no_xs", False):
                return
            dma(SP, stA_t, stA[0:NS, 0:D], xs_d[p * NS:(p + 1) * NS, :], writes=(stA_b,))
            pt = ps1p.alloc()
            for c in range(8):
                tr(pt[0][:, c * NS:(c + 1) * NS], stA[0:NS, c * 128:(c + 1) * 128], ident[0:NS, 0:NS], (stA_b, CM_b), (pt[1],), inc=(c == 7))
            cp(ACT, xf[:, :, NP_:NCOL], pt[0][:, 0:8 * NS].rearrange("p (c n) -> p c n", n=NS), (pt[1],), tuple(xf_b))
            cp(DVE, xb[:, :, NP_:NCOL], xf[:, :, NP_:NCOL], tuple(xf_b), tuple(xb_b))
            ps1p.free(pt)
            if not DBG.get("io_rows", True):
                return
            for j in range(2):
                dma(SP, stB_t, stB[4 * j:4 * j + 4, 0:D], srh_d[j, p * SQP:(p + 1) * SQP, :], writes=(stB_b,))
            for j in range(2):
                dma(SP, stB_t, stB[32 + 12 * j:32 + 12 * j + 12, 0:D], src_d[j, p * SQP * 3:(p + 1) * SQP * 3, :], writes=(stB_b,))
            for g in range(2):
                for i in range(4):
                    dma(SP, stA_t, stA[32 * g + 8 * i:32 * g + 8 * i + 8, 0:SEGW],
                        sfc_d[i, p * SQP * 2:(p + 1) * SQP * 2, g * SEGW:(g + 1) * SEGW], writes=(stA_b,))
            pt = ps1p.alloc()
            for c in range(8):
                tr(pt[0][:, c * 8:(c + 1) * 8], stB[0:8, c * 128:(c + 1) * 128], ident[0:8, 0:8], (stB_b, CM_b), (pt[1],), inc=(c == 7))
            cp(DVE, h0s[:, :, :, :].rearrange("p j c s -> p c j s"),
               pt[0][:, 0:64].rearrange("p (c j s) -> p c j s", j=2, s=SQP), (pt[1],), (h0s_b,))
            ps1p.free(pt)
            pt = ps1p.alloc()
            for c in range(8):
                tr(pt[0][:, c * 24:(c + 1) * 24], stB[32:56, c * 128:(c + 1) * 128], ident[32:56, 32:56], (stB_b, CM_b), (pt[1],), inc=(c == 7))
            cp(DVE, cv0s[:, :, :, :].rearrange("p j c r -> p c j r"),
               pt[0][:, 0:192].rearrange("p (c j r) -> p c j r", j=2, r=SQP * 3), (pt[1],), (cv0s_b,))
            ps1p.free(pt)
            for g in range(2):
                pt = ps1p.alloc()
                for cc in range(11):
                    tr(pt[0][:, cc * 32:(cc + 1) * 32], stA[32 * g:32 * g + 32, cc * 128:(cc + 1) * 128],
                       ident[32 * g:32 * g + 32, 32 * g:32 * g + 32], (stA_b, CM_b), (pt[1],), inc=(cc == 10))
                cp(DVE, fc0s[:, :, 11 * g:11 * g + 11, :].rearrange("p i c r -> p c i r"),
                   pt[0][:, 0:352].rearrange("p (c i r) -> p c i r", i=4, r=SQP * 2), (pt[1],), (fc0s_b,))
                ps1p.free(pt)

        def store_y(p):
            for tbk in range(4):
                pt = ps2p.alloc()
                ptv = ps2(pt)
                for c in range(8):
                    tr(ptv[:, c * 128:(c + 1) * 128], xf[:, c, tbk * 128:(tbk + 1) * 128], ident, (xf_b[c], CM_b), (pt[1][c // 4],), inc=(c % 4 == 3))
                st, stb, stt_ = (stA, stA_b, stA_t) if tbk % 2 == 0 else (stB, stB_b, stB_t)
                cp(ACT, st[:, 0:D], ptv, pt[1], (stb,))
                ps2p.free(pt)
                dma(SP, stt_, yp_d[p * NP_ + tbk * 128:p * NP_ + (tbk + 1) * 128, :], st[:, 0:D], reads=(stb,))
            pt = ps2p.alloc()
            ptv = ps2(pt)
            for c in range(8):
                tr(ptv[0:NS, c * 128:(c + 1) * 128], xf[:, c, NP_:NCOL], ident, (xf_b[c], CM_b), (pt[1][c // 4],), inc=(c % 4 == 3))
            cp(ACT, stA[0:NS, 0:D], ptv[0:NS, :], pt[1], (stA_b,))
            ps2p.free(pt)
            dma(SP, stA_t, ys_d[p * NS:(p + 1) * NS, :], stA[0:NS, 0:D], reads=(stA_b,))

        marks = []

        def mark():
            marks.append(tuple(len(g.prog) for g in (PE, ACT, DVE, POOL, SP)))

        pass_idx = []
        for p in range(DBG["npass"]):
            pass_idx.append(len(marks))
            mark()
            load_x(p)
            for i in range(DBG["depth"]):
                mark()
                if DBG["mixer"]:
                    if i % 2 == 0:
                        rglru(i, p)
                    else:
                        hgrn(i, p)
                    if DBG["ln"]:
                        layer_norm(i, 0)
                if DBG["ffn"]:
                    ffn(i, p)
                    if DBG["ln"]:
                        layer_norm(i, 1)
            if DBG.get("io_store", True):
                store_y(p)

        for t in (stA_t, stB_t, S0o_t, Spo_t):
            if t.cnt:
                SP.wait(t, t.cnt)

        mark()
        engs = (PE, ACT, DVE, POOL, SP)
        bounds = [tuple(0 for _ in engs)] + marks
        nseg = len(bounds)
        starts = [0] + [pi + 1 for pi in pass_idx[1:]] + [nseg]
        if DBG.get("one_block", True):
            starts = [0, nseg]
        for bi in range(len(starts) - 1):
            with nc.Block() as block:
                regs = (block.tensor, block.scalar, block.vector, block.gpsimd, block.sync)
                for gi, (g, reg) in enumerate(zip(engs, regs)):
                    for si in range(starts[bi], starts[bi + 1]):
                        lo = bounds[si][gi]
                        hi = bounds[si + 1][gi] if si + 1 < nseg else len(g.prog)
                        if hi <= lo:
                            continue

                        def body(e, g=g, lo=lo, hi=hi):
                            for f_ in g.prog[lo:hi]:
                                f_(e)
                        reg(body)
    return nc


_CACHE = {}
DBG = {"npass": NPASS, "depth": DEPTH, "mixer": True, "ln": True, "ffn": True, "cores": NCORE}


def _consts():
    cm = np.zeros((128, CMW), np.float32)
    cm[:, M_ID:M_ID + 128] = np.eye(128, dtype=np.float32)
    s = np.arange(128)[:, None]
    t = np.arange(128)[None, :]
    cm[:, M_A2:M_A2 + 128] = ((s // 64 == t // 64) & (s <= t)).astype(np.float32)
    s = np.arange(32)[:, None]
    t = np.arange(32)[None, :]
    cm[0:32, M_AS:M_AS + 32] = ((s // 8 == t // 8) & (s <= t)).astype(np.float32)
    rst = np.ones(NCOL, np.float32)
    rst[0:NP_:64] = 0.0
    rst[NP_:NCOL:8] = 0.0
    cm[:, M_RST:M_RST + NCOL] = rst[None, :]
    for q in range(SQP):
        cm[q * 8:(q + 1) * 8, M_SM + q] = 1.0
    cm[:, M_ONE:M_ONE + 128] = 1.0 / 1024.0
    cm[:, M_ONE2:M_ONE2 + 128] = 1.0 / 128.0
    return cm


def kernel(x_prompt, x_sample, state_rglru_h, state_rglru_conv, state_hgrn_s, state_ffn_conv,
           ln_g, ln_b, rg_w_in, rg_conv_w, rg_conv_b, rg_gate_w, rg_gate_b, rg_lambda, rg_w_out,
           hg_lower, hg_w_in, hg_norm_g, hg_w_out, ffn_w_in, ffn_conv_w, ffn_conv_b, ffn_w_out):
    f = lambda a: np.ascontiguousarray(np.asarray(a, dtype=np.float32))
    x_prompt, x_sample = f(x_prompt), f(x_sample)
    state_rglru_h, state_rglru_conv = f(state_rglru_h), f(state_rglru_conv)
    state_hgrn_s, state_ffn_conv = f(state_hgrn_s), f(state_ffn_conv)
    cvec = np.zeros((NROWS, 128), np.float32)
    parts = [(O_LNG, ln_g), (O_LNB, ln_b), (O_RCW, rg_conv_w), (O_RCB, rg_conv_b), (O_RGB, rg_gate_b),
             (O_LAM, rg_lambda), (O_HLO, hg_lower), (O_HNG, hg_norm_g), (O_FCW, ffn_conv_w), (O_FCB, ffn_conv_b)]
    for off, a in parts:
        r = f(a).reshape(-1, 128)
        cvec[off:off + r.shape[0]] = r
    cmask = _consts()
    if "nc" not in _CACHE:
        _CACHE["nc"] = build_program()
    nc = _CACHE["nc"]
    shared = dict(cvec=cvec, cmask=cmask, rg_w_in=f(rg_w_in), rg_gate_w=f(rg_gate_w), rg_w_out=f(rg_w_out),
                  hg_w_in=f(hg_w_in), hg_w_out=f(hg_w_out), ffn_w_in=f(ffn_w_in), ffn_w_out=f(ffn_w_out))
    in_maps = []
    for c in range(NCORE):
        sl = slice(16 * c, 16 * c + 16)
        m = dict(shared)
        m["xp"] = x_prompt[c]
        m["xs"] = x_sample[sl].reshape(128, D)
        m["srh"] = np.ascontiguousarray(state_rglru_h[:, sl])
        m["src"] = np.ascontiguousarray(state_rglru_conv[:, sl]).reshape(2, 48, D)
        m["shs"] = np.ascontiguousarray(state_hgrn_s[:, sl])
        m["sfc"] = np.ascontiguousarray(state_ffn_conv[:, sl]).reshape(4, 32, DFF)
        in_maps.append(m)
    ncr = DBG["cores"]
    res = run_bass_kernel_spmd(nc, in_maps[:ncr], core_ids=list(range(ncr)))
    R = list(res.results) + [res.results[0]] * (NCORE - ncr)
    y_prompt = np.stack([R[c]["yp"] for c in range(NCORE)], 0)
    y_sample = np.concatenate([R[c]["ys"].reshape(16, 8, D) for c in range(NCORE)], 0)
    p_h = np.stack([R[c]["prh"] for c in range(NCORE)], 1)
    p_rc = np.stack([R[c]["prc"] for c in range(NCORE)], 1)
    p_s = np.stack([R[c]["phs"] for c in range(NCORE)], 1)
    p_fc = np.stack([R[c]["pfc"] for c in range(NCORE)], 1)
    s_h = np.concatenate([R[c]["orh"] for c in range(NCORE)], 1)
    s_rc = np.concatenate([R[c]["orc"].reshape(2, 16, 3, D) for c in range(NCORE)], 1)
    s_s = np.concatenate([R[c]["ohs"] for c in range(NCORE)], 1)
    s_fc = np.concatenate([R[c]["ofc"].reshape(4, 16, 2, DFF) for c in range(NCORE)], 1)
    return tuple(np.ascontiguousarray(a, dtype=np.float32) for a in
                 (y_prompt, y_sample, p_h, p_rc, p_s, p_fc, s_h, s_rc, s_s, s_fc))
```

```python
import contextlib
import numpy as np
import concourse.bass as bass
import concourse.mybir as mybir
from concourse.bass_utils import run_bass_kernel_spmd

F32 = mybir.dt.float32
BF16 = mybir.dt.bfloat16
AF = mybir.ActivationFunctionType
ALU = mybir.AluOpType

NCORE = 8
D = 1024
SEQ = 2048
DEPTH = 4
DFF = 2816
NJ = 22
NP_ = 512
SQP = 4
NS = SQP * 8
NCOL = NP_ + NS
NPASS = 4
ALPHA = (2.0 * DEPTH) ** 0.25
LN_EPS = 1e-5
RMS_EPS = 1e-6
NSLOT = 4
TW = 560
SEGW = 1408

O_LNG = 0
O_LNB = 64
O_RCW = 128
O_RCB = 192
O_RGB = 208
O_LAM = 240
O_HLO = 256
O_HNG = 272
O_FCW = 274
O_FCB = 538
NROWS = 640
O_C1 = 640
O_C2 = 656
O_LB = 672
O_OML = 688
O_FA = 704
O_FB = 720
O_HNH = 736
O_HC1 = 738
O_HGB = 754
CVW = 786

M_ID = 0
M_A2 = 128
M_AS = 256
M_RST = 288
M_SM = 288 + NCOL
M_ONE = M_SM + 4
M_ONE2 = M_ONE + 128
CMW = M_ONE2 + 128


class Tok:
    def __init__(self, sem):
        self.sem = sem
        self.cnt = 0


class Buf:
    __slots__ = ("w", "rs", "name")

    def __init__(self, name=""):
        self.w = None
        self.rs = []
        self.name = name


class Eng:
    def __init__(self, name, tok, is_pe=False):
        self.name = name
        self.prog = []
        self.tok = tok
        self.seen = {}
        self.is_pe = is_pe

    def wait(self, tok, cnt):
        if cnt > self.seen.get(tok, 0):
            sem = tok.sem
            self.prog.append(lambda e: e.wait_ge(sem, cnt))
            self.seen[tok] = cnt


def _deps(eng, reads, writes, skip_tok=None, skip_readers=True):
    need = {}

    def add(st, skippable=True):
        if st is None:
            return
        tok, c = st
        if skippable and tok is skip_tok:
            return
        if need.get(tok, 0) < c:
            need[tok] = c

    for b in reads:
        add(b.w)
    for b in writes:
        add(b.w)
        for r in b.rs:
            add(r, skip_readers)
    for tok, c in need.items():
        eng.wait(tok, c)


def _stamp(reads, writes, st):
    for b in reads:
        b.rs = [r for r in b.rs if r[0] is not st[0]]
        b.rs.append(st)
    for b in writes:
        b.w = st
        b.rs = []


def op(eng, fn, reads=(), writes=(), inc=True):
    _deps(eng, reads, writes, skip_tok=eng.tok if eng.is_pe else None)
    if inc:
        eng.tok.cnt += 1
        sem = eng.tok.sem
        eng.prog.append(lambda e: fn(e).then_inc(sem, 1))
        st = (eng.tok, eng.tok.cnt)
    else:
        eng.prog.append(fn)
        st = (eng.tok, eng.tok.cnt + 1)
    _stamp(reads, writes, st)


def dma(eng, tok, out, in_, reads=(), writes=()):
    _deps(eng, reads, writes, skip_tok=tok, skip_readers=False)
    tok.cnt += 16
    sem = tok.sem
    eng.prog.append(lambda e: e.dma_start(out=out, in_=in_).then_inc(sem, 16))
    _stamp(reads, writes, (tok, tok.cnt))


class Pool_:
    def __init__(self, items):
        self.free_ = list(items)

    def alloc(self):
        assert self.free_, "pool exhausted"
        return self.free_.pop(0)

    def free(self, it):
        self.free_.append(it)


def build_program():
    nc = bass.Bass("TRN2", target_bir_lowering=False)

    def din(name, shape):
        return nc.dram_tensor(name, list(shape), F32, kind="ExternalInput").ap()

    def dout(name, shape):
        return nc.dram_tensor(name, list(shape), F32, kind="ExternalOutput").ap()

    xp_d = din("xp", [SEQ, D])
    xs_d = din("xs", [16 * 8, D])
    srh_d = din("srh", [2, 16, D])
    src_d = din("src", [2, 16 * 3, D])
    shs_d = din("shs", [2, 16, 8, 128, 128])
    sfc_d = din("sfc", [4, 16 * 2, DFF])
    cvec_d = din("cvec", [NROWS, 128])
    cmask_d = din("cmask", [128, CMW])
    rg_w_in = din("rg_w_in", [2, D, 2 * D])
    rg_gate_w = din("rg_gate_w", [2, 2, 8, 128, 128])
    rg_w_out = din("rg_w_out", [2, D, D])
    hg_w_in = din("hg_w_in", [2, D, 4 * D])
    hg_w_out = din("hg_w_out", [2, D, D])
    ffn_w_in = din("ffn_w_in", [4, D, 2 * DFF])
    ffn_w_out = din("ffn_w_out", [4, DFF, D])

    yp_d = dout("yp", [SEQ, D])
    ys_d = dout("ys", [128, D])
    prh_d = dout("prh", [2, D])
    prc_d = dout("prc", [2, 3, D])
    phs_d = dout("phs", [2, 8, 128, 128])
    pfc_d = dout("pfc", [4, 2, DFF])
    orh_d = dout("orh", [2, 16, D])
    orc_d = dout("orc", [2, 16 * 3, D])
    ohs_d = dout("ohs", [2, 16, 8, 128, 128])
    ofc_d = dout("ofc", [4, 16 * 2, DFF])

    es = contextlib.ExitStack()
    with es:
        def sb(name, shape, dt=F32):
            return es.enter_context(nc.sbuf_tensor(name, list(shape), dt))

        def newtok(name):
            return Tok(es.enter_context(nc.semaphore(name)))

        xf = sb("xf", [128, 8, NCOL])
        xb = sb("xb", [128, 8, NCOL], BF16)
        mixo = sb("mixo", [128, 8, NCOL], BF16)
        hff = sb("hff", [128, NJ, NCOL], BF16)
        NT = 14
        NTB = 12
        tf = [sb(f"tf{i}", [128, TW]) for i in range(NT)]
        tb = [sb(f"tb{i}", [128, 640], BF16) for i in range(NTB)]
        wring = sb("wring", [128, NSLOT, 4096], BF16)
        stA = sb("stA", [128, SEGW])
        stB = sb("stB", [128, D])
        CV = sb("CV", [128, CVW])
        CM = sb("CM", [128, CMW])
        ONB = sb("ONB", [128, 128], BF16)
        ONB2 = sb("ONB2", [128, 128], BF16)
        Sst = [sb(f"Sst{j}", [128, 8, 128]) for j in range(2)]
        Sbf = [sb(f"Sbf{j}", [128, 8, 128], BF16) for j in range(2)]
        S0f = sb("S0f", [128, SQP, 8, 128])
        S0b = sb("S0b", [128, SQP, 8, 128], BF16)
        rg_tail = [sb(f"rgtail{j}", [128, 8, 3]) for j in range(2)]
        rg_hst = [sb(f"rghst{j}", [128, 8]) for j in range(2)]
        ffn_tail = [sb(f"fftail{i}", [128, NJ, 2]) for i in range(4)]
        h0s = sb("h0s", [128, 2, 8, SQP])
        cv0s = sb("cv0s", [128, 2, 8, SQP * 3])
        fc0s = sb("fc0s", [128, 4, NJ, SQP * 2])
        hs_o = sb("hs_o", [128, 2, 8, SQP])
        cv_o = sb("cv_o", [128, 2, 8, SQP * 3])
        fc_o = sb("fc_o", [128, 4, NJ, SQP * 2])
        ps_all = es.enter_context(nc.psum_tensor("ps_all", [128, 8, 512], F32))

        pe_t, act_t, dve_t, pool_t = newtok("pe"), newtok("act"), newtok("dve"), newtok("pool")

        xf_b = [Buf(f"xf{c}") for c in range(8)]
        xb_b = [Buf(f"xb{c}") for c in range(8)]
        mixo_b = [Buf() for _ in range(8)]
        hff_b = [Buf() for _ in range(NJ)]
        CV_b, CM_b = Buf("CV"), Buf("CM")
        Sst_b = [[Buf() for _ in range(8)] for _ in range(2)]
        Sbf_b = [[Buf() for _ in range(8)] for _ in range(2)]
        S0f_b, S0b_b = Buf("S0f"), Buf("S0b")
        rg_tail_b = [Buf() for _ in range(2)]
        rg_hst_b = [Buf() for _ in range(2)]
        ffn_tail_b = [Buf() for _ in range(4)]
        h0s_b, cv0s_b, fc0s_b = Buf(), Buf(), Buf()
        hs_o_b, cv_o_b, fc_o_b = Buf(), Buf(), Buf()
        stA_b, stB_b = Buf("stA"), Buf("stB")
        stA_t, stB_t = newtok("stA"), newtok("stB")
        S0f_t, S0b_t, S0o_t, Spo_t, cst_t = newtok("S0f"), newtok("S0b"), newtok("S0o"), newtok("Spo"), newtok("cst")
        wslot_b = [Buf(f"w{s}") for s in range(NSLOT)]
        wslot_t = [newtok(f"w{s}") for s in range(NSLOT)]

        tfp = Pool_([(tf[i], Buf(f"tf{i}"), Buf(f"tfx{i}")) for i in range(NT)])
        tbp = Pool_([(tb[i], Buf(f"tb{i}")) for i in range(NTB)])
        ps2p = Pool_([(ps_all[:, 2 * k:2 * k + 2, :], (Buf(f"ps{2 * k}"), Buf(f"ps{2 * k + 1}"))) for k in range(4)])

        class _Ps1:
            def alloc(self):
                t = ps2p.alloc()
                return (t[0][:, 0, :], t[1][0], t)

            def free(self, x):
                ps2p.free(x[2])
        ps1p = _Ps1()

        PE = Eng("pe", pe_t, is_pe=True)
        ACT = Eng("act", act_t)
        DVE = Eng("dve", dve_t)
        POOL = Eng("pool", pool_t)
        SP = Eng("sp", Tok(None))

        def ps2(t):
            return t[0].rearrange("p a b -> p (a b)")

        def mm(out, lhsT, rhs, start, stop, reads, writes, inc=False):
            op(PE, lambda e: e.matmul(out, lhsT, rhs, start=start, stop=stop), reads, writes, inc=inc)

        def tr(out, in_, ident, reads, writes, inc=False):
            op(PE, lambda e: e.matmul(out, in_, ident, start=True, stop=True), reads, writes, inc=inc)

        def act(out, in_, func, reads, writes, bias=None, scale=None):
            kw = {}
            if bias is not None:
                kw["bias"] = bias
            if scale is not None:
                kw["scale"] = scale
            op(ACT, lambda e: e.activation(out=out, in_=in_, func=func, **kw), reads, writes)

        def ts(eng, out, in0, s1, s2, op0, op1, reads, writes):
            if op1 is None:
                op(eng, lambda e: e.tensor_scalar(out=out, in0=in0, scalar1=s1, scalar2=None, op0=op0), reads, writes)
            else:
                op(eng, lambda e: e.tensor_scalar(out=out, in0=in0, scalar1=s1, scalar2=s2, op0=op0, op1=op1), reads, writes)

        def tt(eng, out, in0, in1, o, reads, writes):
            op(eng, lambda e: e.tensor_tensor(out=out, in0=in0, in1=in1, op=o), reads, writes)

        def stt(eng, out, in0, s, in1, op0, op1, reads, writes):
            op(eng, lambda e: e.scalar_tensor_tensor(out=out, in0=in0, scalar=s, in1=in1, op0=op0, op1=op1), reads, writes)

        def cp(eng, out, in_, reads, writes):
            if eng is ACT:
                op(eng, lambda e: e.copy(out=out, in_=in_), reads, writes)
            else:
                op(eng, lambda e: e.tensor_copy(out=out, in_=in_), reads, writes)

        def memset(eng, ap, val, writes):
            op(eng, lambda e: e.memset(ap, val), (), writes)

        def scan(out, d0, d1, init, reads, writes):
            op(DVE, lambda e: e.tensor_tensor_scan(out=out, data0=d0, data1=d1, initial=init, op0=ALU.mult, op1=ALU.add),
               reads, writes)

        ident = CM[:, M_ID:M_ID + 128]

        witems = []
        wstate = {"issued": 0, "next": 0, "released": set()}

        def w_prefetch():
            while wstate["issued"] < len(witems):
                k = wstate["issued"]
                if k >= NSLOT and (k - NSLOT) not in wstate["released"]:
                    break
                s = k % NSLOT
                for (dst_fn, src) in witems[k]:
                    dma(POOL, wslot_t[s], dst_fn(wring[:, s, :]), src, reads=(), writes=(wslot_b[s],))
                wstate["issued"] += 1

        def w_next():
            k = wstate["next"]
            wstate["next"] += 1
            w_prefetch()
            assert wstate["issued"] > k, "weight ring deadlock"
            s = k % NSLOT
            return wring[:, s, :], wslot_b[s], k

        def w_release(w):
            wstate["released"].add(w[2])
            w_prefetch()

        def it_cols(W, c0, w):
            return [(lambda sl: sl[:, 0:8 * w].rearrange("p (k n) -> p k n", n=w),
                     W[:, c0:c0 + w].rearrange("(k p) n -> p k n", p=128))]

        def build_witems():
            for p in range(DBG["npass"]):
                for i in range(DBG["depth"]):
                    j = i // 2
                    if not DBG["mixer"]:
                        pass
                    elif i % 2 == 0:
                        W = rg_w_in[j]
                        witems.append(it_cols(W, 0, 512))
                        witems.append(it_cols(W, 512, 512))
                        witems.append(it_cols(W, 1024, 512))
                        witems.append([(lambda sl: sl[:, 0:2048].rearrange("p (a n) -> p a n", n=128),
                                        rg_gate_w[j].rearrange("g h i n -> i (g h) n"))])
                        witems.append(it_cols(W, 1536, 512))
                        witems.append(it_cols(rg_w_out[j], 0, 512))
                        witems.append(it_cols(rg_w_out[j], 512, 512))
                    else:
                        W = hg_w_in[j]
                        for h in range(8):
                            witems.append([(lambda sl, t=t: sl[:, 0:4096].rearrange("p (k t n) -> p k t n", t=4, n=128)[:, :, t, :],
                                            W[:, t * 1024 + h * 128:t * 1024 + (h + 1) * 128].rearrange("(k p) n -> p k n", p=128))
                                           for t in range(4)])
                        witems.append(it_cols(hg_w_out[j], 0, 512))
                        witems.append(it_cols(hg_w_out[j], 512, 512))
                    if not DBG["ffn"]:
                        continue
                    W = ffn_w_in[i]
                    for q in range(6):
                        w = 512 if q < 5 else 256
                        witems.append(it_cols(W, q * 512, w))
                        witems.append(it_cols(W, DFF + q * 512, w))
                    for o in range(8):
                        witems.append([(lambda sl: sl[:, 0:NJ * 128].rearrange("p (k n) -> p k n", n=128),
                                        ffn_w_out[i][:, o * 128:(o + 1) * 128].rearrange("(k p) n -> p k n", p=128))])

        build_witems()

        dma(SP, cst_t, CM[:, :], cmask_d[:, :], writes=(CM_b,))
        for blk in range(5):
            dma(SP, stB_t, stB[:, 0:128], cvec_d[blk * 128:(blk + 1) * 128, :], writes=(stB_b,))
            pt = ps1p.alloc()
            tr(pt[0][:, 0:128], stB[:, 0:128], ident, (stB_b, CM_b), (pt[1],), inc=True)
            cp(DVE, CV[:, blk * 128:(blk + 1) * 128], pt[0][:, 0:128], (pt[1],), (CV_b,))
            ps1p.free(pt)
        act(CV[:, O_C1:O_C1 + 16], CV[:, O_LAM:O_LAM + 16], AF.Sigmoid, (CV_b,), (CV_b,))
        act(CV[:, O_C1:O_C1 + 16], CV[:, O_C1:O_C1 + 16], AF.Ln, (CV_b,), (CV_b,))
        ts(DVE, CV[:, O_C2:O_C2 + 16], CV[:, O_C1:O_C1 + 16], 16.0, None, ALU.mult, None, (CV_b,), (CV_b,))
        ts(DVE, CV[:, O_C1:O_C1 + 16], CV[:, O_C1:O_C1 + 16], 8.0, None, ALU.mult, None, (CV_b,), (CV_b,))
        memset(DVE, CV[:, O_LB:O_LB + 8], 0.0, (CV_b,))
        tt(DVE, CV[:, O_LB + 8:O_LB + 16], CV[:, O_HLO + 8:O_HLO + 16], CV[:, O_HLO:O_HLO + 8], ALU.subtract, (CV_b,), (CV_b,))
        act(CV[:, O_LB + 8:O_LB + 16], CV[:, O_LB + 8:O_LB + 16], AF.Sigmoid, (CV_b,), (CV_b,))
        ts(DVE, CV[:, O_OML:O_OML + 16], CV[:, O_LB:O_LB + 16], -1.0, 1.0, ALU.mult, ALU.add, (CV_b,), (CV_b,))
        ts(DVE, CV[:, O_FA:O_FA + 16], CV[:, O_OML:O_OML + 16], 0.5, None, ALU.mult, None, (CV_b,), (CV_b,))
        tt(DVE, CV[:, O_FB:O_FB + 16], CV[:, O_LB:O_LB + 16], CV[:, O_FA:O_FA + 16], ALU.add, (CV_b,), (CV_b,))
        ts(DVE, CV[:, O_HNH:O_HNH + 2], CV[:, O_HNG:O_HNG + 2], 0.5, None, ALU.mult, None, (CV_b,), (CV_b,))
        ts(DVE, CV[:, O_HC1:O_HC1 + 16], CV[:, O_C1:O_C1 + 16], 0.5, None, ALU.mult, None, (CV_b,), (CV_b,))
        ts(DVE, CV[:, O_HGB:O_HGB + 32], CV[:, O_RGB:O_RGB + 32], 0.5, None, ALU.mult, None, (CV_b,), (CV_b,))
        for j in range(2):
            memset(DVE, Sst[j][:, :, :], 0.0, tuple(Sst_b[j]))
            memset(DVE, Sbf[j][:, :, :], 0.0, tuple(Sbf_b[j]))
            memset(DVE, rg_tail[j][:, :, :], 0.0, (rg_tail_b[j],))
            memset(DVE, rg_hst[j][:, :], 0.0, (rg_hst_b[j],))
        for i in range(4):
            memset(DVE, ffn_tail[i][:, :, :], 0.0, (ffn_tail_b[i],))

        ones_ln = CM[:, M_ONE:M_ONE + 128]
        cp(DVE, ONB[:, :], ones_ln, (CM_b,), (CM_b,))
        cp(DVE, ONB2[:, :], CM[:, M_ONE2:M_ONE2 + 128], (CM_b,), (CM_b,))
        ones_rms = CM[:, M_ONE2:M_ONE2 + 128]

        def cvc(off):
            return CV[:, off:off + 1]

        def rows_in(dram, R, C, dst_fn, dst_bufs):
            for s0 in range(0, C, SEGW):
                sw = min(SEGW, C - s0)
                dma(SP, stA_t, stA[0:R, 0:sw], dram[:, s0:s0 + sw], writes=(stA_b,))
                nch = sw // 128
                g = max(1, min(nch, 512 // R))
                c0 = 0
                while c0 < nch:
                    n = min(g, nch - c0)
                    pt = ps1p.alloc()
                    for k in range(n):
                        tr(pt[0][:, k * R:(k + 1) * R], stA[0:R, (c0 + k) * 128:(c0 + k + 1) * 128], ident[0:R, 0:R],
                           (stA_b, CM_b), (pt[1],), inc=(k == n - 1))
                    cp(DVE, dst_fn(s0 // 128 + c0, n), pt[0][:, 0:n * R].rearrange("p (n r) -> p n r", r=R), (pt[1],), dst_bufs)
                    ps1p.free(pt)
                    c0 += n

        def rows_out(src_fn, src_bufs, R, C, dram):
            for s0 in range(0, C, SEGW):
                sw = min(SEGW, C - s0)
                nch = sw // 128
                c0 = 0
                while c0 < nch:
                    n = min(4, nch - c0)
                    pt = ps1p.alloc()
                    for k in range(n):
                        tr(pt[0][0:R, k * 128:(k + 1) * 128], src_fn(s0 // 128 + c0 + k), ident, tuple(src_bufs) + (CM_b,), (pt[1],),
                           inc=(k == n - 1))
                    cp(ACT, stA[0:R, c0 * 128:(c0 + n) * 128], pt[0][0:R, 0:n * 128], (pt[1],), (stA_b,))
                    ps1p.free(pt)
                    c0 += n
                dma(SP, stA_t, dram[:, s0:s0 + sw], stA[0:R, 0:sw], reads=(stA_b,))

        def proj(ps_tile, lhs_fn, rhs_t, rhs_bufs, nk, wbuf, last_inc=True):
            pv = ps2(ps_tile)
            for k in range(nk):
                mm(pv[:, 0:NP_], lhs_fn(k), rhs_t[:, k, 0:NP_], k == 0, k == nk - 1,
                   (wbuf, rhs_bufs[k]), (ps_tile[1][0],))
            for k in range(nk):
                mm(pv[:, 512:512 + NS], lhs_fn(k), rhs_t[:, k, NP_:NCOL], k == 0, k == nk - 1,
                   (wbuf, rhs_bufs[k]), (ps_tile[1][1],), inc=(last_inc and k == nk - 1))

        def pcols(ps_tile):
            pv = ps2(ps_tile)
            return pv[:, 0:NP_], pv[:, 512:512 + NS]

        ln_state = {}

        def ln_stats_begin():
            pm = ps2p.alloc()
            pq = ps2p.alloc()
            ln_state.update(pm=pm, pq=pq, pend=None)

        def ln_stats_prep(c):
            zb = tbp.alloc()
            zq = tbp.alloc()
            cp(DVE, zb[0][:, 0:NCOL], xf[:, c, :], (xf_b[c],), (zb[1],))
            act(zq[0][:, 0:NCOL], xf[:, c, :], AF.Square, (xf_b[c],), (zq[1],))
            return (c, zb, zq)

        def ln_stats_mm(prep):
            c, zb, zq = prep
            pm, pq = ln_state["pm"], ln_state["pq"]
            for (pt, src) in ((pm, zb), (pq, zq)):
                ptv = ps2(pt)
                mm(ptv[:, 0:NP_], ONB[:, :], src[0][:, 0:NP_], c == 0, c == 7, (CM_b, src[1]), (pt[1][0],))
                mm(ptv[:, 512:512 + NS], ONB[:, :], src[0][:, NP_:NCOL], c == 0, c == 7, (CM_b, src[1]), (pt[1][1],),
                   inc=True)
            tbp.free(zb)
            tbp.free(zq)

        def layer_norm(i, s):
            if "pm" not in ln_state:
                ln_stats_begin()
                for c in range(8):
                    ln_stats_mm(ln_stats_prep(c))
            pm, pq = ln_state.pop("pm"), ln_state.pop("pq")
            ln_state.clear()
            pmv, pqv = ps2(pm), ps2(pq)
            m2, rstd, nmr = tfp.alloc(), tfp.alloc(), tfp.alloc()
            for (lo, hi, plo, hb) in ((0, NP_, 0, 0), (NP_, NCOL, 512, 1)):
                n = hi - lo
                act(m2[0][:, lo:hi], pmv[:, plo:plo + n], AF.Square, (pm[1][hb],), (m2[1],))
                tt(DVE, m2[0][:, lo:hi], pqv[:, plo:plo + n], m2[0][:, lo:hi], ALU.subtract, (pq[1][hb], m2[1]), (m2[1],))
            act(rstd[0][:, 0:NCOL], m2[0][:, 0:NCOL], AF.Ln, (m2[1],), (rstd[1],), bias=LN_EPS)
            act(rstd[0][:, 0:NCOL], rstd[0][:, 0:NCOL], AF.Exp, (rstd[1],), (rstd[1],), scale=-0.5)
            for (lo, hi, plo, hb) in ((0, NP_, 0, 0), (NP_, NCOL, 512, 1)):
                n = hi - lo
                stt(DVE, nmr[0][:, lo:hi], pmv[:, plo:plo + n], -1.0, rstd[0][:, lo:hi], ALU.mult, ALU.mult,
                    (pm[1][hb], rstd[1]), (nmr[1],))
            ps2p.free(pm)
            ps2p.free(pq)
            tfp.free(m2)
            for c in range(8):
                t = tfp.alloc()
                tt(DVE, t[0][:, 0:NCOL], xf[:, c, :], rstd[0][:, 0:NCOL], ALU.mult, (xf_b[c], rstd[1]), (t[1],))
                tt(POOL, t[0][:, 0:NCOL], t[0][:, 0:NCOL], nmr[0][:, 0:NCOL], ALU.add, (t[1], nmr[1]), (t[1],))
                g_ap = cvc(O_LNG + (i * 2 + s) * 8 + c)
                b_ap = cvc(O_LNB + (i * 2 + s) * 8 + c)
                act(xf[:, c, :], t[0][:, 0:NCOL], AF.Identity, (t[1], CV_b), (xf_b[c],), bias=b_ap, scale=g_ap)
                act(xb[:, c, :], t[0][:, 0:NCOL], AF.Identity, (t[1], CV_b), (xb_b[c],), bias=b_ap, scale=g_ap)
                tfp.free(t)
            tfp.free(rstd)
            tfp.free(nmr)

        def out_proj_and_z(src_t, src_bufs, nk, items_fn):
            fuse_ln = DBG["ln"]
            if fuse_ln:
                ln_stats_begin()
            prep_prev = None
            for o in range(8):
                lhs_fn, wbuf, rel = items_fn(o)
                pt = ps2p.alloc()
                proj(pt, lhs_fn, src_t, src_bufs, nk, wbuf)
                if rel is not None:
                    rel()
                if prep_prev is not None:
                    ln_stats_mm(prep_prev)
                    prep_prev = None
                a, b = pcols(pt)
                stt(DVE, xf[:, o, 0:NP_], xf[:, o, 0:NP_], ALPHA, a, ALU.mult, ALU.add, (xf_b[o], pt[1][0]), (xf_b[o],))
                stt(DVE, xf[:, o, NP_:NCOL], xf[:, o, NP_:NCOL], ALPHA, b, ALU.mult, ALU.add, (xf_b[o], pt[1][1]), (xf_b[o],))
                ps2p.free(pt)
                if fuse_ln:
                    prep_prev = ln_stats_prep(o)
            if prep_prev is not None:
                ln_stats_mm(prep_prev)

        def std_out_items(nhalf_getter):
            cache = {}

            def f(o):
                hh = o // 4
                if hh not in cache:
                    cache[hh] = w_next()
                ws, wb_, _k = cache[hh]
                ol = o % 4
                rel = (lambda: w_release(cache[hh])) if ol == 3 else None
                return (lambda k: ws[:, k * 512 + ol * 128:k * 512 + (ol + 1) * 128]), wb_, rel
            return f

        def rglru(i, p):
            j = i // 2
            wg = None
            for c in range(8):
                cl = c % 4
                if cl == 0:
                    wg = w_next()
                pg = ps2p.alloc()
                proj(pg, lambda k: wg[0][:, k * 512 + cl * 128:k * 512 + (cl + 1) * 128], xb, xb_b, 8, wg[1])
                act(mixo[:, c, :], ps2(pg)[:, 0:NCOL], AF.Gelu_apprx_tanh, pg[1], (mixo_b[c],))
                ps2p.free(pg)
                if cl == 3:
                    w_release(wg)
            wst = {}

            def s1(c):
                cl = c % 4
                if cl == 0:
                    wst["wx"] = w_next()
                    if c == 0:
                        wst["gw"] = w_next()
                wx = wst["wx"]
                px = ps2p.alloc()
                proj(px, lambda k: wx[0][:, k * 512 + cl * 128:k * 512 + (cl + 1) * 128], xb, xb_b, 8, wx[1])
                if cl == 3:
                    w_release(wx)
                XB = tfp.alloc()
                xbs = XB[0][:, 515:515 + SQP * 11].rearrange("p (s k) -> p s k", k=11)
                pxa, pxb = pcols(px)
                cp(DVE, XB[0][:, 0:3], rg_tail[j][:, c, :], (rg_tail_b[j],), (XB[2],))
                cp(DVE, xbs[:, :, 0:3], cv0s[:, j, c, :].rearrange("p (s k) -> p s k", k=3), (cv0s_b,), (XB[2],))
                cp(ACT, XB[0][:, 3:515], pxa, (px[1][0],), (XB[1],))
                cp(ACT, xbs[:, :, 3:11], pxb.rearrange("p (s k) -> p s k", k=8), (px[1][1],), (XB[1],))
                ps2p.free(px)
                cp(POOL, rg_tail[j][:, c, :], XB[0][:, 512:515], (XB[1],), (rg_tail_b[j],))
                cp(POOL, cv_o[:, j, c, :].rearrange("p (s k) -> p s k", k=3), xbs[:, :, 8:11], (XB[1],), (cv_o_b,))
                XC = tfp.alloc()
                xcs = XC[0][:, NP_:NCOL].rearrange("p (s k) -> p s k", k=8)
                wcol = lambda tap: cvc(O_RCW + (j * 4 + tap) * 8 + c)
                bcol = cvc(O_RCB + j * 8 + c)
                act(XC[0][:, 0:NP_], XB[0][:, 0:512], AF.Identity, (XB[1], XB[2], CV_b), (XC[1],), bias=bcol, scale=wcol(0))
                ts(POOL, xcs, xbs[:, :, 0:8], wcol(0), bcol, ALU.mult, ALU.add, (XB[1], XB[2], CV_b), (XC[2],))
                for tap in (1, 2, 3):
                    stt(DVE, XC[0][:, 0:NP_], XB[0][:, tap:tap + 512], wcol(tap), XC[0][:, 0:NP_], ALU.mult, ALU.add,
                        (XB[1], XB[2], CV_b, XC[1]), (XC[1],))
                    stt(DVE, xcs, xbs[:, :, tap:tap + 8], wcol(tap), xcs, ALU.mult, ALU.add, (XB[1], XB[2], CV_b, XC[2]), (XC[2],))
                tfp.free(XB)
                XCB = tbp.alloc()
                cp(ACT, XCB[0][:, 0:NCOL], XC[0][:, 0:NCOL], (XC[1], XC[2]), (XCB[1],))
                return dict(c=c, XC=XC, XCB=XCB)

            def s2(cx):
                c, XCB = cx["c"], cx["XCB"]
                gw = wst["gw"]
                gwv = gw[0][:, 0:2048].rearrange("p (a n) -> p a n", n=128)
                pr = ps2p.alloc()
                pi = ps2p.alloc()
                for (pt, gi) in ((pr, 0), (pi, 1)):
                    ptv = ps2(pt)
                    mm(ptv[:, 0:NP_], gwv[:, gi * 8 + c, :], XCB[0][:, 0:NP_], True, True, (gw[1], XCB[1]), (pt[1][0],))
                    mm(ptv[:, 512:512 + NS], gwv[:, gi * 8 + c, :], XCB[0][:, NP_:NCOL], True, True, (gw[1], XCB[1]), (pt[1][1],), inc=True)
                tbp.free(XCB)
                if c == 7:
                    w_release(gw)
                R, IG, A = tfp.alloc(), tfp.alloc(), tfp.alloc()
                act(R[0][:, 0:NCOL], ps2(pr)[:, 0:NCOL], AF.Tanh, tuple(pr[1]) + (CV_b,), (R[1],),
                    bias=cvc(O_HGB + (j * 2 + 0) * 8 + c), scale=0.5)
                act(IG[0][:, 0:NCOL], ps2(pi)[:, 0:NCOL], AF.Tanh, tuple(pi[1]) + (CV_b,), (IG[1],),
                    bias=cvc(O_HGB + (j * 2 + 1) * 8 + c), scale=0.5)
                ps2p.free(pr)
                ps2p.free(pi)
                act(A[0][:, 0:NCOL], R[0][:, 0:NCOL], AF.Exp, (R[1], CV_b), (A[1],),
                    bias=cvc(O_HC1 + j * 8 + c), scale=cvc(O_HC1 + j * 8 + c))
                act(R[0][:, 0:NCOL], R[0][:, 0:NCOL], AF.Exp, (R[1], CV_b), (R[1],),
                    bias=cvc(O_C1 + j * 8 + c), scale=cvc(O_C1 + j * 8 + c))
                act(R[0][:, 0:NCOL], R[0][:, 0:NCOL], AF.Ln, (R[1],), (R[1],), bias=1.0, scale=-1.0)
                act(R[0][:, 0:NCOL], R[0][:, 0:NCOL], AF.Exp, (R[1],), (R[1],), scale=0.5)
                cx.update(R=R, IG=IG, A=A)
                return cx

            def s3(cx):
                c, XC, R, IG, A = cx["c"], cx["XC"], cx["R"], cx["IG"], cx["A"]
                stt(DVE, IG[0][:, 0:NCOL], IG[0][:, 0:NCOL], 1.0, R[0][:, 0:NCOL], ALU.add, ALU.mult, (IG[1], R[1]), (IG[1],))
                stt(DVE, IG[0][:, 0:NCOL], IG[0][:, 0:NCOL], 0.5, XC[0][:, 0:NCOL], ALU.mult, ALU.mult,
                    (IG[1], XC[1], XC[2]), (IG[1],))
                tfp.free(XC)
                As = A[0][:, NP_:NCOL].rearrange("p (s k) -> p s k", k=8)
                Bs = IG[0][:, NP_:NCOL].rearrange("p (s k) -> p s k", k=8)
                tmp = R[0][:, 0:SQP].rearrange("p (s k) -> p s k", k=1)
                tt(DVE, tmp, As[:, :, 0:1], h0s[:, j, c, :].rearrange("p (s k) -> p s k", k=1), ALU.mult, (A[1], h0s_b, R[1]), (R[1],))
                tt(DVE, Bs[:, :, 0:1], Bs[:, :, 0:1], tmp, ALU.add, (IG[1], R[1]), (IG[1],))
                memset(DVE, As[:, :, 0:1], 0.0, (A[1],))
                H = R
                scan(H[0][:, 0:NP_], A[0][:, 0:NP_], IG[0][:, 0:NP_], rg_hst[j][:, c:c + 1], (A[1], IG[1], rg_hst_b[j]), (H[1],))
                scan(H[0][:, NP_:NCOL], A[0][:, NP_:NCOL], IG[0][:, NP_:NCOL], 0.0, (A[1], IG[1]), (H[1],))
                cp(POOL, rg_hst[j][:, c:c + 1], H[0][:, NP_ - 1:NP_], (H[1],), (rg_hst_b[j],))
                Hs = H[0][:, NP_:NCOL].rearrange("p (s k) -> p s k", k=8)
                cp(POOL, hs_o[:, j, c, :].rearrange("p (s k) -> p s k", k=1), Hs[:, :, 7:8], (H[1],), (hs_o_b,))
                tt(DVE, mixo[:, c, :], H[0][:, 0:NCOL], mixo[:, c, :], ALU.mult, (H[1], mixo_b[c]), (mixo_b[c],))
                tfp.free(R)
                tfp.free(IG)
                tfp.free(A)

            st1, st2 = {}, {}
            for t in range(8 + 2):
                if t < 8:
                    st1[t] = s1(t)
                if 0 <= t - 1 < 8:
                    st2[t - 1] = s2(st1.pop(t - 1))
                if 0 <= t - 2 < 8:
                    s3(st2.pop(t - 2))
            out_proj_and_z(mixo, mixo_b, 8, std_out_items(None))
            rows_out(lambda c: hs_o[:, j, c, :], (hs_o_b,), SQP, D, orh_d[j, p * SQP:(p + 1) * SQP, :])
            rows_out(lambda c: cv_o[:, j, c, :], (cv_o_b,), SQP * 3, D, orc_d[j, p * SQP * 3:(p + 1) * SQP * 3, :])
            if p == NPASS - 1:
                rows_out(lambda c: rg_hst[j][:, c:c + 1], (rg_hst_b[j],), 1, D, prh_d[j:j + 1, :])
                rows_out(lambda c: rg_tail[j][:, c, :], (rg_tail_b[j],), 3, D, prc_d[j])

        def ffn(i, p):
            def proj_kmajor(specs):
                for k in range(8):
                    for (pt, lhs_fn, wbuf) in specs:
                        mm(ps2(pt)[:, 0:NP_], lhs_fn(k), xb[:, k, 0:NP_], k == 0, k == 7, (wbuf, xb_b[k]), (pt[1][0],))
                for k in range(8):
                    for (pt, lhs_fn, wbuf) in specs:
                        mm(ps2(pt)[:, 512:512 + NS], lhs_fn(k), xb[:, k, NP_:NCOL], k == 0, k == 7, (wbuf, xb_b[k]), (pt[1][1],),
                           inc=(k == 7))

            def stage_a(jc, cl, w, wg, wu, pre=None):
                if pre is not None:
                    pg, pu = pre
                else:
                    pg = ps2p.alloc()
                    pu = ps2p.alloc()
                    proj(pg, lambda k: wg[0][:, k * w + cl * 128:k * w + (cl + 1) * 128], xb, xb_b, 8, wg[1])
                    proj(pu, lambda k: wu[0][:, k * w + cl * 128:k * w + (cl + 1) * 128], xb, xb_b, 8, wu[1])
                GB = tfp.alloc()
                gbs = GB[0][:, 514:514 + SQP * 10].rearrange("p (s k) -> p s k", k=10)
                pga, pgb = pcols(pg)
                cp(DVE, GB[0][:, 0:2], ffn_tail[i][:, jc, :], (ffn_tail_b[i],), (GB[2],))
                cp(DVE, gbs[:, :, 0:2], fc0s[:, i, jc, :].rearrange("p (s k) -> p s k", k=2), (fc0s_b,), (GB[2],))
                cp(ACT, GB[0][:, 2:514], pga, (pg[1][0],), (GB[1],))
                cp(ACT, gbs[:, :, 2:10], pgb.rearrange("p (s k) -> p s k", k=8), (pg[1][1],), (GB[1],))
                ps2p.free(pg)
                UP = tfp.alloc()
                cp(DVE, UP[0][:, 0:NCOL], ps2(pu)[:, 0:NCOL], pu[1], (UP[1],))
                ps2p.free(pu)
                cp(POOL, ffn_tail[i][:, jc, :], GB[0][:, 512:514], (GB[1],), (ffn_tail_b[i],))
                cp(POOL, fc_o[:, i, jc, :].rearrange("p (s k) -> p s k", k=2), gbs[:, :, 8:10], (GB[1],), (fc_o_b,))
                AC = tfp.alloc()
                acs = AC[0][:, NP_:NCOL].rearrange("p (s k) -> p s k", k=8)
                bcol = cvc(O_FCB + i * NJ + jc)
                w0 = cvc(O_FCW + (i * 3 + 0) * NJ + jc)
                act(AC[0][:, 0:NP_], GB[0][:, 0:512], AF.Identity, (GB[1], GB[2], CV_b), (AC[1],), bias=bcol, scale=w0)
                ts(POOL, acs, gbs[:, :, 0:8], w0, bcol, ALU.mult, ALU.add, (GB[1], GB[2], CV_b), (AC[2],))
                return dict(jc=jc, GB=GB, gbs=gbs, AC=AC, acs=acs, UP=UP)

            def stage_b(cx):
                jc, GB, gbs, AC, acs, UP = cx["jc"], cx["GB"], cx["gbs"], cx["AC"], cx["acs"], cx["UP"]
                for tap in (1, 2):
                    wt = cvc(O_FCW + (i * 3 + tap) * NJ + jc)
                    stt(DVE, AC[0][:, 0:NP_], GB[0][:, tap:tap + 512], wt, AC[0][:, 0:NP_], ALU.mult, ALU.add,
                        (GB[1], GB[2], CV_b, AC[1]), (AC[1],))
                    stt(DVE, acs, gbs[:, :, tap:tap + 8], wt, acs, ALU.mult, ALU.add, (GB[1], GB[2], CV_b, AC[2]), (AC[2],))
                tfp.free(GB)
                act(AC[0][:, 0:NCOL], AC[0][:, 0:NCOL], AF.Gelu_apprx_tanh, (AC[1], AC[2]), (AC[1], AC[2]))
                tt(POOL, hff[:, jc, :], UP[0][:, 0:NCOL], AC[0][:, 0:NCOL], ALU.mult, (UP[1], AC[1], AC[2]), (hff_b[jc],))
                tfp.free(UP)
                tfp.free(AC)

            pending = None
            for q in range(6):
                w = 512 if q < 5 else 256
                wg = w_next()
                wu = w_next()
                ncl = w // 128
                pre = {}
                if q == 0:
                    specs = []
                    for cl in (0, 1):
                        pg, pu = ps2p.alloc(), ps2p.alloc()
                        pre[cl] = (pg, pu)
                        specs.append((pg, (lambda k, cl=cl: wg[0][:, k * w + cl * 128:k * w + (cl + 1) * 128]), wg[1]))
                        specs.append((pu, (lambda k, cl=cl: wu[0][:, k * w + cl * 128:k * w + (cl + 1) * 128]), wu[1]))
                    proj_kmajor(specs)
                for cl in range(ncl):
                    cx = stage_a(q * 4 + cl, cl, w, wg, wu, pre=pre.get(cl))
                    if cl == ncl - 1:
                        w_release(wg)
                        w_release(wu)
                    if pending is not None:
                        stage_b(pending)
                    pending = cx
            stage_b(pending)

            def items(o):
                w_ = w_next()
                return (lambda k: w_[0][:, k * 128:(k + 1) * 128]), w_[1], (lambda: w_release(w_))
            out_proj_and_z(hff, hff_b, NJ, items)
            rows_out(lambda c: fc_o[:, i, c, :], (fc_o_b,), SQP * 2, DFF, ofc_d[i, p * SQP * 2:(p + 1) * SQP * 2, :])
            if p == NPASS - 1:
                rows_out(lambda c: ffn_tail[i][:, c, :], (ffn_tail_b[i],), 2, DFF, pfc_d[i])

        def hgrn(i, p):
            j = i // 2
            for s_ in range(SQP):
                dma(SP, S0f_t, S0f[:, s_, :, :], shs_d[j, p * SQP + s_].rearrange("h k v -> k h v"), writes=(S0f_b,))
            for s_ in range(SQP):
                dma(POOL, S0b_t, S0b[:, s_, :, :], shs_d[j, p * SQP + s_].rearrange("h k v -> k h v"), writes=(S0b_b,))
            rst = CM[:, M_RST:M_RST + NCOL]

            def proj_gen(h, ctx):
                wit = w_next()
                ws, wb_ = wit[0], wit[1]
                wv = ws[:, 0:4096].rearrange("p (k t n) -> p k t n", t=4, n=128)
                ctx["wv"], ctx["wb"], ctx["wit"] = wv, wb_, wit
                pq = ps2p.alloc()
                proj(pq, lambda k: wv[:, k, 0, :], xb, xb_b, 8, wb_)
                TQ = tfp.alloc()
                act(TQ[0][:, 0:NCOL], ps2(pq)[:, 0:NCOL], AF.Tanh, pq[1], (TQ[1],), scale=0.5)
                stt(DVE, TQ[0][:, 0:NCOL], TQ[0][:, 0:NCOL], 1.0, ps2(pq)[:, 0:NCOL], ALU.add, ALU.mult,
                    (TQ[1],) + tuple(pq[1]), (TQ[1],))
                ps2p.free(pq)
                yield
                pf = ps2p.alloc()
                proj(pf, lambda k: wv[:, k, 1, :], xb, xb_b, 8, wb_)
                T1 = tfp.alloc()
                act(T1[0][:, 0:NCOL], ps2(pf)[:, 0:NCOL], AF.Tanh, pf[1], (T1[1],), scale=0.5)
                ps2p.free(pf)
                yield
                pg = ps2p.alloc()
                proj(pg, lambda k: wv[:, k, 3, :], xb, xb_b, 8, wb_)
                SG = tfp.alloc()
                act(SG[0][:, 0:NCOL], ps2(pg)[:, 0:NCOL], AF.Tanh, pg[1], (SG[1],), scale=0.5)
                stt(DVE, SG[0][:, 0:NCOL], SG[0][:, 0:NCOL], 1.0, ps2(pg)[:, 0:NCOL], ALU.add, ALU.mult,
                    (SG[1],) + tuple(pg[1]), (SG[1],))
                ps2p.free(pg)
                ctx["SG"] = SG
                yield
                T2, T3 = tfp.alloc(), tfp.alloc()
                ts(DVE, T1[0][:, 0:NCOL], T1[0][:, 0:NCOL], cvc(O_FA + j * 8 + h), cvc(O_FB + j * 8 + h), ALU.mult, ALU.add,
                   (T1[1], CV_b), (T1[1],))
                yield
                ts(DVE, T1[0][:, 0:NCOL], T1[0][:, 0:NCOL], 1e-30, None, ALU.max, None, (T1[1],), (T1[1],))
                yield
                act(T2[0][:, 0:NCOL], T1[0][:, 0:NCOL], AF.Ln, (T1[1],), (T2[1],))
                yield
                ts(DVE, T1[0][:, 0:NCOL], T1[0][:, 0:NCOL], -1.0, 1.0, ALU.mult, ALU.add, (T1[1],), (T1[1],))
                yield
                scan(T3[0][:, 0:NCOL], rst, T2[0][:, 0:NCOL], 0.0, (CM_b, T2[1]), (T3[1],))
                yield
                EC = tfp.alloc()
                act(EC[0][:, 0:NCOL], T3[0][:, 0:NCOL], AF.Exp, (T3[1],), (EC[1],))
                act(T2[0][:, 0:NCOL], T3[0][:, 0:NCOL], AF.Exp, (T3[1],), (T2[1],), scale=-1.0)
                yield
                ctx["EC"] = EC
                tfp.free(T3)
                QT = tbp.alloc()
                stt(DVE, QT[0][:, 0:NCOL], TQ[0][:, 0:NCOL], 0.5 * 128.0 ** -0.5, EC[0][:, 0:NCOL], ALU.mult, ALU.mult,
                    (TQ[1], EC[1]), (QT[1],))
                yield
                ctx["QT"] = QT
                tfp.free(TQ)
                tt(DVE, T2[0][:, 0:NCOL], T1[0][:, 0:NCOL], T2[0][:, 0:NCOL], ALU.mult, (T1[1], T2[1]), (T2[1],))
                yield
                tfp.free(T1)
                KT = tbp.alloc()
                cp(ACT, KT[0][:, 0:NCOL], T2[0][:, 0:NCOL], (T2[1],), (KT[1],))
                yield
                KH = tfp.alloc()
                tt(DVE, KH[0][:, 0:NP_].rearrange("p (c k) -> p c k", k=64), T2[0][:, 0:NP_].rearrange("p (c k) -> p c k", k=64),
                   EC[0][:, 63:NP_:64].unsqueeze(2).broadcast_to([128, 8, 64]), ALU.mult, (T2[1], EC[1]), (KH[1],))
                tt(DVE, KH[0][:, NP_:NCOL].rearrange("p (c k) -> p c k", k=8), T2[0][:, NP_:NCOL].rearrange("p (c k) -> p c k", k=8),
                   EC[0][:, NP_ + 7:NCOL:8].unsqueeze(2).broadcast_to([128, SQP, 8]), ALU.mult, (T2[1], EC[1]), (KH[1],))
                tfp.free(T2)
                yield
                pvt = ps2p.alloc()
                pv = (pvt[0][:, 0, :], pvt[1][0])
                pvs = (pvt[0][:, 1, :], pvt[1][1])
                for tbk in range(4):
                    for k in range(8):
                        mm(pv[0][:, tbk * 128:(tbk + 1) * 128], xb[:, k, tbk * 128:(tbk + 1) * 128], wv[:, k, 2, :], k == 0, k == 7,
                           (xb_b[k], wb_), (pv[1],), inc=(tbk == 3 and k == 7))
                    if tbk % 2 == 1:
                        yield
                for k in range(8):
                    mm(pvs[0][0:NS, 0:128], xb[:, k, NP_:NCOL], wv[:, k, 2, :], k == 0, k == 7, (xb_b[k], wb_), (pvs[1],), inc=(k == 7))
                w_release(ctx["wit"])
                VT = tbp.alloc()
                vtv = VT[0][:, 0:640].rearrange("p (b n) -> p b n", n=128)
                cp(ACT, vtv[:, 0:4, :], pv[0][:, 0:512].rearrange("p (b n) -> p b n", n=128), (pv[1],), (VT[1],))
                cp(ACT, vtv[0:NS, 4, :], pvs[0][0:NS, 0:128], (pvs[1],), (VT[1],))
                yield
                ctx["VT"] = VT
                for sc in range(4):
                    mm(pv[0][:, sc * 128:(sc + 1) * 128], KT[0][:, sc * 128:(sc + 1) * 128], QT[0][:, sc * 128:(sc + 1) * 128], True, True,
                       (KT[1], QT[1]), (pv[1],), inc=(sc == 3))
                mm(pvs[0][0:NS, 0:NS], KT[0][:, NP_:NCOL], QT[0][:, NP_:NCOL], True, True, (KT[1], QT[1]), (pvs[1],), inc=True)
                AB = tbp.alloc()
                tt(DVE, AB[0][:, 0:512].rearrange("p (b n) -> p b n", n=128), pv[0][:, 0:512].rearrange("p (b n) -> p b n", n=128),
                   CM[:, M_A2:M_A2 + 128].unsqueeze(1).broadcast_to([128, 4, 128]), ALU.mult, (pv[1], CM_b), (AB[1],))
                tt(DVE, AB[0][0:NS, 512:512 + NS], pvs[0][0:NS, 0:NS], CM[0:NS, M_AS:M_AS + NS], ALU.mult, (pvs[1], CM_b), (AB[1],))
                yield
                ctx["AB"] = AB
                tbp.free(KT)
                yield
                for sc in range(4):
                    tr(pv[0][:, sc * 128:(sc + 1) * 128], KH[0][:, sc * 128:(sc + 1) * 128], ident, (KH[1], CM_b), (pv[1],), inc=(sc == 3))
                tr(pvs[0][0:NS, 0:128], KH[0][:, NP_:NCOL], ident, (KH[1], CM_b), (pvs[1],), inc=True)
                KK = tbp.alloc()
                cp(ACT, KK[0][:, 0:512], pv[0][:, 0:512], (pv[1],), (KK[1],))
                cp(ACT, KK[0][0:NS, 512:640], pvs[0][0:NS, 0:128], (pvs[1],), (KK[1],))
                yield
                ctx["KK"] = KK
                tfp.free(KH)
                ps2p.free(pvt)
                VB = tbp.alloc()
                tt(DVE, VB[0][0:NS, 0:512].rearrange("p (s n) -> p s n", n=128),
                   vtv[0:NS, 4, :].unsqueeze(1).broadcast_to([NS, SQP, 128]),
                   CM[0:NS, M_SM:M_SM + SQP].unsqueeze(2).broadcast_to([NS, SQP, 128]), ALU.mult, (VT[1], CM_b), (VB[1],))
                ctx["VB"] = VB
                yield

            def chain_gen(h, ctx):
                wv, wb_ = ctx["wv"], ctx["wb"]
                EC, QT, VT, AB, KK, VB = ctx["EC"], ctx["QT"], ctx["VT"], ctx["AB"], ctx["KK"], ctx["VB"]
                vtv = VT[0][:, 0:640].rearrange("p (b n) -> p b n", n=128)
                kkv = KK[0][:, 0:640].rearrange("p (b n) -> p b n", n=128)
                po = ps2p.alloc()
                pov = ps2(po)
                for sc in range(4):
                    mm(pov[:, sc * 128:(sc + 1) * 128], vtv[:, sc, :], AB[0][:, sc * 128:(sc + 1) * 128], sc == 0, False,
                       (VT[1], AB[1]), (po[1][0],))
                    for cc in range(2):
                        cch = sc * 2 + cc
                        lo = cch * 64
                        mm(pov[:, lo:lo + 64], Sbf[j][:, h, :], QT[0][:, lo:lo + 64], False, (cch == 7),
                           (Sbf_b[j][h], QT[1]), (po[1][0],))
                        pd = psp_alloc1()
                        mm(pd[0][:, 0:128], kkv[cc * 64:(cc + 1) * 64, sc, :], vtv[cc * 64:(cc + 1) * 64, sc, :], True, True,
                           (KK[1], VT[1]), (pd[1],), inc=True)
                        stt(DVE, Sst[j][:, h, :], Sst[j][:, h, :], EC[0][:, lo + 63:lo + 64], pd[0][:, 0:128], ALU.mult, ALU.add,
                            (Sst_b[j][h], EC[1], pd[1]), (Sst_b[j][h],))
                        psp_free1(pd)
                        cp(DVE, Sbf[j][:, h, :], Sst[j][:, h, :], (Sst_b[j][h],), (Sbf_b[j][h],))
                        yield
                mm(pov[:, 512:512 + NS], vtv[0:NS, 4, :], AB[0][0:NS, 512:512 + NS], True, False, (VT[1], AB[1]), (po[1][1],))
                for s_ in range(SQP):
                    lo = NP_ + s_ * 8
                    mm(pov[:, 512 + s_ * 8:512 + (s_ + 1) * 8], S0b[:, s_, h, :], QT[0][:, lo:lo + 8], False, s_ == SQP - 1,
                       (S0b_b, QT[1]), (po[1][1],), inc=(s_ == SQP - 1))
                pd = psp_alloc1()
                mm(pd[0][:, 0:512], kkv[0:NS, 4, :], VB[0][0:NS, 0:512], True, True, (KK[1], VB[1]), (pd[1],), inc=True)
                for s_ in range(SQP):
                    lo = NP_ + s_ * 8
                    stt(DVE, S0f[:, s_, h, :], S0f[:, s_, h, :], EC[0][:, lo + 7:lo + 8], pd[0][:, s_ * 128:(s_ + 1) * 128], ALU.mult, ALU.add,
                        (S0f_b, EC[1], pd[1]), (S0f_b,))
                psp_free1(pd)
                tbp.free(KK)
                tbp.free(VB)
                tbp.free(AB)
                tbp.free(VT)
                tbp.free(QT)
                tfp.free(EC)
                yield
                OS = tfp.alloc()
                OQ = tbp.alloc()
                act(OQ[0][:, 0:NCOL], ps2(po)[:, 0:NCOL], AF.Square, po[1], (OQ[1],))
                pm = ps2p.alloc()
                pmv = ps2(pm)
                mm(pmv[:, 0:NP_], ONB2[:, :], OQ[0][:, 0:NP_], True, True, (CM_b, OQ[1]), (pm[1][0],))
                mm(pmv[:, 512:512 + NS], ONB2[:, :], OQ[0][:, NP_:NCOL], True, True, (CM_b, OQ[1]), (pm[1][1],), inc=True)
                tbp.free(OQ)
                yield
                act(OS[0][:, 0:NCOL], pmv[:, 0:NCOL], AF.Ln, pm[1], (OS[1],), bias=RMS_EPS)
                ps2p.free(pm)
                act(OS[0][:, 0:NCOL], OS[0][:, 0:NCOL], AF.Exp, (OS[1],), (OS[1],), scale=-0.5)
                ON = tfp.alloc()
                stt(DVE, ON[0][:, 0:NCOL], ps2(po)[:, 0:NCOL], cvc(O_HNH + j), OS[0][:, 0:NCOL], ALU.mult, ALU.mult,
                    tuple(po[1]) + (CV_b, OS[1]), (ON[1],))
                ps2p.free(po)
                SG = ctx["SG"]
                tt(DVE, mixo[:, h, :], ON[0][:, 0:NCOL], SG[0][:, 0:NCOL], ALU.mult, (ON[1], SG[1]), (mixo_b[h],))
                tfp.free(OS)
                tfp.free(ON)
                tfp.free(SG)
                yield

            def run_interleaved(gens):
                gens = [g for g in gens if g is not None]
                while gens:
                    for g in list(gens):
                        try:
                            next(g)
                        except StopIteration:
                            gens.remove(g)

            ctxs = [dict() for _ in range(8)]
            run_interleaved([proj_gen(0, ctxs[0])])
            for h in range(8):
                run_interleaved([chain_gen(h, ctxs[h]), proj_gen(h + 1, ctxs[h + 1]) if h + 1 < 8 else None])
            out_proj_and_z(mixo, mixo_b, 8, std_out_items(None))
            for s_ in range(SQP):
                dma(SP, S0o_t, ohs_d[j, p * SQP + s_].rearrange("h k v -> k h v"), S0f[:, s_, :, :], reads=(S0f_b,))
            if p == NPASS - 1:
                dma(SP, Spo_t, phs_d[j].rearrange("h k v -> k h v"), Sst[j][:, :, :], reads=tuple(Sst_b[j]))

        def psp_alloc1():
            return ps1p.alloc()

        def psp_free1(t):
            ps1p.free(t)

        def load_x(p):
            if DBG.get("p0", False):
                p = 0
            for tbk in range(DBG.get("ntbk", 4)):
                st, stb, stt_ = (stA, stA_b, stA_t) if tbk % 2 == 0 else (stB, stB_b, stB_t)
                dma(SP, stt_, st[:, 0:D], xp_d[p * NP_ + tbk * 128:p * NP_ + (tbk + 1) * 128, :], writes=(stb,))
                pt = ps2p.alloc()
                ptv = ps2(pt)
                for c in range(8):
                    tr(ptv[:, c * 128:(c + 1) * 128], st[:, c * 128:(c + 1) * 128], ident, (stb, CM_b), (pt[1][c // 4],), inc=(c % 4 == 3))
                cp(ACT, xf[:, :, tbk * 128:(tbk + 1) * 128], ptv.rearrange("p (c n) -> p c n", n=128), pt[1], tuple(xf_b))
                cp(DVE, xb[:, :, tbk * 128:(tbk + 1) * 128], xf[:, :, tbk * 128:(tbk + 1) * 128], tuple(xf_b), tuple(xb_b))
                ps2p.free(pt)
            if DBG.get("no_xs", False):
                return
            dma(SP, stA_t, stA[0:NS, 0:D], xs_d[p * NS:(p + 1) * NS, :], writes=(stA_b,))
            pt = ps1p.alloc()
            for c in range(8):
                tr(pt[0][:, c * NS:(c + 1) * NS], stA[0:NS, c * 128:(c + 1) * 128], ident[0:NS, 0:NS], (stA_b, CM_b), (pt[1],), inc=(c == 7))
            cp(ACT, xf[:, :, NP_:NCOL], pt[0][:, 0:8 * NS].rearrange("p (c n) -> p c n", n=NS), (pt[1],), tuple(xf_b))
            cp(DVE, xb[:, :, NP_:NCOL], xf[:, :, NP_:NCOL], tuple(xf_b), tuple(xb_b))
            ps1p.free(pt)
            if not DBG.get("io_rows", True):
                return
            for j in range(2):
                dma(SP, stB_t, stB[4 * j:4 * j + 4, 0:D], srh_d[j, p * SQP:(p + 1) * SQP, :], writes=(stB_b,))
            for j in range(2):
                dma(SP, stB_t, stB[32 + 12 * j:32 + 12 * j + 12, 0:D], src_d[j, p * SQP * 3:(p + 1) * SQP * 3, :], writes=(stB_b,))
            for g in range(2):
                for i in range(4):
                    dma(SP, stA_t, stA[32 * g + 8 * i:32 * g + 8 * i + 8, 0:SEGW],
                        sfc_d[i, p * SQP * 2:(p + 1) * SQP * 2, g * SEGW:(g + 1) * SEGW], writes=(stA_b,))
            pt = ps1p.alloc()
            for c in range(8):
                tr(pt[0][:, c * 8:(c + 1) * 8], stB[0:8, c * 128:(c + 1) * 128], ident[0:8, 0:8], (stB_b, CM_b), (pt[1],), inc=(c == 7))
            cp(DVE, h0s[:, :, :, :].rearrange("p j c s -> p c j s"),
               pt[0][:, 0:64].rearrange("p (c j s) -> p c j s", j=2, s=SQP), (pt[1],), (h0s_b,))
            ps1p.free(pt)
            pt = ps1p.alloc()
            for c in range(8):
                tr(pt[0][:, c * 24:(c + 1) * 24], stB[32:56, c * 128:(c + 1) * 128], ident[32:56, 32:56], (stB_b, CM_b), (pt[1],), inc=(c == 7))
            cp(DVE, cv0s[:, :, :, :].rearrange("p j c r -> p c j r"),
               pt[0][:, 0:192].rearrange("p (c j r) -> p c j r", j=2, r=SQP * 3), (pt[1],), (cv0s_b,))
            ps1p.free(pt)
            for g in range(2):
                pt = ps1p.alloc()
                for cc in range(11):
                    tr(pt[0][:, cc * 32:(cc + 1) * 32], stA[32 * g:32 * g + 32, cc * 128:(cc + 1) * 128],
                       ident[32 * g:32 * g + 32, 32 * g:32 * g + 32], (stA_b, CM_b), (pt[1],), inc=(cc == 10))
                cp(DVE, fc0s[:, :, 11 * g:11 * g + 11, :].rearrange("p i c r -> p c i r"),
                   pt[0][:, 0:352].rearrange("p (c i r) -> p c i r", i=4, r=SQP * 2), (pt[1],), (fc0s_b,))
                ps1p.free(pt)

        def store_y(p):
            for tbk in range(4):
                pt = ps2p.alloc()
                ptv = ps2(pt)
                for c in range(8):
                    tr(ptv[:, c * 128:(c + 1) * 128], xf[:, c, tbk * 128:(tbk + 1) * 128], ident, (xf_b[c], CM_b), (pt[1][c // 4],), inc=(c % 4 == 3))
                st, stb, stt_ = (stA, stA_b, stA_t) if tbk % 2 == 0 else (stB, stB_b, stB_t)
                cp(ACT, st[:, 0:D], ptv, pt[1], (stb,))
                ps2p.free(pt)
                dma(SP, stt_, yp_d[p * NP_ + tbk * 128:p * NP_ + (tbk + 1) * 128, :], st[:, 0:D], reads=(stb,))
            pt = ps2p.alloc()
            ptv = ps2(pt)
            for c in range(8):
                tr(ptv[0:NS, c * 128:(c + 1) * 128], xf[:, c, NP_:NCOL], ident, (xf_b[c], CM_b), (pt[1][c // 4],), inc=(c % 4 == 3))
            cp(ACT, stA[0:NS, 0:D], ptv[0:NS, :], pt[1], (stA_b,))
            ps2p.free(pt)
            dma(SP, stA_t, ys_d[p * NS:(p + 1) * NS, :], stA[0:NS, 0:D], reads=(stA_b,))

        marks = []

        def mark():
            marks.append(tuple(len(g.prog) for g in (PE, ACT, DVE, POOL, SP)))

        pass_idx = []
        for p in range(DBG["npass"]):
            pass_idx.append(len(marks))
            mark()
            load_x(p)
            for i in range(DBG["depth"]):
                mark()
                if DBG["mixer"]:
                    if i % 2 == 0:
                        rglru(i, p)
                    else:
                        hgrn(i, p)
                    if DBG["ln"]:
                        layer_norm(i, 0)
                if DBG["ffn"]:
                    ffn(i, p)
                    if DBG["ln"]:
                        layer_norm(i, 1)
            if DBG.get("io_store", True):
                store_y(p)

        for t in (stA_t, stB_t, S0o_t, Spo_t):
            if t.cnt:
                SP.wait(t, t.cnt)

        mark()
        engs = (PE, ACT, DVE, POOL, SP)
        bounds = [tuple(0 for _ in engs)] + marks
        nseg = len(bounds)
        starts = [0] + [pi + 1 for pi in pass_idx[1:]] + [nseg]
        if DBG.get("one_block", True):
            starts = [0, nseg]
        for bi in range(len(starts) - 1):
            with nc.Block() as block:
                regs = (block.tensor, block.scalar, block.vector, block.gpsimd, block.sync)
                for gi, (g, reg) in enumerate(zip(engs, regs)):
                    for si in range(starts[bi], starts[bi + 1]):
                        lo = bounds[si][gi]
                        hi = bounds[si + 1][gi] if si + 1 < nseg else len(g.prog)
                        if hi <= lo:
                            continue

                        def body(e, g=g, lo=lo, hi=hi):
                            for f_ in g.prog[lo:hi]:
                                f_(e)
                        reg(body)
    return nc


_CACHE = {}
DBG = {"npass": NPASS, "depth": DEPTH, "mixer": True, "ln": True, "ffn": True, "cores": NCORE}


def _consts():
    cm = np.zeros((128, CMW), np.float32)
    cm[:, M_ID:M_ID + 128] = np.eye(128, dtype=np.float32)
    s = np.arange(128)[:, None]
    t = np.arange(128)[None, :]
    cm[:, M_A2:M_A2 + 128] = ((s // 64 == t // 64) & (s <= t)).astype(np.float32)
    s = np.arange(32)[:, None]
    t = np.arange(32)[None, :]
    cm[0:32, M_AS:M_AS + 32] = ((s // 8 == t // 8) & (s <= t)).astype(np.float32)
    rst = np.ones(NCOL, np.float32)
    rst[0:NP_:64] = 0.0
    rst[NP_:NCOL:8] = 0.0
    cm[:, M_RST:M_RST + NCOL] = rst[None, :]
    for q in range(SQP):
        cm[q * 8:(q + 1) * 8, M_SM + q] = 1.0
    cm[:, M_ONE:M_ONE + 128] = 1.0 / 1024.0
    cm[:, M_ONE2:M_ONE2 + 128] = 1.0 / 128.0
    return cm


def kernel(x_prompt, x_sample, state_rglru_h, state_rglru_conv, state_hgrn_s, state_ffn_conv,
           ln_g, ln_b, rg_w_in, rg_conv_w, rg_conv_b, rg_gate_w, rg_gate_b, rg_lambda, rg_w_out,
           hg_lower, hg_w_in, hg_norm_g, hg_w_out, ffn_w_in, ffn_conv_w, ffn_conv_b, ffn_w_out):
    f = lambda a: np.ascontiguousarray(np.asarray(a, dtype=np.float32))
    x_prompt, x_sample = f(x_prompt), f(x_sample)
    state_rglru_h, state_rglru_conv = f(state_rglru_h), f(state_rglru_conv)
    state_hgrn_s, state_ffn_conv = f(state_hgrn_s), f(state_ffn_conv)
    cvec = np.zeros((NROWS, 128), np.float32)
    parts = [(O_LNG, ln_g), (O_LNB, ln_b), (O_RCW, rg_conv_w), (O_RCB, rg_conv_b), (O_RGB, rg_gate_b),
             (O_LAM, rg_lambda), (O_HLO, hg_lower), (O_HNG, hg_norm_g), (O_FCW, ffn_conv_w), (O_FCB, ffn_conv_b)]
    for off, a in parts:
        r = f(a).reshape(-1, 128)
        cvec[off:off + r.shape[0]] = r
    cmask = _consts()
    if "nc" not in _CACHE:
        _CACHE["nc"] = build_program()
    nc = _CACHE["nc"]
    shared = dict(cvec=cvec, cmask=cmask, rg_w_in=f(rg_w_in), rg_gate_w=f(rg_gate_w), rg_w_out=f(rg_w_out),
                  hg_w_in=f(hg_w_in), hg_w_out=f(hg_w_out), ffn_w_in=f(ffn_w_in), ffn_w_out=f(ffn_w_out))
    in_maps = []
    for c in range(NCORE):
        sl = slice(16 * c, 16 * c + 16)
        m = dict(shared)
        m["xp"] = x_prompt[c]
        m["xs"] = x_sample[sl].reshape(128, D)
        m["srh"] = np.ascontiguousarray(state_rglru_h[:, sl])
        m["src"] = np.ascontiguousarray(state_rglru_conv[:, sl]).reshape(2, 48, D)
        m["shs"] = np.ascontiguousarray(state_hgrn_s[:, sl])
        m["sfc"] = np.ascontiguousarray(state_ffn_conv[:, sl]).reshape(4, 32, DFF)
        in_maps.append(m)
    ncr = DBG["cores"]
    res = run_bass_kernel_spmd(nc, in_maps[:ncr], core_ids=list(range(ncr)))
    R = list(res.results) + [res.results[0]] * (NCORE - ncr)
    y_prompt = np.stack([R[c]["yp"] for c in range(NCORE)], 0)
    y_sample = np.concatenate([R[c]["ys"].reshape(16, 8, D) for c in range(NCORE)], 0)
    p_h = np.stack([R[c]["prh"] for c in range(NCORE)], 1)
    p_rc = np.stack([R[c]["prc"] for c in range(NCORE)], 1)
    p_s = np.stack([R[c]["phs"] for c in range(NCORE)], 1)
    p_fc = np.stack([R[c]["pfc"] for c in range(NCORE)], 1)
    s_h = np.concatenate([R[c]["orh"] for c in range(NCORE)], 1)
    s_rc = np.concatenate([R[c]["orc"].reshape(2, 16, 3, D) for c in range(NCORE)], 1)
    s_s = np.concatenate([R[c]["ohs"] for c in range(NCORE)], 1)
    s_fc = np.concatenate([R[c]["ofc"].reshape(4, 16, 2, DFF) for c in range(NCORE)], 1)
    return tuple(np.ascontiguousarray(a, dtype=np.float32) for a in
                 (y_prompt, y_sample, p_h, p_rc, p_s, p_fc, s_h, s_rc, s_s, s_fc))
```

```python
import contextlib
import numpy as np
import concourse.bass as bass
import concourse.mybir as mybir
from concourse.bass_utils import run_bass_kernel_spmd

F32 = mybir.dt.float32
BF16 = mybir.dt.bfloat16
AF = mybir.ActivationFunctionType
ALU = mybir.AluOpType

NCORE = 8
D = 1024
SEQ = 2048
DEPTH = 4
DFF = 2816
NJ = 22
NP_ = 512
SQP = 4
NS = SQP * 8
NCOL = NP_ + NS
NPASS = 4
ALPHA = (2.0 * DEPTH) ** 0.25
LN_EPS = 1e-5
RMS_EPS = 1e-6
NSLOT = 4
TW = 560
SEGW = 1408

O_LNG = 0
O_LNB = 64
O_RCW = 128
O_RCB = 192
O_RGB = 208
O_LAM = 240
O_HLO = 256
O_HNG = 272
O_FCW = 274
O_FCB = 538
NROWS = 640
O_C1 = 640
O_C2 = 656
O_LB = 672
O_OML = 688
O_FA = 704
O_FB = 720
O_HNH = 736
O_HC1 = 738
O_HGB = 754
CVW = 786

M_ID = 0
M_A2 = 128
M_AS = 256
M_RST = 288
M_SM = 288 + NCOL
M_ONE = M_SM + 4
M_ONE2 = M_ONE + 128
CMW = M_ONE2 + 128


class Tok:
    def __init__(self, sem):
        self.sem = sem
        self.cnt = 0


class Buf:
    __slots__ = ("w", "rs", "name")

    def __init__(self, name=""):
        self.w = None
        self.rs = []
        self.name = name


class Eng:
    def __init__(self, name, tok, is_pe=False):
        self.name = name
        self.prog = []
        self.tok = tok
        self.seen = {}
        self.is_pe = is_pe

    def wait(self, tok, cnt):
        if cnt > self.seen.get(tok, 0):
            sem = tok.sem
            self.prog.append(lambda e: e.wait_ge(sem, cnt))
            self.seen[tok] = cnt


def _deps(eng, reads, writes, skip_tok=None, skip_readers=True):
    need = {}

    def add(st, skippable=True):
        if st is None:
            return
        tok, c = st
        if skippable and tok is skip_tok:
            return
        if need.get(tok, 0) < c:
            need[tok] = c

    for b in reads:
        add(b.w)
    for b in writes:
        add(b.w)
        for r in b.rs:
            add(r, skip_readers)
    for tok, c in need.items():
        eng.wait(tok, c)


def _stamp(reads, writes, st):
    for b in reads:
        b.rs = [r for r in b.rs if r[0] is not st[0]]
        b.rs.append(st)
    for b in writes:
        b.w = st
        b.rs = []


def op(eng, fn, reads=(), writes=(), inc=True):
    _deps(eng, reads, writes, skip_tok=eng.tok if eng.is_pe else None)
    if inc:
        eng.tok.cnt += 1
        sem = eng.tok.sem
        eng.prog.append(lambda e: fn(e).then_inc(sem, 1))
        st = (eng.tok, eng.tok.cnt)
    else:
        eng.prog.append(fn)
        st = (eng.tok, eng.tok.cnt + 1)
    _stamp(reads, writes, st)


def dma(eng, tok, out, in_, reads=(), writes=()):
    _deps(eng, reads, writes, skip_tok=tok, skip_readers=False)
    tok.cnt += 16
    sem = tok.sem
    eng.prog.append(lambda e: e.dma_start(out=out, in_=in_).then_inc(sem, 16))
    _stamp(reads, writes, (tok, tok.cnt))


class Pool_:
    def __init__(self, items):
        self.free_ = list(items)

    def alloc(self):
        assert self.free_, "pool exhausted"
        return self.free_.pop(0)

    def free(self, it):
        self.free_.append(it)


def build_program():
    nc = bass.Bass("TRN2", target_bir_lowering=False)

    def din(name, shape):
        return nc.dram_tensor(name, list(shape), F32, kind="ExternalInput").ap()

    def dout(name, shape):
        return nc.dram_tensor(name, list(shape), F32, kind="ExternalOutput").ap()

    xp_d = din("xp", [SEQ, D])
    xs_d = din("xs", [16 * 8, D])
    srh_d = din("srh", [2, 16, D])
    src_d = din("src", [2, 16 * 3, D])
    shs_d = din("shs", [2, 16, 8, 128, 128])
    sfc_d = din("sfc", [4, 16 * 2, DFF])
    cvec_d = din("cvec", [NROWS, 128])
    cmask_d = din("cmask", [128, CMW])
    rg_w_in = din("rg_w_in", [2, D, 2 * D])
    rg_gate_w = din("rg_gate_w", [2, 2, 8, 128, 128])
    rg_w_out = din("rg_w_out", [2, D, D])
    hg_w_in = din("hg_w_in", [2, D, 4 * D])
    hg_w_out = din("hg_w_out", [2, D, D])
    ffn_w_in = din("ffn_w_in", [4, D, 2 * DFF])
    ffn_w_out = din("ffn_w_out", [4, DFF, D])

    yp_d = dout("yp", [SEQ, D])
    ys_d = dout("ys", [128, D])
    prh_d = dout("prh", [2, D])
    prc_d = dout("prc", [2, 3, D])
    phs_d = dout("phs", [2, 8, 128, 128])
    pfc_d = dout("pfc", [4, 2, DFF])
    orh_d = dout("orh", [2, 16, D])
    orc_d = dout("orc", [2, 16 * 3, D])
    ohs_d = dout("ohs", [2, 16, 8, 128, 128])
    ofc_d = dout("ofc", [4, 16 * 2, DFF])

    es = contextlib.ExitStack()
    with es:
        def sb(name, shape, dt=F32):
            return es.enter_context(nc.sbuf_tensor(name, list(shape), dt))

        def newtok(name):
            return Tok(es.enter_context(nc.semaphore(name)))

        xf = sb("xf", [128, 8, NCOL])
        xb = sb("xb", [128, 8, NCOL], BF16)
        mixo = sb("mixo", [128, 8, NCOL], BF16)
        hff = sb("hff", [128, NJ, NCOL], BF16)
        NT = 14
        NTB = 12
        tf = [sb(f"tf{i}", [128, TW]) for i in range(NT)]
        tb = [sb(f"tb{i}", [128, 640], BF16) for i in range(NTB)]
        wring = sb("wring", [128, NSLOT, 4096], BF16)
        stA = sb("stA", [128, SEGW])
        stB = sb("stB", [128, D])
        CV = sb("CV", [128, CVW])
        CM = sb("CM", [128, CMW])
        ONB = sb("ONB", [128, 128], BF16)
        ONB2 = sb("ONB2", [128, 128], BF16)
        Sst = [sb(f"Sst{j}", [128, 8, 128]) for j in range(2)]
        Sbf = [sb(f"Sbf{j}", [128, 8, 128], BF16) for j in range(2)]
        S0f = sb("S0f", [128, SQP, 8, 128])
        S0b = sb("S0b", [128, SQP, 8, 128], BF16)
        rg_tail = [sb(f"rgtail{j}", [128, 8, 3]) for j in range(2)]
        rg_hst = [sb(f"rghst{j}", [128, 8]) for j in range(2)]
        ffn_tail = [sb(f"fftail{i}", [128, NJ, 2]) for i in range(4)]
        h0s = sb("h0s", [128, 2, 8, SQP])
        cv0s = sb("cv0s", [128, 2, 8, SQP * 3])
        fc0s = sb("fc0s", [128, 4, NJ, SQP * 2])
        hs_o = sb("hs_o", [128, 2, 8, SQP])
        cv_o = sb("cv_o", [128, 2, 8, SQP * 3])
        fc_o = sb("fc_o", [128, 4, NJ, SQP * 2])
        ps_all = es.enter_context(nc.psum_tensor("ps_all", [128, 8, 512], F32))

        pe_t, act_t, dve_t, pool_t = newtok("pe"), newtok("act"), newtok("dve"), newtok("pool")

        xf_b = [Buf(f"xf{c}") for c in range(8)]
        xb_b = [Buf(f"xb{c}") for c in range(8)]
        mixo_b = [Buf() for _ in range(8)]
        hff_b = [Buf() for _ in range(NJ)]
        CV_b, CM_b = Buf("CV"), Buf("CM")
        Sst_b = [[Buf() for _ in range(8)] for _ in range(2)]
        Sbf_b = [[Buf() for _ in range(8)] for _ in range(2)]
        S0f_b, S0b_b = Buf("S0f"), Buf("S0b")
        rg_tail_b = [Buf() for _ in range(2)]
        rg_hst_b = [Buf() for _ in range(2)]
        ffn_tail_b = [Buf() for _ in range(4)]
        h0s_b, cv0s_b, fc0s_b = Buf(), Buf(), Buf()
        hs_o_b, cv_o_b, fc_o_b = Buf(), Buf(), Buf()
        stA_b, stB_b = Buf("stA"), Buf("stB")
        stA_t, stB_t = newtok("stA"), newtok("stB")
        S0f_t, S0b_t, S0o_t, Spo_t, cst_t = newtok("S0f"), newtok("S0b"), newtok("S0o"), newtok("Spo"), newtok("cst")
        wslot_b = [Buf(f"w{s}") for s in range(NSLOT)]
        wslot_t = [newtok(f"w{s}") for s in range(NSLOT)]

        tfp = Pool_([(tf[i], Buf(f"tf{i}"), Buf(f"tfx{i}")) for i in range(NT)])
        tbp = Pool_([(tb[i], Buf(f"tb{i}")) for i in range(NTB)])
        ps2p = Pool_([(ps_all[:, 2 * k:2 * k + 2, :], (Buf(f"ps{2 * k}"), Buf(f"ps{2 * k + 1}"))) for k in range(4)])

        class _Ps1:
            def alloc(self):
                t = ps2p.alloc()
                return (t[0][:, 0, :], t[1][0], t)

            def free(self, x):
                ps2p.free(x[2])
        ps1p = _Ps1()

        PE = Eng("pe", pe_t, is_pe=True)
        ACT = Eng("act", act_t)
        DVE = Eng("dve", dve_t)
        POOL = Eng("pool", pool_t)
        SP = Eng("sp", Tok(None))

        def ps2(t):
            return t[0].rearrange("p a b -> p (a b)")

        def mm(out, lhsT, rhs, start, stop, reads, writes, inc=False):
            op(PE, lambda e: e.matmul(out, lhsT, rhs, start=start, stop=stop), reads, writes, inc=inc)

        def tr(out, in_, ident, reads, writes, inc=False):
            op(PE, lambda e: e.matmul(out, in_, ident, start=True, stop=True), reads, writes, inc=inc)

        def act(out, in_, func, reads, writes, bias=None, scale=None):
            kw = {}
            if bias is not None:
                kw["bias"] = bias
            if scale is not None:
                kw["scale"] = scale
            op(ACT, lambda e: e.activation(out=out, in_=in_, func=func, **kw), reads, writes)

        def ts(eng, out, in0, s1, s2, op0, op1, reads, writes):
            if op1 is None:
                op(eng, lambda e: e.tensor_scalar(out=out, in0=in0, scalar1=s1, scalar2=None, op0=op0), reads, writes)
            else:
                op(eng, lambda e: e.tensor_scalar(out=out, in0=in0, scalar1=s1, scalar2=s2, op0=op0, op1=op1), reads, writes)

        def tt(eng, out, in0, in1, o, reads, writes):
            op(eng, lambda e: e.tensor_tensor(out=out, in0=in0, in1=in1, op=o), reads, writes)

        def stt(eng, out, in0, s, in1, op0, op1, reads, writes):
            op(eng, lambda e: e.scalar_tensor_tensor(out=out, in0=in0, scalar=s, in1=in1, op0=op0, op1=op1), reads, writes)

        def cp(eng, out, in_, reads, writes):
            if eng is ACT:
                op(eng, lambda e: e.copy(out=out, in_=in_), reads, writes)
            else:
                op(eng, lambda e: e.tensor_copy(out=out, in_=in_), reads, writes)

        def memset(eng, ap, val, writes):
            op(eng, lambda e: e.memset(ap, val), (), writes)

        def scan(out, d0, d1, init, reads, writes):
            op(DVE, lambda e: e.tensor_tensor_scan(out=out, data0=d0, data1=d1, initial=init, op0=ALU.mult, op1=ALU.add),
               reads, writes)

        ident = CM[:, M_ID:M_ID + 128]

        witems = []
        wstate = {"issued": 0, "next": 0, "released": set()}

        def w_prefetch():
            while wstate["issued"] < len(witems):
                k = wstate["issued"]
                if k >= NSLOT and (k - NSLOT) not in wstate["released"]:
                    break
                s = k % NSLOT
                for (dst_fn, src) in witems[k]:
                    dma(POOL, wslot_t[s], dst_fn(wring[:, s, :]), src, reads=(), writes=(wslot_b[s],))
                wstate["issued"] += 1

        def w_next():
            k = wstate["next"]
            wstate["next"] += 1
            w_prefetch()
            assert wstate["issued"] > k, "weight ring deadlock"
            s = k % NSLOT
            return wring[:, s, :], wslot_b[s], k

        def w_release(w):
            wstate["released"].add(w[2])
            w_prefetch()

        def it_cols(W, c0, w):
            return [(lambda sl: sl[:, 0:8 * w].rearrange("p (k n) -> p k n", n=w),
                     W[:, c0:c0 + w].rearrange("(k p) n -> p k n", p=128))]

        def build_witems():
            for p in range(DBG["npass"]):
                for i in range(DBG["depth"]):
                    j = i // 2
                    if not DBG["mixer"]:
                        pass
                    elif i % 2 == 0:
                        W = rg_w_in[j]
                        witems.append(it_cols(W, 0, 512))
                        witems.append(it_cols(W, 512, 512))
                        witems.append(it_cols(W, 1024, 512))
                        witems.append([(lambda sl: sl[:, 0:2048].rearrange("p (a n) -> p a n", n=128),
                                        rg_gate_w[j].rearrange("g h i n -> i (g h) n"))])
                        witems.append(it_cols(W, 1536, 512))
                        witems.append(it_cols(rg_w_out[j], 0, 512))
                        witems.append(it_cols(rg_w_out[j], 512, 512))
                    else:
                        W = hg_w_in[j]
                        for h in range(8):
                            witems.append([(lambda sl, t=t: sl[:, 0:4096].rearrange("p (k t n) -> p k t n", t=4, n=128)[:, :, t, :],
                                            W[:, t * 1024 + h * 128:t * 1024 + (h + 1) * 128].rearrange("(k p) n -> p k n", p=128))
                                           for t in range(4)])
                        witems.append(it_cols(hg_w_out[j], 0, 512))
                        witems.append(it_cols(hg_w_out[j], 512, 512))
                    if not DBG["ffn"]:
                        continue
                    W = ffn_w_in[i]
                    for q in range(6):
                        w = 512 if q < 5 else 256
                        witems.append(it_cols(W, q * 512, w))
                        witems.append(it_cols(W, DFF + q * 512, w))
                    for o in range(8):
                        witems.append([(lambda sl: sl[:, 0:NJ * 128].rearrange("p (k n) -> p k n", n=128),
                                        ffn_w_out[i][:, o * 128:(o + 1) * 128].rearrange("(k p) n -> p k n", p=128))])

        build_witems()

        dma(SP, cst_t, CM[:, :], cmask_d[:, :], writes=(CM_b,))
        for blk in range(5):
            dma(SP, stB_t, stB[:, 0:128], cvec_d[blk * 128:(blk + 1) * 128, :], writes=(stB_b,))
            pt = ps1p.alloc()
            tr(pt[0][:, 0:128], stB[:, 0:128], ident, (stB_b, CM_b), (pt[1],), inc=True)
            cp(DVE, CV[:, blk * 128:(blk + 1) * 128], pt[0][:, 0:128], (pt[1],), (CV_b,))
            ps1p.free(pt)
        act(CV[:, O_C1:O_C1 + 16], CV[:, O_LAM:O_LAM + 16], AF.Sigmoid, (CV_b,), (CV_b,))
        act(CV[:, O_C1:O_C1 + 16], CV[:, O_C1:O_C1 + 16], AF.Ln, (CV_b,), (CV_b,))
        ts(DVE, CV[:, O_C2:O_C2 + 16], CV[:, O_C1:O_C1 + 16], 16.0, None, ALU.mult, None, (CV_b,), (CV_b,))
        ts(DVE, CV[:, O_C1:O_C1 + 16], CV[:, O_C1:O_C1 + 16], 8.0, None, ALU.mult, None, (CV_b,), (CV_b,))
        memset(DVE, CV[:, O_LB:O_LB + 8], 0.0, (CV_b,))
        tt(DVE, CV[:, O_LB + 8:O_LB + 16], CV[:, O_HLO + 8:O_HLO + 16], CV[:, O_HLO:O_HLO + 8], ALU.subtract, (CV_b,), (CV_b,))
        act(CV[:, O_LB + 8:O_LB + 16], CV[:, O_LB + 8:O_LB + 16], AF.Sigmoid, (CV_b,), (CV_b,))
        ts(DVE, CV[:, O_OML:O_OML + 16], CV[:, O_LB:O_LB + 16], -1.0, 1.0, ALU.mult, ALU.add, (CV_b,), (CV_b,))
        ts(DVE, CV[:, O_FA:O_FA + 16], CV[:, O_OML:O_OML + 16], 0.5, None, ALU.mult, None, (CV_b,), (CV_b,))
        tt(DVE, CV[:, O_FB:O_FB + 16], CV[:, O_LB:O_LB + 16], CV[:, O_FA:O_FA + 16], ALU.add, (CV_b,), (CV_b,))
        ts(DVE, CV[:, O_HNH:O_HNH + 2], CV[:, O_HNG:O_HNG + 2], 0.5, None, ALU.mult, None, (CV_b,), (CV_b,))
        ts(DVE, CV[:, O_HC1:O_HC1 + 16], CV[:, O_C1:O_C1 + 16], 0.5, None, ALU.mult, None, (CV_b,), (CV_b,))
        ts(DVE, CV[:, O_HGB:O_HGB + 32], CV[:, O_RGB:O_RGB + 32], 0.5, None, ALU.mult, None, (CV_b,), (CV_b,))
        for j in range(2):
            memset(DVE, Sst[j][:, :, :], 0.0, tuple(Sst_b[j]))
            memset(DVE, Sbf[j][:, :, :], 0.0, tuple(Sbf_b[j]))
            memset(DVE, rg_tail[j][:, :, :], 0.0, (rg_tail_b[j],))
            memset(DVE, rg_hst[j][:, :], 0.0, (rg_hst_b[j],))
        for i in range(4):
            memset(DVE, ffn_tail[i][:, :, :], 0.0, (ffn_tail_b[i],))

        ones_ln = CM[:, M_ONE:M_ONE + 128]
        cp(DVE, ONB[:, :], ones_ln, (CM_b,), (CM_b,))
        cp(DVE, ONB2[:, :], CM[:, M_ONE2:M_ONE2 + 128], (CM_b,), (CM_b,))
        ones_rms = CM[:, M_ONE2:M_ONE2 + 128]

        def cvc(off):
            return CV[:, off:off + 1]

        def rows_in(dram, R, C, dst_fn, dst_bufs):
            for s0 in range(0, C, SEGW):
                sw = min(SEGW, C - s0)
                dma(SP, stA_t, stA[0:R, 0:sw], dram[:, s0:s0 + sw], writes=(stA_b,))
                nch = sw // 128
                g = max(1, min(nch, 512 // R))
                c0 = 0
                while c0 < nch:
                    n = min(g, nch - c0)
                    pt = ps1p.alloc()
                    for k in range(n):
                        tr(pt[0][:, k * R:(k + 1) * R], stA[0:R, (c0 + k) * 128:(c0 + k + 1) * 128], ident[0:R, 0:R],
                           (stA_b, CM_b), (pt[1],), inc=(k == n - 1))
                    cp(DVE, dst_fn(s0 // 128 + c0, n), pt[0][:, 0:n * R].rearrange("p (n r) -> p n r", r=R), (pt[1],), dst_bufs)
                    ps1p.free(pt)
                    c0 += n

        def rows_out(src_fn, src_bufs, R, C, dram):
            for s0 in range(0, C, SEGW):
                sw = min(SEGW, C - s0)
                nch = sw // 128
                c0 = 0
                while c0 < nch:
                    n = min(4, nch - c0)
                    pt = ps1p.alloc()
                    for k in range(n):
                        tr(pt[0][0:R, k * 128:(k + 1) * 128], src_fn(s0 // 128 + c0 + k), ident, tuple(src_bufs) + (CM_b,), (pt[1],),
                           inc=(k == n - 1))
                    cp(ACT, stA[0:R, c0 * 128:(c0 + n) * 128], pt[0][0:R, 0:n * 128], (pt[1],), (stA_b,))
                    ps1p.free(pt)
                    c0 += n
                dma(SP, stA_t, dram[:, s0:s0 + sw], stA[0:R, 0:sw], reads=(stA_b,))

        def proj(ps_tile, lhs_fn, rhs_t, rhs_bufs, nk, wbuf, last_inc=True):
            pv = ps2(ps_tile)
            for k in range(nk):
                mm(pv[:, 0:NP_], lhs_fn(k), rhs_t[:, k, 0:NP_], k == 0, k == nk - 1,
                   (wbuf, rhs_bufs[k]), (ps_tile[1][0],))
            for k in range(nk):
                mm(pv[:, 512:512 + NS], lhs_fn(k), rhs_t[:, k, NP_:NCOL], k == 0, k == nk - 1,
                   (wbuf, rhs_bufs[k]), (ps_tile[1][1],), inc=(last_inc and k == nk - 1))

        def pcols(ps_tile):
            pv = ps2(ps_tile)
            return pv[:, 0:NP_], pv[:, 512:512 + NS]

        ln_state = {}

        def ln_stats_begin():
            pm = ps2p.alloc()
            pq = ps2p.alloc()
            ln_state.update(pm=pm, pq=pq, pend=None)

        def ln_stats_prep(c):
            zb = tbp.alloc()
            zq = tbp.alloc()
            cp(DVE, zb[0][:, 0:NCOL], xf[:, c, :], (xf_b[c],), (zb[1],))
            act(zq[0][:, 0:NCOL], xf[:, c, :], AF.Square, (xf_b[c],), (zq[1],))
            return (c, zb, zq)

        def ln_stats_mm(prep):
            c, zb, zq = prep
            pm, pq = ln_state["pm"], ln_state["pq"]
            for (pt, src) in ((pm, zb), (pq, zq)):
                ptv = ps2(pt)
                mm(ptv[:, 0:NP_], ONB[:, :], src[0][:, 0:NP_], c == 0, c == 7, (CM_b, src[1]), (pt[1][0],))
                mm(ptv[:, 512:512 + NS], ONB[:, :], src[0][:, NP_:NCOL], c == 0, c == 7, (CM_b, src[1]), (pt[1][1],),
                   inc=True)
            tbp.free(zb)
            tbp.free(zq)

        def layer_norm(i, s):
            if "pm" not in ln_state:
                ln_stats_begin()
                for c in range(8):
                    ln_stats_mm(ln_stats_prep(c))
            pm, pq = ln_state.pop("pm"), ln_state.pop("pq")
            ln_state.clear()
            pmv, pqv = ps2(pm), ps2(pq)
            m2, rstd, nmr = tfp.alloc(), tfp.alloc(), tfp.alloc()
            for (lo, hi, plo, hb) in ((0, NP_, 0, 0), (NP_, NCOL, 512, 1)):
                n = hi - lo
                act(m2[0][:, lo:hi], pmv[:, plo:plo + n], AF.Square, (pm[1][hb],), (m2[1],))
                tt(DVE, m2[0][:, lo:hi], pqv[:, plo:plo + n], m2[0][:, lo:hi], ALU.subtract, (pq[1][hb], m2[1]), (m2[1],))
            act(rstd[0][:, 0:NCOL], m2[0][:, 0:NCOL], AF.Ln, (m2[1],), (rstd[1],), bias=LN_EPS)
            act(rstd[0][:, 0:NCOL], rstd[0][:, 0:NCOL], AF.Exp, (rstd[1],), (rstd[1],), scale=-0.5)
            for (lo, hi, plo, hb) in ((0, NP_, 0, 0), (NP_, NCOL, 512, 1)):
                n = hi - lo
                stt(DVE, nmr[0][:, lo:hi], pmv[:, plo:plo + n], -1.0, rstd[0][:, lo:hi], ALU.mult, ALU.mult,
                    (pm[1][hb], rstd[1]), (nmr[1],))
            ps2p.free(pm)
            ps2p.free(pq)
            tfp.free(m2)
            for c in range(8):
                t = tfp.alloc()
                tt(DVE, t[0][:, 0:NCOL], xf[:, c, :], rstd[0][:, 0:NCOL], ALU.mult, (xf_b[c], rstd[1]), (t[1],))
                tt(POOL, t[0][:, 0:NCOL], t[0][:, 0:NCOL], nmr[0][:, 0:NCOL], ALU.add, (t[1], nmr[1]), (t[1],))
                g_ap = cvc(O_LNG + (i * 2 + s) * 8 + c)
                b_ap = cvc(O_LNB + (i * 2 + s) * 8 + c)
                act(xf[:, c, :], t[0][:, 0:NCOL], AF.Identity, (t[1], CV_b), (xf_b[c],), bias=b_ap, scale=g_ap)
                act(xb[:, c, :], t[0][:, 0:NCOL], AF.Identity, (t[1], CV_b), (xb_b[c],), bias=b_ap, scale=g_ap)
                tfp.free(t)
            tfp.free(rstd)
            tfp.free(nmr)

        def out_proj_and_z(src_t, src_bufs, nk, items_fn):
            fuse_ln = DBG["ln"]
            if fuse_ln:
                ln_stats_begin()
            prep_prev = None
            for o in range(8):
                lhs_fn, wbuf, rel = items_fn(o)
                pt = ps2p.alloc()
                proj(pt, lhs_fn, src_t, src_bufs, nk, wbuf)
                if rel is not None:
                    rel()
                if prep_prev is not None:
                    ln_stats_mm(prep_prev)
                    prep_prev = None
                a, b = pcols(pt)
                stt(DVE, xf[:, o, 0:NP_], xf[:, o, 0:NP_], ALPHA, a, ALU.mult, ALU.add, (xf_b[o], pt[1][0]), (xf_b[o],))
                stt(DVE, xf[:, o, NP_:NCOL], xf[:, o, NP_:NCOL], ALPHA, b, ALU.mult, ALU.add, (xf_b[o], pt[1][1]), (xf_b[o],))
                ps2p.free(pt)
                if fuse_ln:
                    prep_prev = ln_stats_prep(o)
            if prep_prev is not None:
                ln_stats_mm(prep_prev)

        def std_out_items(nhalf_getter):
            cache = {}

            def f(o):
                hh = o // 4
                if hh not in cache:
                    cache[hh] = w_next()
                ws, wb_, _k = cache[hh]
                ol = o % 4
                rel = (lambda: w_release(cache[hh])) if ol == 3 else None
                return (lambda k: ws[:, k * 512 + ol * 128:k * 512 + (ol + 1) * 128]), wb_, rel
            return f

        def rglru(i, p):
            j = i // 2
            wg = None
            for c in range(8):
                cl = c % 4
                if cl == 0:
                    wg = w_next()
                pg = ps2p.alloc()
                proj(pg, lambda k: wg[0][:, k * 512 + cl * 128:k * 512 + (cl + 1) * 128], xb, xb_b, 8, wg[1])
                act(mixo[:, c, :], ps2(pg)[:, 0:NCOL], AF.Gelu_apprx_tanh, pg[1], (mixo_b[c],))
                ps2p.free(pg)
                if cl == 3:
                    w_release(wg)
            wst = {}

            def s1(c):
                cl = c % 4
                if cl == 0:
                    wst["wx"] = w_next()
                    if c == 0:
                        wst["gw"] = w_next()
                wx = wst["wx"]
                px = ps2p.alloc()
                proj(px, lambda k: wx[0][:, k * 512 + cl * 128:k * 512 + (cl + 1) * 128], xb, xb_b, 8, wx[1])
                if cl == 3:
                    w_release(wx)
                XB = tfp.alloc()
                xbs = XB[0][:, 515:515 + SQP * 11].rearrange("p (s k) -> p s k", k=11)
                pxa, pxb = pcols(px)
                cp(DVE, XB[0][:, 0:3], rg_tail[j][:, c, :], (rg_tail_b[j],), (XB[2],))
                cp(DVE, xbs[:, :, 0:3], cv0s[:, j, c, :].rearrange("p (s k) -> p s k", k=3), (cv0s_b,), (XB[2],))
                cp(ACT, XB[0][:, 3:515], pxa, (px[1][0],), (XB[1],))
                cp(ACT, xbs[:, :, 3:11], pxb.rearrange("p (s k) -> p s k", k=8), (px[1][1],), (XB[1],))
                ps2p.free(px)
                cp(POOL, rg_tail[j][:, c, :], XB[0][:, 512:515], (XB[1],), (rg_tail_b[j],))
                cp(POOL, cv_o[:, j, c, :].rearrange("p (s k) -> p s k", k=3), xbs[:, :, 8:11], (XB[1],), (cv_o_b,))
                XC = tfp.alloc()
                xcs = XC[0][:, NP_:NCOL].rearrange("p (s k) -> p s k", k=8)
                wcol = lambda tap: cvc(O_RCW + (j * 4 + tap) * 8 + c)
                bcol = cvc(O_RCB + j * 8 + c)
                act(XC[0][:, 0:NP_], XB[0][:, 0:512], AF.Identity, (XB[1], XB[2], CV_b), (XC[1],), bias=bcol, scale=wcol(0))
                ts(POOL, xcs, xbs[:, :, 0:8], wcol(0), bcol, ALU.mult, ALU.add, (XB[1], XB[2], CV_b), (XC[2],))
                for tap in (1, 2, 3):
                    stt(DVE, XC[0][:, 0:NP_], XB[0][:, tap:tap + 512], wcol(tap), XC[0][:, 0:NP_], ALU.mult, ALU.add,
                        (XB[1], XB[2], CV_b, XC[1]), (XC[1],))
                    stt(DVE, xcs, xbs[:, :, tap:tap + 8], wcol(tap), xcs, ALU.mult, ALU.add, (XB[1], XB[2], CV_b, XC[2]), (XC[2],))
                tfp.free(XB)
                XCB = tbp.alloc()
                cp(ACT, XCB[0][:, 0:NCOL], XC[0][:, 0:NCOL], (XC[1], XC[2]), (XCB[1],))
                return dict(c=c, XC=XC, XCB=XCB)

            def s2(cx):
                c, XCB = cx["c"], cx["XCB"]
                gw = wst["gw"]
                gwv = gw[0][:, 0:2048].rearrange("p (a n) -> p a n", n=128)
                pr = ps2p.alloc()
                pi = ps2p.alloc()
                for (pt, gi) in ((pr, 0), (pi, 1)):
                    ptv = ps2(pt)
                    mm(ptv[:, 0:NP_], gwv[:, gi * 8 + c, :], XCB[0][:, 0:NP_], True, True, (gw[1], XCB[1]), (pt[1][0],))
                    mm(ptv[:, 512:512 + NS], gwv[:, gi * 8 + c, :], XCB[0][:, NP_:NCOL], True, True, (gw[1], XCB[1]), (pt[1][1],), inc=True)
                tbp.free(XCB)
                if c == 7:
                    w_release(gw)
                R, IG, A = tfp.alloc(), tfp.alloc(), tfp.alloc()
                act(R[0][:, 0:NCOL], ps2(pr)[:, 0:NCOL], AF.Tanh, tuple(pr[1]) + (CV_b,), (R[1],),
                    bias=cvc(O_HGB + (j * 2 + 0) * 8 + c), scale=0.5)
                act(IG[0][:, 0:NCOL], ps2(pi)[:, 0:NCOL], AF.Tanh, tuple(pi[1]) + (CV_b,), (IG[1],),
                    bias=cvc(O_HGB + (j * 2 + 1) * 8 + c), scale=0.5)
                ps2p.free(pr)
                ps2p.free(pi)
                act(A[0][:, 0:NCOL], R[0][:, 0:NCOL], AF.Exp, (R[1], CV_b), (A[1],),
                    bias=cvc(O_HC1 + j * 8 + c), scale=cvc(O_HC1 + j * 8 + c))
                act(R[0][:, 0:NCOL], R[0][:, 0:NCOL], AF.Exp, (R[1], CV_b), (R[1],),
                    bias=cvc(O_C1 + j * 8 + c), scale=cvc(O_C1 + j * 8 + c))
                act(R[0][:, 0:NCOL], R[0][:, 0:NCOL], AF.Ln, (R[1],), (R[1],), bias=1.0, scale=-1.0)
                act(R[0][:, 0:NCOL], R[0][:, 0:NCOL], AF.Exp, (R[1],), (R[1],), scale=0.5)
                cx.update(R=R, IG=IG, A=A)
                return cx

            def s3(cx):
                c, XC, R, IG, A = cx["c"], cx["XC"], cx["R"], cx["IG"], cx["A"]
                stt(DVE, IG[0][:, 0:NCOL], IG[0][:, 0:NCOL], 1.0, R[0][:, 0:NCOL], ALU.add, ALU.mult, (IG[1], R[1]), (IG[1],))
                stt(DVE, IG[0][:, 0:NCOL], IG[0][:, 0:NCOL], 0.5, XC[0][:, 0:NCOL], ALU.mult, ALU.mult,
                    (IG[1], XC[1], XC[2]), (IG[1],))
                tfp.free(XC)
                As = A[0][:, NP_:NCOL].rearrange("p (s k) -> p s k", k=8)
                Bs = IG[0][:, NP_:NCOL].rearrange("p (s k) -> p s k", k=8)
                tmp = R[0][:, 0:SQP].rearrange("p (s k) -> p s k", k=1)
                tt(DVE, tmp, As[:, :, 0:1], h0s[:, j, c, :].rearrange("p (s k) -> p s k", k=1), ALU.mult, (A[1], h0s_b, R[1]), (R[1],))
                tt(DVE, Bs[:, :, 0:1], Bs[:, :, 0:1], tmp, ALU.add, (IG[1], R[1]), (IG[1],))
                memset(DVE, As[:, :, 0:1], 0.0, (A[1],))
                H = R
                scan(H[0][:, 0:NP_], A[0][:, 0:NP_], IG[0][:, 0:NP_], rg_hst[j][:, c:c + 1], (A[1], IG[1], rg_hst_b[j]), (H[1],))
                scan(H[0][:, NP_:NCOL], A[0][:, NP_:NCOL], IG[0][:, NP_:NCOL], 0.0, (A[1], IG[1]), (H[1],))
                cp(POOL, rg_hst[j][:, c:c + 1], H[0][:, NP_ - 1:NP_], (H[1],), (rg_hst_b[j],))
                Hs = H[0][:, NP_:NCOL].rearrange("p (s k) -> p s k", k=8)
                cp(POOL, hs_o[:, j, c, :].rearrange("p (s k) -> p s k", k=1), Hs[:, :, 7:8], (H[1],), (hs_o_b,))
                tt(DVE, mixo[:, c, :], H[0][:, 0:NCOL], mixo[:, c, :], ALU.mult, (H[1], mixo_b[c]), (mixo_b[c],))
                tfp.free(R)
                tfp.free(IG)
                tfp.free(A)

            st1, st2 = {}, {}
            for t in range(8 + 2):
                if t < 8:
                    st1[t] = s1(t)
                if 0 <= t - 1 < 8:
                    st2[t - 1] = s2(st1.pop(t - 1))
                if 0 <= t - 2 < 8:
                    s3(st2.pop(t - 2))
            out_proj_and_z(mixo, mixo_b, 8, std_out_items(None))
            rows_out(lambda c: hs_o[:, j, c, :], (hs_o_b,), SQP, D, orh_d[j, p * SQP:(p + 1) * SQP, :])
            rows_out(lambda c: cv_o[:, j, c, :], (cv_o_b,), SQP * 3, D, orc_d[j, p * SQP * 3:(p + 1) * SQP * 3, :])
            if p == NPASS - 1:
                rows_out(lambda c: rg_hst[j][:, c:c + 1], (rg_hst_b[j],), 1, D, prh_d[j:j + 1, :])
                rows_out(lambda c: rg_tail[j][:, c, :], (rg_tail_b[j],), 3, D, prc_d[j])

        def ffn(i, p):
            def stage_a(jc, cl, w, wg, wu):
                pg = ps2p.alloc()
                pu = ps2p.alloc()
                proj(pg, lambda k: wg[0][:, k * w + cl * 128:k * w + (cl + 1) * 128], xb, xb_b, 8, wg[1])
                proj(pu, lambda k: wu[0][:, k * w + cl * 128:k * w + (cl + 1) * 128], xb, xb_b, 8, wu[1])
                GB = tfp.alloc()
                gbs = GB[0][:, 514:514 + SQP * 10].rearrange("p (s k) -> p s k", k=10)
                pga, pgb = pcols(pg)
                cp(DVE, GB[0][:, 0:2], ffn_tail[i][:, jc, :], (ffn_tail_b[i],), (GB[2],))
                cp(DVE, gbs[:, :, 0:2], fc0s[:, i, jc, :].rearrange("p (s k) -> p s k", k=2), (fc0s_b,), (GB[2],))
                cp(ACT, GB[0][:, 2:514], pga, (pg[1][0],), (GB[1],))
                cp(ACT, gbs[:, :, 2:10], pgb.rearrange("p (s k) -> p s k", k=8), (pg[1][1],), (GB[1],))
                ps2p.free(pg)
                UP = tfp.alloc()
                cp(ACT, UP[0][:, 0:NCOL], ps2(pu)[:, 0:NCOL], pu[1], (UP[1],))
                ps2p.free(pu)
                cp(POOL, ffn_tail[i][:, jc, :], GB[0][:, 512:514], (GB[1],), (ffn_tail_b[i],))
                cp(POOL, fc_o[:, i, jc, :].rearrange("p (s k) -> p s k", k=2), gbs[:, :, 8:10], (GB[1],), (fc_o_b,))
                AC = tfp.alloc()
                acs = AC[0][:, NP_:NCOL].rearrange("p (s k) -> p s k", k=8)
                bcol = cvc(O_FCB + i * NJ + jc)
                w0 = cvc(O_FCW + (i * 3 + 0) * NJ + jc)
                act(AC[0][:, 0:NP_], GB[0][:, 0:512], AF.Identity, (GB[1], GB[2], CV_b), (AC[1],), bias=bcol, scale=w0)
                ts(POOL, acs, gbs[:, :, 0:8], w0, bcol, ALU.mult, ALU.add, (GB[1], GB[2], CV_b), (AC[2],))
                return dict(jc=jc, GB=GB, gbs=gbs, AC=AC, acs=acs, UP=UP)

            def stage_b(cx):
                jc, GB, gbs, AC, acs, UP = cx["jc"], cx["GB"], cx["gbs"], cx["AC"], cx["acs"], cx["UP"]
                for tap in (1, 2):
                    wt = cvc(O_FCW + (i * 3 + tap) * NJ + jc)
                    stt(DVE, AC[0][:, 0:NP_], GB[0][:, tap:tap + 512], wt, AC[0][:, 0:NP_], ALU.mult, ALU.add,
                        (GB[1], GB[2], CV_b, AC[1]), (AC[1],))
                    stt(DVE, acs, gbs[:, :, tap:tap + 8], wt, acs, ALU.mult, ALU.add, (GB[1], GB[2], CV_b, AC[2]), (AC[2],))
                tfp.free(GB)
                act(AC[0][:, 0:NCOL], AC[0][:, 0:NCOL], AF.Gelu_apprx_tanh, (AC[1], AC[2]), (AC[1], AC[2]))
                tt(POOL, hff[:, jc, :], UP[0][:, 0:NCOL], AC[0][:, 0:NCOL], ALU.mult, (UP[1], AC[1], AC[2]), (hff_b[jc],))
                tfp.free(UP)
                tfp.free(AC)

            pending = None
            for q in range(6):
                w = 512 if q < 5 else 256
                wg = w_next()
                wu = w_next()
                ncl = w // 128
                for cl in range(ncl):
                    cx = stage_a(q * 4 + cl, cl, w, wg, wu)
                    if cl == ncl - 1:
                        w_release(wg)
                        w_release(wu)
                    if pending is not None:
                        stage_b(pending)
                    pending = cx
            stage_b(pending)

            def items(o):
                w_ = w_next()
                return (lambda k: w_[0][:, k * 128:(k + 1) * 128]), w_[1], (lambda: w_release(w_))
            out_proj_and_z(hff, hff_b, NJ, items)
            rows_out(lambda c: fc_o[:, i, c, :], (fc_o_b,), SQP * 2, DFF, ofc_d[i, p * SQP * 2:(p + 1) * SQP * 2, :])
            if p == NPASS - 1:
                rows_out(lambda c: ffn_tail[i][:, c, :], (ffn_tail_b[i],), 2, DFF, pfc_d[i])

        def hgrn(i, p):
            j = i // 2
            for s_ in range(SQP):
                dma(SP, S0f_t, S0f[:, s_, :, :], shs_d[j, p * SQP + s_].rearrange("h k v -> k h v"), writes=(S0f_b,))
            for s_ in range(SQP):
                dma(POOL, S0b_t, S0b[:, s_, :, :], shs_d[j, p * SQP + s_].rearrange("h k v -> k h v"), writes=(S0b_b,))
            rst = CM[:, M_RST:M_RST + NCOL]

            def proj_gen(h, ctx):
                wit = w_next()
                ws, wb_ = wit[0], wit[1]
                wv = ws[:, 0:4096].rearrange("p (k t n) -> p k t n", t=4, n=128)
                ctx["wv"], ctx["wb"], ctx["wit"] = wv, wb_, wit
                pq = ps2p.alloc()
                proj(pq, lambda k: wv[:, k, 0, :], xb, xb_b, 8, wb_)
                TQ = tfp.alloc()
                act(TQ[0][:, 0:NCOL], ps2(pq)[:, 0:NCOL], AF.Tanh, pq[1], (TQ[1],), scale=0.5)
                stt(DVE, TQ[0][:, 0:NCOL], TQ[0][:, 0:NCOL], 1.0, ps2(pq)[:, 0:NCOL], ALU.add, ALU.mult,
                    (TQ[1],) + tuple(pq[1]), (TQ[1],))
                ps2p.free(pq)
                yield
                pf = ps2p.alloc()
                proj(pf, lambda k: wv[:, k, 1, :], xb, xb_b, 8, wb_)
                T1 = tfp.alloc()
                act(T1[0][:, 0:NCOL], ps2(pf)[:, 0:NCOL], AF.Tanh, pf[1], (T1[1],), scale=0.5)
                ps2p.free(pf)
                yield
                pg = ps2p.alloc()
                proj(pg, lambda k: wv[:, k, 3, :], xb, xb_b, 8, wb_)
                SG = tfp.alloc()
                act(SG[0][:, 0:NCOL], ps2(pg)[:, 0:NCOL], AF.Tanh, pg[1], (SG[1],), scale=0.5)
                stt(DVE, SG[0][:, 0:NCOL], SG[0][:, 0:NCOL], 1.0, ps2(pg)[:, 0:NCOL], ALU.add, ALU.mult,
                    (SG[1],) + tuple(pg[1]), (SG[1],))
                ps2p.free(pg)
                ctx["SG"] = SG
                yield
                T2, T3 = tfp.alloc(), tfp.alloc()
                ts(DVE, T1[0][:, 0:NCOL], T1[0][:, 0:NCOL], cvc(O_FA + j * 8 + h), cvc(O_FB + j * 8 + h), ALU.mult, ALU.add,
                   (T1[1], CV_b), (T1[1],))
                yield
                ts(DVE, T1[0][:, 0:NCOL], T1[0][:, 0:NCOL], 1e-30, None, ALU.max, None, (T1[1],), (T1[1],))
                yield
                act(T2[0][:, 0:NCOL], T1[0][:, 0:NCOL], AF.Ln, (T1[1],), (T2[1],))
                yield
                ts(DVE, T1[0][:, 0:NCOL], T1[0][:, 0:NCOL], -1.0, 1.0, ALU.mult, ALU.add, (T1[1],), (T1[1],))
                yield
                scan(T3[0][:, 0:NCOL], rst, T2[0][:, 0:NCOL], 0.0, (CM_b, T2[1]), (T3[1],))
                yield
                EC = tfp.alloc()
                act(EC[0][:, 0:NCOL], T3[0][:, 0:NCOL], AF.Exp, (T3[1],), (EC[1],))
                act(T2[0][:, 0:NCOL], T3[0][:, 0:NCOL], AF.Exp, (T3[1],), (T2[1],), scale=-1.0)
                yield
                ctx["EC"] = EC
                tfp.free(T3)
                QT = tbp.alloc()
                stt(DVE, QT[0][:, 0:NCOL], TQ[0][:, 0:NCOL], 0.5 * 128.0 ** -0.5, EC[0][:, 0:NCOL], ALU.mult, ALU.mult,
                    (TQ[1], EC[1]), (QT[1],))
                yield
                ctx["QT"] = QT
                tfp.free(TQ)
                tt(DVE, T2[0][:, 0:NCOL], T1[0][:, 0:NCOL], T2[0][:, 0:NCOL], ALU.mult, (T1[1], T2[1]), (T2[1],))
                yield
                tfp.free(T1)
                KT = tbp.alloc()
                cp(ACT, KT[0][:, 0:NCOL], T2[0][:, 0:NCOL], (T2[1],), (KT[1],))
                yield
                KH = tfp.alloc()
                tt(DVE, KH[0][:, 0:NP_].rearrange("p (c k) -> p c k", k=64), T2[0][:, 0:NP_].rearrange("p (c k) -> p c k", k=64),
                   EC[0][:, 63:NP_:64].unsqueeze(2).broadcast_to([128, 8, 64]), ALU.mult, (T2[1], EC[1]), (KH[1],))
                tt(DVE, KH[0][:, NP_:NCOL].rearrange("p (c k) -> p c k", k=8), T2[0][:, NP_:NCOL].rearrange("p (c k) -> p c k", k=8),
                   EC[0][:, NP_ + 7:NCOL:8].unsqueeze(2).broadcast_to([128, SQP, 8]), ALU.mult, (T2[1], EC[1]), (KH[1],))
                tfp.free(T2)
                yield
                pvt = ps2p.alloc()
                pv = (pvt[0][:, 0, :], pvt[1][0])
                pvs = (pvt[0][:, 1, :], pvt[1][1])
                for tbk in range(4):
                    for k in range(8):
                        mm(pv[0][:, tbk * 128:(tbk + 1) * 128], xb[:, k, tbk * 128:(tbk + 1) * 128], wv[:, k, 2, :], k == 0, k == 7,
                           (xb_b[k], wb_), (pv[1],), inc=(tbk == 3 and k == 7))
                    if tbk % 2 == 1:
                        yield
                for k in range(8):
                    mm(pvs[0][0:NS, 0:128], xb[:, k, NP_:NCOL], wv[:, k, 2, :], k == 0, k == 7, (xb_b[k], wb_), (pvs[1],), inc=(k == 7))
                w_release(ctx["wit"])
                VT = tbp.alloc()
                vtv = VT[0][:, 0:640].rearrange("p (b n) -> p b n", n=128)
                cp(ACT, vtv[:, 0:4, :], pv[0][:, 0:512].rearrange("p (b n) -> p b n", n=128), (pv[1],), (VT[1],))
                cp(ACT, vtv[0:NS, 4, :], pvs[0][0:NS, 0:128], (pvs[1],), (VT[1],))
                yield
                ctx["VT"] = VT
                for sc in range(4):
                    mm(pv[0][:, sc * 128:(sc + 1) * 128], KT[0][:, sc * 128:(sc + 1) * 128], QT[0][:, sc * 128:(sc + 1) * 128], True, True,
                       (KT[1], QT[1]), (pv[1],), inc=(sc == 3))
                mm(pvs[0][0:NS, 0:NS], KT[0][:, NP_:NCOL], QT[0][:, NP_:NCOL], True, True, (KT[1], QT[1]), (pvs[1],), inc=True)
                AB = tbp.alloc()
                tt(DVE, AB[0][:, 0:512].rearrange("p (b n) -> p b n", n=128), pv[0][:, 0:512].rearrange("p (b n) -> p b n", n=128),
                   CM[:, M_A2:M_A2 + 128].unsqueeze(1).broadcast_to([128, 4, 128]), ALU.mult, (pv[1], CM_b), (AB[1],))
                tt(DVE, AB[0][0:NS, 512:512 + NS], pvs[0][0:NS, 0:NS], CM[0:NS, M_AS:M_AS + NS], ALU.mult, (pvs[1], CM_b), (AB[1],))
                yield
                ctx["AB"] = AB
                tbp.free(KT)
                yield
                for sc in range(4):
                    tr(pv[0][:, sc * 128:(sc + 1) * 128], KH[0][:, sc * 128:(sc + 1) * 128], ident, (KH[1], CM_b), (pv[1],), inc=(sc == 3))
                tr(pvs[0][0:NS, 0:128], KH[0][:, NP_:NCOL], ident, (KH[1], CM_b), (pvs[1],), inc=True)
                KK = tbp.alloc()
                cp(ACT, KK[0][:, 0:512], pv[0][:, 0:512], (pv[1],), (KK[1],))
                cp(ACT, KK[0][0:NS, 512:640], pvs[0][0:NS, 0:128], (pvs[1],), (KK[1],))
                yield
                ctx["KK"] = KK
                tfp.free(KH)
                ps2p.free(pvt)
                VB = tbp.alloc()
                tt(DVE, VB[0][0:NS, 0:512].rearrange("p (s n) -> p s n", n=128),
                   vtv[0:NS, 4, :].unsqueeze(1).broadcast_to([NS, SQP, 128]),
                   CM[0:NS, M_SM:M_SM + SQP].unsqueeze(2).broadcast_to([NS, SQP, 128]), ALU.mult, (VT[1], CM_b), (VB[1],))
                ctx["VB"] = VB
                yield

            def chain_gen(h, ctx):
                wv, wb_ = ctx["wv"], ctx["wb"]
                EC, QT, VT, AB, KK, VB = ctx["EC"], ctx["QT"], ctx["VT"], ctx["AB"], ctx["KK"], ctx["VB"]
                vtv = VT[0][:, 0:640].rearrange("p (b n) -> p b n", n=128)
                kkv = KK[0][:, 0:640].rearrange("p (b n) -> p b n", n=128)
                po = ps2p.alloc()
                pov = ps2(po)
                for sc in range(4):
                    mm(pov[:, sc * 128:(sc + 1) * 128], vtv[:, sc, :], AB[0][:, sc * 128:(sc + 1) * 128], sc == 0, False,
                       (VT[1], AB[1]), (po[1][0],))
                    for cc in range(2):
                        cch = sc * 2 + cc
                        lo = cch * 64
                        mm(pov[:, lo:lo + 64], Sbf[j][:, h, :], QT[0][:, lo:lo + 64], False, (cch == 7),
                           (Sbf_b[j][h], QT[1]), (po[1][0],))
                        pd = psp_alloc1()
                        mm(pd[0][:, 0:128], kkv[cc * 64:(cc + 1) * 64, sc, :], vtv[cc * 64:(cc + 1) * 64, sc, :], True, True,
                           (KK[1], VT[1]), (pd[1],), inc=True)
                        stt(DVE, Sst[j][:, h, :], Sst[j][:, h, :], EC[0][:, lo + 63:lo + 64], pd[0][:, 0:128], ALU.mult, ALU.add,
                            (Sst_b[j][h], EC[1], pd[1]), (Sst_b[j][h],))
                        psp_free1(pd)
                        cp(DVE, Sbf[j][:, h, :], Sst[j][:, h, :], (Sst_b[j][h],), (Sbf_b[j][h],))
                        yield
                mm(pov[:, 512:512 + NS], vtv[0:NS, 4, :], AB[0][0:NS, 512:512 + NS], True, False, (VT[1], AB[1]), (po[1][1],))
                for s_ in range(SQP):
                    lo = NP_ + s_ * 8
                    mm(pov[:, 512 + s_ * 8:512 + (s_ + 1) * 8], S0b[:, s_, h, :], QT[0][:, lo:lo + 8], False, s_ == SQP - 1,
                       (S0b_b, QT[1]), (po[1][1],), inc=(s_ == SQP - 1))
                pd = psp_alloc1()
                mm(pd[0][:, 0:512], kkv[0:NS, 4, :], VB[0][0:NS, 0:512], True, True, (KK[1], VB[1]), (pd[1],), inc=True)
                for s_ in range(SQP):
                    lo = NP_ + s_ * 8
                    stt(DVE, S0f[:, s_, h, :], S0f[:, s_, h, :], EC[0][:, lo + 7:lo + 8], pd[0][:, s_ * 128:(s_ + 1) * 128], ALU.mult, ALU.add,
                        (S0f_b, EC[1], pd[1]), (S0f_b,))
                psp_free1(pd)
                tbp.free(KK)
                tbp.free(VB)
                tbp.free(AB)
                tbp.free(VT)
                tbp.free(QT)
                tfp.free(EC)
                yield
                OS = tfp.alloc()
                OQ = tbp.alloc()
                act(OQ[0][:, 0:NCOL], ps2(po)[:, 0:NCOL], AF.Square, po[1], (OQ[1],))
                pm = ps2p.alloc()
                pmv = ps2(pm)
                mm(pmv[:, 0:NP_], ONB2[:, :], OQ[0][:, 0:NP_], True, True, (CM_b, OQ[1]), (pm[1][0],))
                mm(pmv[:, 512:512 + NS], ONB2[:, :], OQ[0][:, NP_:NCOL], True, True, (CM_b, OQ[1]), (pm[1][1],), inc=True)
                tbp.free(OQ)
                yield
                act(OS[0][:, 0:NCOL], pmv[:, 0:NCOL], AF.Ln, pm[1], (OS[1],), bias=RMS_EPS)
                ps2p.free(pm)
                act(OS[0][:, 0:NCOL], OS[0][:, 0:NCOL], AF.Exp, (OS[1],), (OS[1],), scale=-0.5)
                ON = tfp.alloc()
                stt(DVE, ON[0][:, 0:NCOL], ps2(po)[:, 0:NCOL], cvc(O_HNH + j), OS[0][:, 0:NCOL], ALU.mult, ALU.mult,
                    tuple(po[1]) + (CV_b, OS[1]), (ON[1],))
                ps2p.free(po)
                SG = ctx["SG"]
                tt(DVE, mixo[:, h, :], ON[0][:, 0:NCOL], SG[0][:, 0:NCOL], ALU.mult, (ON[1], SG[1]), (mixo_b[h],))
                tfp.free(OS)
                tfp.free(ON)
                tfp.free(SG)
                yield

            def run_interleaved(gens):
                gens = [g for g in gens if g is not None]
                while gens:
                    for g in list(gens):
                        try:
                            next(g)
                        except StopIteration:
                            gens.remove(g)

            ctxs = [dict() for _ in range(8)]
            run_interleaved([proj_gen(0, ctxs[0])])
            for h in range(8):
                run_interleaved([chain_gen(h, ctxs[h]), proj_gen(h + 1, ctxs[h + 1]) if h + 1 < 8 else None])
            out_proj_and_z(mixo, mixo_b, 8, std_out_items(None))
            for s_ in range(SQP):
                dma(SP, S0o_t, ohs_d[j, p * SQP + s_].rearrange("h k v -> k h v"), S0f[:, s_, :, :], reads=(S0f_b,))
            if p == NPASS - 1:
                dma(SP, Spo_t, phs_d[j].rearrange("h k v -> k h v"), Sst[j][:, :, :], reads=tuple(Sst_b[j]))

        def psp_alloc1():
            return ps1p.alloc()

        def psp_free1(t):
            ps1p.free(t)

        def load_x(p):
            if DBG.get("p0", False):
                p = 0
            for tbk in range(DBG.get("ntbk", 4)):
                st, stb, stt_ = (stA, stA_b, stA_t) if tbk % 2 == 0 else (stB, stB_b, stB_t)
                dma(SP, stt_, st[:, 0:D], xp_d[p * NP_ + tbk * 128:p * NP_ + (tbk + 1) * 128, :], writes=(stb,))
                pt = ps2p.alloc()
                ptv = ps2(pt)
                for c in range(8):
                    tr(ptv[:, c * 128:(c + 1) * 128], st[:, c * 128:(c + 1) * 128], ident, (stb, CM_b), (pt[1][c // 4],), inc=(c % 4 == 3))
                cp(ACT, xf[:, :, tbk * 128:(tbk + 1) * 128], ptv.rearrange("p (c n) -> p c n", n=128), pt[1], tuple(xf_b))
                cp(DVE, xb[:, :, tbk * 128:(tbk + 1) * 128], xf[:, :, tbk * 128:(tbk + 1) * 128], tuple(xf_b), tuple(xb_b))
                ps2p.free(pt)
            if DBG.get("no_xs", False):
                return
            dma(SP, stA_t, stA[0:NS, 0:D], xs_d[p * NS:(p + 1) * NS, :], writes=(stA_b,))
            pt = ps1p.alloc()
            for c in range(8):
                tr(pt[0][:, c * NS:(c + 1) * NS], stA[0:NS, c * 128:(c + 1) * 128], ident[0:NS, 0:NS], (stA_b, CM_b), (pt[1],), inc=(c == 7))
            cp(ACT, xf[:, :, NP_:NCOL], pt[0][:, 0:8 * NS].rearrange("p (c n) -> p c n", n=NS), (pt[1],), tuple(xf_b))
            cp(DVE, xb[:, :, NP_:NCOL], xf[:, :, NP_:NCOL], tuple(xf_b), tuple(xb_b))
            ps1p.free(pt)
            if not DBG.get("io_rows", True):
                return
            for j in range(2):
                dma(SP, stB_t, stB[4 * j:4 * j + 4, 0:D], srh_d[j, p * SQP:(p + 1) * SQP, :], writes=(stB_b,))
            for j in range(2):
                dma(SP, stB_t, stB[32 + 12 * j:32 + 12 * j + 12, 0:D], src_d[j, p * SQP * 3:(p + 1) * SQP * 3, :], writes=(stB_b,))
            for g in range(2):
                for i in range(4):
                    dma(SP, stA_t, stA[32 * g + 8 * i:32 * g + 8 * i + 8, 0:SEGW],
                        sfc_d[i, p * SQP * 2:(p + 1) * SQP * 2, g * SEGW:(g + 1) * SEGW], writes=(stA_b,))
            pt = ps1p.alloc()
            for c in range(8):
                tr(pt[0][:, c * 8:(c + 1) * 8], stB[0:8, c * 128:(c + 1) * 128], ident[0:8, 0:8], (stB_b, CM_b), (pt[1],), inc=(c == 7))
            cp(DVE, h0s[:, :, :, :].rearrange("p j c s -> p c j s"),
               pt[0][:, 0:64].rearrange("p (c j s) -> p c j s", j=2, s=SQP), (pt[1],), (h0s_b,))
            ps1p.free(pt)
            pt = ps1p.alloc()
            for c in range(8):
                tr(pt[0][:, c * 24:(c + 1) * 24], stB[32:56, c * 128:(c + 1) * 128], ident[32:56, 32:56], (stB_b, CM_b), (pt[1],), inc=(c == 7))
            cp(DVE, cv0s[:, :, :, :].rearrange("p j c r -> p c j r"),
               pt[0][:, 0:192].rearrange("p (c j r) -> p c j r", j=2, r=SQP * 3), (pt[1],), (cv0s_b,))
            ps1p.free(pt)
            for g in range(2):
                pt = ps1p.alloc()
                for cc in range(11):
                    tr(pt[0][:, cc * 32:(cc + 1) * 32], stA[32 * g:32 * g + 32, cc * 128:(cc + 1) * 128],
                       ident[32 * g:32 * g + 32, 32 * g:32 * g + 32], (stA_b, CM_b), (pt[1],), inc=(cc == 10))
                cp(DVE, fc0s[:, :, 11 * g:11 * g + 11, :].rearrange("p i c r -> p c i r"),
                   pt[0][:, 0:352].rearrange("p (c i r) -> p c i r", i=4, r=SQP * 2), (pt[1],), (fc0s_b,))
                ps1p.free(pt)

        def store_y(p):
            for tbk in range(4):
                pt = ps2p.alloc()
                ptv = ps2(pt)
                for c in range(8):
                    tr(ptv[:, c * 128:(c + 1) * 128], xf[:, c, tbk * 128:(tbk + 1) * 128], ident, (xf_b[c], CM_b), (pt[1][c // 4],), inc=(c % 4 == 3))
                st, stb, stt_ = (stA, stA_b, stA_t) if tbk % 2 == 0 else (stB, stB_b, stB_t)
                cp(ACT, st[:, 0:D], ptv, pt[1], (stb,))
                ps2p.free(pt)
                dma(SP, stt_, yp_d[p * NP_ + tbk * 128:p * NP_ + (tbk + 1) * 128, :], st[:, 0:D], reads=(stb,))
            pt = ps2p.alloc()
            ptv = ps2(pt)
            for c in range(8):
                tr(ptv[0:NS, c * 128:(c + 1) * 128], xf[:, c, NP_:NCOL], ident, (xf_b[c], CM_b), (pt[1][c // 4],), inc=(c % 4 == 3))
            cp(ACT, stA[0:NS, 0:D], ptv[0:NS, :], pt[1], (stA_b,))
            ps2p.free(pt)
            dma(SP, stA_t, ys_d[p * NS:(p + 1) * NS, :], stA[0:NS, 0:D], reads=(stA_b,))

        marks = []

        def mark():
            marks.append(tuple(len(g.prog) for g in (PE, ACT, DVE, POOL, SP)))

        pass_idx = []
        for p in range(DBG["npass"]):
            pass_idx.append(len(marks))
            mark()
            load_x(p)
            for i in range(DBG["depth"]):
                mark()
                if DBG["mixer"]:
                    if i % 2 == 0:
                        rglru(i, p)
                    else:
                        hgrn(i, p)
                    if DBG["ln"]:
                        layer_norm(i, 0)
                if DBG["ffn"]:
                    ffn(i, p)
                    if DBG["ln"]:
                        layer_norm(i, 1)
            if DBG.get("io_store", True):
                store_y(p)

        for t in (stA_t, stB_t, S0o_t, Spo_t):
            if t.cnt:
                SP.wait(t, t.cnt)

        mark()
        engs = (PE, ACT, DVE, POOL, SP)
        bounds = [tuple(0 for _ in engs)] + marks
        nseg = len(bounds)
        starts = [0] + [pi + 1 for pi in pass_idx[1:]] + [nseg]
        if DBG.get("one_block", True):
            starts = [0, nseg]
        for bi in range(len(starts) - 1):
            with nc.Block() as block:
                regs = (block.tensor, block.scalar, block.vector, block.gpsimd, block.sync)
                for gi, (g, reg) in enumerate(zip(engs, regs)):
                    for si in range(starts[bi], starts[bi + 1]):
                        lo = bounds[si][gi]
                        hi = bounds[si + 1][gi] if si + 1 < nseg else len(g.prog)
                        if hi <= lo:
                            continue

                        def body(e, g=g, lo=lo, hi=hi):
                            for f_ in g.prog[lo:hi]:
                                f_(e)
                        reg(body)
    return nc


_CACHE = {}
DBG = {"npass": NPASS, "depth": DEPTH, "mixer": True, "ln": True, "ffn": True, "cores": NCORE}


def _consts():
    cm = np.zeros((128, CMW), np.float32)
    cm[:, M_ID:M_ID + 128] = np.eye(128, dtype=np.float32)
    s = np.arange(128)[:, None]
    t = np.arange(128)[None, :]
    cm[:, M_A2:M_A2 + 128] = ((s // 64 == t // 64) & (s <= t)).astype(np.float32)
    s = np.arange(32)[:, None]
    t = np.arange(32)[None, :]
    cm[0:32, M_AS:M_AS + 32] = ((s // 8 == t // 8) & (s <= t)).astype(np.float32)
    rst = np.ones(NCOL, np.float32)
    rst[0:NP_:64] = 0.0
    rst[NP_:NCOL:8] = 0.0
    cm[:, M_RST:M_RST + NCOL] = rst[None, :]
    for q in range(SQP):
        cm[q * 8:(q + 1) * 8, M_SM + q] = 1.0
    cm[:, M_ONE:M_ONE + 128] = 1.0 / 1024.0
    cm[:, M_ONE2:M_ONE2 + 128] = 1.0 / 128.0
    return cm


def kernel(x_prompt, x_sample, state_rglru_h, state_rglru_conv, state_hgrn_s, state_ffn_conv,
           ln_g, ln_b, rg_w_in, rg_conv_w, rg_conv_b, rg_gate_w, rg_gate_b, rg_lambda, rg_w_out,
           hg_lower, hg_w_in, hg_norm_g, hg_w_out, ffn_w_in, ffn_conv_w, ffn_conv_b, ffn_w_out):
    f = lambda a: np.ascontiguousarray(np.asarray(a, dtype=np.float32))
    x_prompt, x_sample = f(x_prompt), f(x_sample)
    state_rglru_h, state_rglru_conv = f(state_rglru_h), f(state_rglru_conv)
    state_hgrn_s, state_ffn_conv = f(state_hgrn_s), f(state_ffn_conv)
    cvec = np.zeros((NROWS, 128), np.float32)
    parts = [(O_LNG, ln_g), (O_LNB, ln_b), (O_RCW, rg_conv_w), (O_RCB, rg_conv_b), (O_RGB, rg_gate_b),
             (O_LAM, rg_lambda), (O_HLO, hg_lower), (O_HNG, hg_norm_g), (O_FCW, ffn_conv_w), (O_FCB, ffn_conv_b)]
    for off, a in parts:
        r = f(a).reshape(-1, 128)
        cvec[off:off + r.shape[0]] = r
    cmask = _consts()
    if "nc" not in _CACHE:
        _CACHE["nc"] = build_program()
    nc = _CACHE["nc"]
    shared = dict(cvec=cvec, cmask=cmask, rg_w_in=f(rg_w_in), rg_gate_w=f(rg_gate_w), rg_w_out=f(rg_w_out),
                  hg_w_in=f(hg_w_in), hg_w_out=f(hg_w_out), ffn_w_in=f(ffn_w_in), ffn_w_out=f(ffn_w_out))
    in_maps = []
    for c in range(NCORE):
        sl = slice(16 * c, 16 * c + 16)
        m = dict(shared)
        m["xp"] = x_prompt[c]
        m["xs"] = x_sample[sl].reshape(128, D)
        m["srh"] = np.ascontiguousarray(state_rglru_h[:, sl])
        m["src"] = np.ascontiguousarray(state_rglru_conv[:, sl]).reshape(2, 48, D)
        m["shs"] = np.ascontiguousarray(state_hgrn_s[:, sl])
        m["sfc"] = np.ascontiguousarray(state_ffn_conv[:, sl]).reshape(4, 32, DFF)
        in_maps.append(m)
    ncr = DBG["cores"]
    res = run_bass_kernel_spmd(nc, in_maps[:ncr], core_ids=list(range(ncr)))
    R = list(res.results) + [res.results[0]] * (NCORE - ncr)
    y_prompt = np.stack([R[c]["yp"] for c in range(NCORE)], 0)
    y_sample = np.concatenate([R[c]["ys"].reshape(16, 8, D) for c in range(NCORE)], 0)
    p_h = np.stack([R[c]["prh"] for c in range(NCORE)], 1)
    p_rc = np.stack([R[c]["prc"] for c in range(NCORE)], 1)
    p_s = np.stack([R[c]["phs"] for c in range(NCORE)], 1)
    p_fc = np.stack([R[c]["pfc"] for c in range(NCORE)], 1)
    s_h = np.concatenate([R[c]["orh"] for c in range(NCORE)], 1)
    s_rc = np.concatenate([R[c]["orc"].reshape(2, 16, 3, D) for c in range(NCORE)], 1)
    s_s = np.concatenate([R[c]["ohs"] for c in range(NCORE)], 1)
    s_fc = np.concatenate([R[c]["ofc"].reshape(4, 16, 2, DFF) for c in range(NCORE)], 1)
    return tuple(np.ascontiguousarray(a, dtype=np.float32) for a in
                 (y_prompt, y_sample, p_h, p_rc, p_s, p_fc, s_h, s_rc, s_s, s_fc))
```
